# Optimizing a Trainium2 kernel written in Bass

```python
import math
import jax, jax.numpy as jnp
from jax import lax
import numpy as np

D_MODEL = 1024
BATCH = 8
SEQ = 4096
DEPTH = 2

GRID_W = 64
CTX_LEN = 256
HEAD_DIM = 64
N_HEADS_TOTAL = D_MODEL // HEAD_DIM
A_HEADS = N_HEADS_TOTAL // 2
A_KV_HEADS = A_HEADS // 4
A_REP = A_HEADS // A_KV_HEADS
B_HEADS = N_HEADS_TOTAL // 4
C_HEADS = N_HEADS_TOTAL // 4
DIFF_DIM = HEAD_DIM // 2
WIN_H_MAX = 8
WIN_W = 16
Q_BLOCK = 128
FFN_DIM = ((8 * D_MODEL // 3 + 127) // 128) * 128
ROPE_THETA = 10000.0
EPS = 1e-6
IN_SIZES = (A_HEADS * HEAD_DIM, A_KV_HEADS * HEAD_DIM, A_KV_HEADS * HEAD_DIM,
            B_HEADS * HEAD_DIM, B_HEADS * HEAD_DIM, B_HEADS * HEAD_DIM,
            C_HEADS * HEAD_DIM, C_HEADS * HEAD_DIM, C_HEADS * HEAD_DIM)
IN_COLS = A_HEADS * HEAD_DIM + 2 * A_KV_HEADS * HEAD_DIM + 3 * B_HEADS * HEAD_DIM + 3 * C_HEADS * HEAD_DIM

kernel_name = "hybrid_gqa_natten_diffattn_convffn_prefix"


def rms_norm(x, g):
    xf = x.astype(jnp.float32)
    y = xf * lax.rsqrt(jnp.mean(xf * xf, axis=-1, keepdims=True) + EPS)
    return y.astype(x.dtype) * g


def modulate(h, shift, scale):
    return h * (1.0 + scale) + shift


def heads(t, n):
    b, s, _ = t.shape
    return t.reshape(b, s, n, -1).transpose(0, 2, 1, 3)


def diff_heads(t):
    b, s, _ = t.shape
    return t.reshape(b, s, C_HEADS, 2, DIFF_DIM).transpose(0, 2, 3, 1, 4)


def merge_heads(o):
    b, h, s, d = o.shape
    return o.transpose(0, 2, 1, 3).reshape(b, s, h * d)


def axial_rope_angles(L, dim):
    half = dim // 2
    inv = ROPE_THETA ** (-jnp.arange(0, half, 2, dtype=jnp.float32) / half)
    t = jnp.arange(L, dtype=jnp.int32)
    row = (t // GRID_W).astype(jnp.float32)
    col = (t % GRID_W).astype(jnp.float32)
    ang = jnp.concatenate([row[:, None] * inv, col[:, None] * inv], axis=-1)
    return jnp.cos(ang), jnp.sin(ang)


def apply_rope(x, cos, sin):
    x1, x2 = x[..., 0::2], x[..., 1::2]
    c = cos.astype(x.dtype)
    s = sin.astype(x.dtype)
    return jnp.stack([x1 * c - x2 * s, x1 * s + x2 * c], axis=-1).reshape(x.shape)


def sweep_blocks(fn, q, axis, out_axis):
    s = q.shape[axis]
    n = s // Q_BLOCK
    qb = jnp.moveaxis(q.reshape(q.shape[:axis] + (n, Q_BLOCK) + q.shape[axis + 1:]), axis, 0)
    o = jnp.moveaxis(lax.map(fn, qb), 0, out_axis)
    return o.reshape(o.shape[:out_axis] + (s,) + o.shape[out_axis + 2:])


def softmax_attend(q, kvs, s_spec, o_spec, scale):
    s = jnp.concatenate([jnp.einsum(s_spec, q, k, preferred_element_type=jnp.float32) for k, _ in kvs], axis=-1) * scale
    p = jax.nn.softmax(s, axis=-1)
    bounds = np.cumsum([k.shape[-2] for k, _ in kvs])[:-1].tolist()
    parts = jnp.split(p, bounds, axis=-1)
    terms = [jnp.einsum(o_spec, pi.astype(v.dtype), v) for pi, (_, v) in zip(parts, kvs)]
    return sum(terms[1:], terms[0])


def diff_attend(q, kvs, lam, scale):
    s = jnp.concatenate([jnp.einsum('bhmqd,bhmkd->bhmqk', q, k, preferred_element_type=jnp.float32)
                         for k, _ in kvs], axis=-1) * scale
    p = jax.nn.softmax(s, axis=-1)
    w = p[:, :, 0] - lam * p[:, :, 1]
    bounds = np.cumsum([k.shape[-2] for k, _ in kvs])[:-1].tolist()
    parts = jnp.split(w, bounds, axis=-1)
    terms = [jnp.einsum('bhqk,bhkd->bhqd', wi.astype(v.dtype), v) for wi, (_, v) in zip(parts, kvs)]
    return sum(terms[1:], terms[0])


def neighbourhood_attend(q, k, v, k_ctx, v_ctx, rpb):
    b, h, L, d = q.shape
    rows = L // GRID_W
    win_h = min(WIN_H_MAX, rows)
    n_win = win_h * WIN_W
    scale = d ** -0.5
    qg = q.reshape(b, h, rows, GRID_W, d)
    kg = k.reshape(b, h, rows, GRID_W, d)
    vg = v.reshape(b, h, rows, GRID_W, d)
    col = jnp.arange(GRID_W)
    col_idx = jnp.clip(col - WIN_W // 2, 0, GRID_W - WIN_W)[:, None] + jnp.arange(WIN_W)[None, :]
    col_bias = rpb[:, :, col_idx - col[:, None] + WIN_W - 1]

    def one_row(r):
        rs = jnp.clip(r - win_h // 2, 0, rows - win_h)
        q_r = lax.dynamic_index_in_dim(qg, r, axis=2, keepdims=False)
        k_win = lax.dynamic_slice_in_dim(kg, rs, win_h, axis=2)[:, :, :, col_idx]
        v_win = lax.dynamic_slice_in_dim(vg, rs, win_h, axis=2)[:, :, :, col_idx]
        row_off = rs + jnp.arange(win_h) - r
        bias = col_bias[:, row_off + WIN_H_MAX - 1].transpose(0, 2, 1, 3)
        s_win = jnp.einsum('bhqd,bhiqjd->bhqij', q_r, k_win, preferred_element_type=jnp.float32) * scale + bias
        s_ctx = jnp.einsum('bhqd,bhkd->bhqk', q_r, k_ctx, preferred_element_type=jnp.float32) * scale
        p = jax.nn.softmax(jnp.concatenate([s_win.reshape(b, h, GRID_W, n_win), s_ctx], axis=-1), axis=-1)
        p_win = p[..., :n_win].reshape(b, h, GRID_W, win_h, WIN_W).astype(v.dtype)
        p_ctx = p[..., n_win:].astype(v.dtype)
        return (jnp.einsum('bhqij,bhiqjd->bhqd', p_win, v_win)
                + jnp.einsum('bhqk,bhkd->bhqd', p_ctx, v_ctx))

    o = lax.map(one_row, jnp.arange(rows))
    return o.transpose(1, 2, 0, 3, 4).reshape(b, h, L, d)


def mixer_a(q, k, v, q_ctx, k_ctx, v_ctx, gq, gk, cos, sin, with_ctx):
    q = apply_rope(rms_norm(heads(q, A_HEADS), gq), cos, sin)
    k = apply_rope(rms_norm(heads(k, A_KV_HEADS), gk), cos, sin)
    v = heads(v, A_KV_HEADS)
    k_ctx = rms_norm(heads(k_ctx, A_KV_HEADS), gk)
    v_ctx = heads(v_ctx, A_KV_HEADS)
    b, _, L, d = q.shape
    scale = d ** -0.5
    s_spec, o_spec = 'bgrqd,bgkd->bgrqk', 'bgrqk,bgkd->bgrqd'
    qg = q.reshape(b, A_KV_HEADS, A_REP, L, d)
    o = sweep_blocks(lambda qb: softmax_attend(qb, ((k, v), (k_ctx, v_ctx)), s_spec, o_spec, scale), qg, 3, 3)
    out = merge_heads(o.reshape(b, A_HEADS, L, d))
    out_ctx = None
    if with_ctx:
        qc = rms_norm(heads(q_ctx, A_HEADS), gq)
        lc = qc.shape[2]
        oc = softmax_attend(qc.reshape(b, A_KV_HEADS, A_REP, lc, d), ((k_ctx, v_ctx),), s_spec, o_spec, scale)
        out_ctx = merge_heads(oc.reshape(b, A_HEADS, lc, d))
    return out, out_ctx


def mixer_b(q, k, v, q_ctx, k_ctx, v_ctx, gq, gk, rpb, with_ctx):
    q = rms_norm(heads(q, B_HEADS), gq)
    k = rms_norm(heads(k, B_HEADS), gk)
    v = heads(v, B_HEADS)
    k_ctx = rms_norm(heads(k_ctx, B_HEADS), gk)
    v_ctx = heads(v_ctx, B_HEADS)
    out = merge_heads(neighbourhood_attend(q, k, v, k_ctx, v_ctx, rpb))
    out_ctx = None
    if with_ctx:
        qc = rms_norm(heads(q_ctx, B_HEADS), gq)
        oc = softmax_attend(qc, ((k_ctx, v_ctx),), 'bhqd,bhkd->bhqk', 'bhqk,bhkd->bhqd', HEAD_DIM ** -0.5)
        out_ctx = merge_heads(oc)
    return out, out_ctx


def mixer_c(q, k, v, q_ctx, k_ctx, v_ctx, gq, gk, lq1, lk1, lq2, lk2, g_sub, lam_init, cos, sin, with_ctx):
    lam = (jnp.exp(jnp.sum(lq1.astype(jnp.float32) * lk1.astype(jnp.float32)))
           - jnp.exp(jnp.sum(lq2.astype(jnp.float32) * lk2.astype(jnp.float32))) + lam_init)
    q = apply_rope(rms_norm(diff_heads(q), gq), cos, sin)
    k = apply_rope(rms_norm(diff_heads(k), gk), cos, sin)
    v = heads(v, C_HEADS)
    k_ctx = rms_norm(diff_heads(k_ctx), gk)
    v_ctx = heads(v_ctx, C_HEADS)
    scale = DIFF_DIM ** -0.5
    o = sweep_blocks(lambda qb: diff_attend(qb, ((k, v), (k_ctx, v_ctx)), lam, scale), q, 3, 2)
    out = merge_heads(rms_norm(o, g_sub) * (1.0 - lam_init))
    out_ctx = None
    if with_ctx:
        qc = rms_norm(diff_heads(q_ctx), gq)
        oc = diff_attend(qc, ((k_ctx, v_ctx),), lam, scale)
        out_ctx = merge_heads(rms_norm(oc, g_sub) * (1.0 - lam_init))
    return out, out_ctx


def dwconv3(x, w, b):
    xp = jnp.pad(x, ((0, 0), (1, 1), (0, 0)))
    return xp[:, :-2] * w[0] + xp[:, 1:-1] * w[1] + xp[:, 2:] * w[2] + b


def conv_ffn(h, w_up, conv_w, conv_b, w_down):
    ug = dwconv3(h @ w_up, conv_w, conv_b)
    u, g = jnp.split(ug, 2, axis=-1)
    return (jax.nn.silu(g) * u) @ w_down


def setup_inputs(seed: int = 0) -> dict:
    key = jax.random.key(seed)
    ks = jax.random.split(key, 26)
    D = D_MODEL

    def nrm(k, shape, s):
        return jax.random.normal(k, shape, jnp.float32) * s

    return {
        "x": nrm(ks[0], (BATCH, SEQ, D), 1.0),
        "c": nrm(ks[1], (BATCH, D), 1.0),
        "ctx": nrm(ks[2], (BATCH, CTX_LEN, D), 1.0),
        "c_ctx": nrm(ks[3], (D,), 1.0),
        "w_ada": nrm(ks[4], (DEPTH, D, 6 * D), 0.5 * D ** -0.5),
        "b_ada": nrm(ks[5], (DEPTH, 6 * D), 0.01),
        "g_norm1": 1.0 + nrm(ks[6], (DEPTH, D), 0.01),
        "w_in": nrm(ks[7], (DEPTH, D, IN_COLS), D ** -0.5),
        "gq_a": 1.0 + nrm(ks[8], (DEPTH, HEAD_DIM), 0.01),
        "gk_a": 1.0 + nrm(ks[9], (DEPTH, HEAD_DIM), 0.01),
        "gq_b": 1.0 + nrm(ks[10], (DEPTH, HEAD_DIM), 0.01),
        "gk_b": 1.0 + nrm(ks[11], (DEPTH, HEAD_DIM), 0.01),
        "rpb_b": nrm(ks[12], (DEPTH, B_HEADS, 2 * WIN_H_MAX - 1, 2 * WIN_W - 1), 0.02),
        "gq_c": 1.0 + nrm(ks[13], (DEPTH, DIFF_DIM), 0.01),
        "gk_c": 1.0 + nrm(ks[14], (DEPTH, DIFF_DIM), 0.01),
        "lambda_q1": nrm(ks[15], (DEPTH, DIFF_DIM), 0.1),
        "lambda_k1": nrm(ks[16], (DEPTH, DIFF_DIM), 0.1),
        "lambda_q2": nrm(ks[17], (DEPTH, DIFF_DIM), 0.1),
        "lambda_k2": nrm(ks[18], (DEPTH, DIFF_DIM), 0.1),
        "g_subln": 1.0 + nrm(ks[19], (DEPTH, 2 * DIFF_DIM), 0.01),
        "w_out": nrm(ks[20], (DEPTH, D, D), D ** -0.5),
        "g_norm2": 1.0 + nrm(ks[21], (DEPTH, D), 0.01),
        "w_up": nrm(ks[22], (DEPTH, D, 2 * FFN_DIM), D ** -0.5),
        "conv_w": nrm(ks[23], (DEPTH, 3, 2 * FFN_DIM), 3 ** -0.5),
        "conv_b": nrm(ks[24], (DEPTH, 2 * FFN_DIM), 0.01),
        "w_down": nrm(ks[25], (DEPTH, FFN_DIM, D), FFN_DIM ** -0.5),
    }


def reference(x, c, ctx, c_ctx, w_ada, b_ada, g_norm1, w_in, gq_a, gk_a, gq_b, gk_b, rpb_b, gq_c, gk_c,
              lambda_q1, lambda_k1, lambda_q2, lambda_k2, g_subln, w_out, g_norm2, w_up, conv_w, conv_b, w_down):
    L = x.shape[1]
    cos_a, sin_a = axial_rope_angles(L, HEAD_DIM)
    cos_c, sin_c = axial_rope_angles(L, DIFF_DIM)
    silu_c = jax.nn.silu(c)
    silu_cc = jax.nn.silu(c_ctx)
    bounds = np.cumsum(IN_SIZES)[:-1].tolist()
    for l in range(DEPTH):
        with_ctx = l < DEPTH - 1
        lam_init = 0.8 - 0.6 * math.exp(-0.3 * l)
        mod_x = (silu_c @ w_ada[l] + b_ada[l])[:, None, :]
        mod_c = (silu_cc @ w_ada[l] + b_ada[l])[None, None, :]
        sh1_x, sc1_x, gt1_x, sh2_x, sc2_x, gt2_x = jnp.split(mod_x, 6, axis=-1)
        sh1_c, sc1_c, gt1_c, sh2_c, sc2_c, gt2_c = jnp.split(mod_c, 6, axis=-1)

        hx = modulate(rms_norm(x, g_norm1[l]), sh1_x, sc1_x)
        hc = modulate(rms_norm(ctx, g_norm1[l]), sh1_c, sc1_c)
        qa, ka, va, qb, kb, vb, qc, kc, vc = jnp.split(hx @ w_in[l], bounds, axis=-1)
        qa_t, ka_t, va_t, qb_t, kb_t, vb_t, qc_t, kc_t, vc_t = jnp.split(hc @ w_in[l], bounds, axis=-1)

        oa, oa_t = mixer_a(qa, ka, va, qa_t, ka_t, va_t, gq_a[l], gk_a[l], cos_a, sin_a, with_ctx)
        ob, ob_t = mixer_b(qb, kb, vb, qb_t, kb_t, vb_t, gq_b[l], gk_b[l], rpb_b[l], with_ctx)
        oc, oc_t = mixer_c(qc, kc, vc, qc_t, kc_t, vc_t, gq_c[l], gk_c[l], lambda_q1[l], lambda_k1[l],
                           lambda_q2[l], lambda_k2[l], g_subln[l], lam_init, cos_c, sin_c, with_ctx)

        x = x + gt1_x * (jnp.concatenate([oa, ob, oc], axis=-1) @ w_out[l])
        hx2 = modulate(rms_norm(x, g_norm2[l]), sh2_x, sc2_x)
        x = x + gt2_x * conv_ffn(hx2, w_up[l], conv_w[l], conv_b[l], w_down[l])

        if with_ctx:
            ctx = ctx + gt1_c * (jnp.concatenate([oa_t, ob_t, oc_t], axis=-1) @ w_out[l])
            hc2 = modulate(rms_norm(ctx, g_norm2[l]), sh2_c, sc2_c)
            ctx = ctx + gt2_c * conv_ffn(hc2, w_up[l], conv_w[l], conv_b[l], w_down[l])
    return x
```

```python
import math
import os
CUT = int(os.environ.get('KCUT', '99'))
from contextlib import ExitStack

import numpy as np
import concourse.bass as bass
import concourse.mybir as mybir
from concourse.bass_utils import run_bass_kernel_spmd

F32 = mybir.dt.float32
BF16 = mybir.dt.bfloat16
ALU = mybir.AluOpType
AF = mybir.ActivationFunctionType
AX = mybir.AxisListType

D = 1024
L = 4096
LC = 256
T = L + LC
DEPTH = 2
NKC = 8
FFN = 2816
NJ = FFN // 128
EPS = 1e-6
NEG = -30000.0
NSLAB = 24
W1C = NSLAB * 128 + 640
GROUP = 2
TB = 410

PT_BADA = 0
PT_G1 = 48
PT_G2 = 56
PT_GTAB = 64
PT_CW = 88
PT_CB = 220
PT_GSUB = 264
PT_LAM = 265
NPT = 393


class Eng:
    def __init__(self, nc, es, eng, name, skip_own=False):
        self.e = eng
        self.sem = es.enter_context(nc.semaphore(name))
        self.cnt = 0
        self.seen = {}
        self.skip_own = skip_own

    def wait(self, toks):
        best = {}
        for t in toks:
            if t is None:
                continue
            s, v = t
            if self.skip_own and s is self.sem:
                continue
            k = id(s)
            if self.seen.get(k, 0) >= v:
                continue
            if k not in best or best[k][1] < v:
                best[k] = (s, v)
        for k, (s, v) in best.items():
            self.e.wait_ge(s, v)
            self.seen[k] = v

    def sig(self, ins):
        self.cnt += 1
        ins.then_inc(self.sem, 1)
        return (self.sem, self.cnt)


class Buf:
    def __init__(self, t):
        self.t = t
        self.wr = None
        self.rd = {}
        self.dsem = None
        self.dcnt = 0
        self.excl = False

    def add_rd(self, tok):
        k = id(tok[0])
        if k not in self.rd or self.rd[k][1] < tok[1]:
            self.rd[k] = tok


class K:
    pass


def op(E, fn, rd=(), wr=()):
    toks = [b.wr for b in rd]
    for b in rd:
        if b.excl:
            toks.extend(b.rd.values())
    for b in wr:
        toks.append(b.wr)
        toks.extend(b.rd.values())
    E.wait(toks)
    tok = E.sig(fn())
    for b in rd:
        b.add_rd(tok)
    for b in wr:
        b.wr = tok
        b.rd = {}
    return tok


def _dsem(k, b):
    if b.dsem is None:
        k.nsem += 1
        b.dsem = k.es.enter_context(k.nc.semaphore("d%d" % k.nsem))
    return b.dsem


def dma_load(k, Q, b, dst_ap, src_ap, part=False):
    if not part:
        toks = [b.wr] + list(b.rd.values())
        Q.wait(toks)
    s = _dsem(k, b)
    ins = Q.e.dma_start(out=dst_ap, in_=src_ap)
    b.dcnt += 16
    ins.then_inc(s, 16)
    b.wr = (s, b.dcnt)
    b.rd = {}


def dma_store(k, Q, dst_ap, b, src_ap):
    Q.wait([b.wr])
    s = _dsem(k, b)
    ins = Q.e.dma_start(out=dst_ap, in_=src_ap)
    b.dcnt += 16
    ins.then_inc(s, 16)
    tok = (s, b.dcnt)
    b.add_rd(tok)
    k.stores[id(s)] = tok


def barrier(k):
    toks = [(E.sem, E.cnt) for E in k.engs if E.cnt > 0] + list(k.stores.values())
    for E in k.engs + [k.sp]:
        E.wait(toks)
    k.stores = {}


def build_nc(dbg=False, nlayers=DEPTH, stop_after=None):
    nc = bass.Bass("TRN2", target_bir_lowering=False)
    k = K()
    k.nc = nc
    k.nsem = 0
    k.stores = {}

    def din(name, shape, dt=F32):
        return nc.dram_tensor(name, list(shape), dt, kind="ExternalInput").ap()

    xT = din("xT", [D, T])
    cvec = din("cvec", [128, NKC, 2])
    w_ada = din("w_ada", [DEPTH, D, 6 * D])
    ptab = din("ptab", [DEPTH, 128, NPT])
    w1 = din("w1", [DEPTH, D, W1C])
    ropeA = din("ropeA", [2, 128, T])
    ropeC = din("ropeC", [2, 128, T])
    biasB = din("biasB", [DEPTH, 4, 128, 25, 128])
    w_out = din("w_out", [DEPTH, D, D])
    w_up = din("w_up", [DEPTH, D, 2 * FFN])
    w_down = din("w_down", [DEPTH, FFN, D])
    consts = din("consts", [3, 128, 128])
    yT = nc.dram_tensor("yT", [D, L], F32, kind="ExternalOutput").ap()

    skind = "ExternalOutput" if dbg else "Internal"

    def dscr(name, shape, dt):
        return nc.dram_tensor(name, list(shape), dt, kind=skind).ap()

    xs_a = dscr("xs_a", [D, T], F32)
    xs_b = dscr("xs_b", [D, T], F32)
    qs = dscr("qs", [8, 128, T], BF16)
    ks = dscr("ks", [16, 128, T], BF16)
    vs = dscr("vs", [10, T, 128], BF16)
    attn = dscr("attn", [D, T], BF16)
    modd = dscr("modd", [DEPTH, 128, 96], F32) if dbg else None

    with ExitStack() as es:
        k.es = es
        pe = Eng(nc, es, nc.tensor, "s_pe", skip_own=True)
        act = Eng(nc, es, nc.scalar, "s_act")
        dve = Eng(nc, es, nc.vector, "s_dve")
        pool = Eng(nc, es, nc.gpsimd, "s_pool")
        sp = Eng(nc, es, nc.sync, "s_sp")
        k.engs = [pe, act, dve, pool]
        k.sp = sp

        uniq = [0]

        def sb(st, name, shape, dt):
            uniq[0] += 1
            return Buf(st.enter_context(nc.sbuf_tensor("%s_u%d" % (name, uniq[0]), list(shape), dt)))

        def ps(st, name, shape=(128, 512), dt=F32):
            uniq[0] += 1
            b = Buf(st.enter_context(nc.psum_tensor("%s_u%d" % (name, uniq[0]), list(shape), dt)))
            b.excl = True
            return b

        ones_bf = sb(es, "ones_bf", [128, 128], BF16)
        blk64 = sb(es, "blk64", [128, 128], BF16)
        blk32 = sb(es, "blk32", [128, 128], BF16)
        ident = sb(es, "ident", [128, 128], F32)
        ptb = [sb(es, "ptb%d" % l, [128, NPT], F32) for l in range(DEPTH)]
        mods = [sb(es, "mods%d" % l, [128, 48, 2], F32) for l in range(DEPTH)]
        gm = [sb(es, "gm%d" % l, [128, 2, NKC, 2], F32) for l in range(DEPTH)]
        lamt = [sb(es, "lamt%d" % l, [128, 4], F32) for l in range(DEPTH)]

        op(pool, lambda: nc.gpsimd.memset(ones_bf.t[:], 1.0), wr=[ones_bf])
        zbf = sb(es, "zbf", [128, 128], BF16)
        op(pool, lambda: nc.gpsimd.memset(zbf.t[:], 0.0), wr=[zbf])
        dma_load(k, sp, ident, ident.t[:], consts[0])
        dma_load(k, pool, blk64, blk64.t[:], consts[1])
        dma_load(k, pool, blk32, blk32.t[:], consts[2])
        for l in range(DEPTH):
            dma_load(k, sp, ptb[l], ptb[l].t[:], ptab[l])

        with ExitStack() as p0:
            csb = sb(p0, "csb", [128, NKC, 2], F32)
            scs = sb(p0, "scs", [128, NKC, 2], F32)
            wa = [sb(p0, "wa%d" % i, [128, NKC, 768], F32) for i in range(2)]
            modp = ps(p0, "modp", [128, 512])
            tmpl = sb(p0, "tmpl", [128, 32], F32)
            zer_bf = sb(p0, "zer_bf", [128, T], BF16)
            op(pool, lambda: nc.gpsimd.memset(zer_bf.t[:], 0.0), wr=[zer_bf])
            for v in range(16):
                dma_store(k, sp, ks[v], zer_bf, zer_bf.t[:])
            dma_load(k, sp, csb, csb.t[:], cvec)
            op(act, lambda: nc.scalar.activation(out=scs.t[:], in_=csb.t[:], func=AF.Silu), rd=[csb], wr=[scs])
            for l in range(nlayers):
                for pc in range(8):
                    wb = wa[pc % 2]
                    dma_load(k, sp, wb, wb.t[:],
                             w_ada[l, :, pc * 768:(pc + 1) * 768].rearrange("(k p) c -> p k c", p=128))

                    def mm(wb=wb, pc=pc):
                        ins = None
                        for jl in range(6):
                            j = pc * 6 + jl
                            for kc in range(NKC):
                                ins = nc.tensor.matmul(modp.t[:, 2 * j:2 * j + 2], lhsT=wb.t[:, kc, jl * 128:(jl + 1) * 128],
                                                       rhs=scs.t[:, kc, :], start=(kc == 0), stop=(kc == NKC - 1))
                        return ins
                    op(pe, mm, rd=[wb, scs], wr=[modp])
                mv = modp.t[:, 0:96].rearrange("p (j t) -> p j t", t=2)
                for t in range(2):
                    op(dve, lambda t=t: nc.vector.tensor_tensor(out=mods[l].t[:, :, t], in0=mv[:, :, t],
                                                                in1=ptb[l].t[:, PT_BADA:PT_BADA + 48], op=ALU.add),
                       rd=[modp, ptb[l]], wr=[mods[l]])
                for n, (sc0, g0) in enumerate(((8, PT_G1), (32, PT_G2))):
                    for t in range(2):
                        op(dve, lambda n=n, sc0=sc0, g0=g0, t=t: nc.vector.scalar_tensor_tensor(
                            out=gm[l].t[:, n, :, t], in0=mods[l].t[:, sc0:sc0 + 8, t], scalar=1.0,
                            in1=ptb[l].t[:, g0:g0 + 8], op0=ALU.add, op1=ALU.mult),
                           rd=[mods[l], ptb[l]], wr=[gm[l]])
                lam_init = 0.8 - 0.6 * math.exp(-0.3 * l)
                for i in range(2):
                    a0 = PT_LAM + 64 * i
                    op(dve, lambda a0=a0: nc.vector.tensor_tensor(out=tmpl.t[:], in0=ptb[l].t[:, a0:a0 + 32],
                                                                  in1=ptb[l].t[:, a0 + 32:a0 + 64], op=ALU.mult),
                       rd=[ptb[l]], wr=[tmpl])
                    op(dve, lambda i=i: nc.vector.reduce_sum(out=lamt[l].t[:, 2 + i:3 + i], in_=tmpl.t[:], axis=AX.X),
                       rd=[tmpl], wr=[lamt[l]])
                op(act, lambda: nc.scalar.activation(out=lamt[l].t[:, 2:4], in_=lamt[l].t[:, 2:4], func=AF.Exp),
                   rd=[lamt[l]], wr=[lamt[l]])
                op(dve, lambda: nc.vector.scalar_tensor_tensor(out=lamt[l].t[:, 0:1], in0=lamt[l].t[:, 3:4], scalar=-lam_init,
                                                               in1=lamt[l].t[:, 2:3], op0=ALU.add, op1=ALU.subtract),
                   rd=[lamt[l]], wr=[lamt[l]])
                op(dve, lambda: nc.vector.tensor_scalar(out=lamt[l].t[:, 1:2], in0=ptb[l].t[:, PT_GSUB:PT_GSUB + 1],
                                                        scalar1=1.0 - lam_init, scalar2=None, op0=ALU.mult),
                   rd=[ptb[l]], wr=[lamt[l]])
                if dbg:
                    dma_store(k, sp, modd[l].rearrange("p (j t) -> p j t", t=2), mods[l], mods[l].t[:])
            barrier(k)

        blocks = [(512 * i, 512) for i in range(8)] + [(L, LC)]

        for l in range(nlayers if stop_after != 'p0' else 0):
            last = (l == DEPTH - 1)
            with_ctx = not last
            src = xT if l == 0 else xs_b

            with ExitStack() as p1:
                W1 = sb(p1, "W1", [128, NKC, W1C], BF16)
                for c0 in range(0, W1C, 1856):
                    dma_load(k, pool, W1, W1.t[:, :, c0:c0 + 1856],
                             w1[l, :, c0:c0 + 1856].rearrange("(k p) c -> p k c", p=128), part=(c0 > 0))
                xin = [sb(p1, "xin%d" % i, [128, NKC, 512], F32) for i in range(2)]
                sq = sb(p1, "sq", [128, NKC, 512], BF16)
                rs = sb(p1, "rs", [128, 512], F32)
                tmpf = [sb(p1, "tmpf%d" % i, [128, 512], F32) for i in range(2)]
                hT = [sb(p1, "hT%d" % i, [128, NKC, 512], BF16) for i in range(2)]
                rop = [[sb(p1, "rop%d_%d" % (i, j), [128, 512], F32) for j in range(4)] for i in range(2)]
                sqq = [sb(p1, "sqq%d" % i, [128, 512], BF16) for i in range(2)]
                rq = [sb(p1, "rq%d" % i, [128, 512], F32) for i in range(2)]
                t1 = [sb(p1, "t1_%d" % i, [128, 512], F32) for i in range(2)]
                t2 = [sb(p1, "t2_%d" % i, [128, 512], F32) for i in range(2)]
                outb = [sb(p1, "outb%d" % i, [128, 512], BF16) for i in range(3)]
                vout = [sb(p1, "vout%d" % i, [128, 10, 128], BF16) for i in range(2)]
                pss = ps(p1, "pss")
                pv0 = ps(p1, "pv0")
                pm = [ps(p1, "pm%d" % i) for i in range(2)]
                pw = [ps(p1, "pw%d" % i) for i in range(2)]
                pq = [ps(p1, "pq%d" % i) for i in range(2)]
                for i in range(2):
                    op(pool, lambda i=i: nc.gpsimd.memset(vout[i].t[:], 1.0), wr=[vout[i]])

                jobs1 = []
                for s in range(4):
                    jobs1.append((s, 4 + s, 'A', 64, True, [(qs[s], 0, 128)]))
                for g in range(2):
                    jobs1.append((8 + g, 10 + g, 'A', 64, False, [(ks[2 * g], 0, 64), (ks[2 * g + 1], 64, 128)]))
                for s in range(2):
                    jobs1.append((12 + s, None, None, 64, True, [(qs[4 + s], 0, 128)]))
                for s in range(2):
                    jobs1.append((14 + s, None, None, 64, False, [(ks[4 + 2 * s], 0, 64), (ks[5 + 2 * s], 64, 128)]))
                for s in range(2):
                    jobs1.append((16 + s, 18 + s, 'C', 32, True, [(qs[6 + s], 0, 128)]))
                for s in range(2):
                    jobs1.append((20 + s, 22 + s, 'C', 32, False,
                                  [(ks[8 + 4 * s + j], 32 * j, 32 * j + 32) for j in range(4)]))

                cnt1 = 0
                cnto = 0
                for bi, (t0, n) in enumerate(blocks if CUT > 1 else []):
                    if CUT < 10 and bi > 0:
                        break
                    tcol = 0 if t0 < L else 1
                    xb = xin[bi % 2]
                    hb = hT[bi % 2]
                    rp = rop[bi % 2]
                    dma_load(k, sp, xb, xb.t[:, :, 0:n], src[:, t0:t0 + n].rearrange("(c p) t -> p c t", p=128))
                    for j, (tab, idx) in enumerate(((ropeA, 0), (ropeA, 1), (ropeC, 0), (ropeC, 1))):
                        dma_load(k, sp, rp[j], rp[j].t[:, 0:n], tab[idx, :, t0:t0 + n])
                    op(act, lambda: nc.scalar.activation(out=sq.t[:, :, 0:n], in_=xb.t[:, :, 0:n], func=AF.Square),
                       rd=[xb], wr=[sq])

                    def mm_ss():
                        ins = None
                        for c in range(NKC):
                            ins = nc.tensor.matmul(pss.t[:, 0:n], lhsT=ones_bf.t[:], rhs=sq.t[:, c, 0:n],
                                                   start=(c == 0), stop=(c == NKC - 1))
                        return ins
                    op(pe, mm_ss, rd=[ones_bf, sq], wr=[pss])
                    op(act, lambda: nc.scalar.activation(out=rs.t[:, 0:n], in_=pss.t[:, 0:n], func=AF.Ln,
                                                         scale=1.0 / D, bias=EPS), rd=[pss], wr=[rs])
                    op(act, lambda: nc.scalar.activation(out=rs.t[:, 0:n], in_=rs.t[:, 0:n], func=AF.Exp, scale=-0.5),
                       rd=[rs], wr=[rs])
                    for c in range(NKC):
                        tf = tmpf[c % 2]
                        op(dve, lambda c=c, tf=tf: nc.vector.scalar_tensor_tensor(
                            out=tf.t[:, 0:n], in0=xb.t[:, c, 0:n], scalar=gm[l].t[:, 0, c, tcol:tcol + 1],
                            in1=rs.t[:, 0:n], op0=ALU.mult, op1=ALU.mult), rd=[xb, gm[l], rs], wr=[tf])
                        op(act, lambda c=c, tf=tf: nc.scalar.activation(
                            out=hb.t[:, c, 0:n], in_=tf.t[:, 0:n], func=AF.Identity,
                            bias=mods[l].t[:, c, tcol:tcol + 1], scale=1.0), rd=[tf, mods[l]], wr=[hb])

                    for tt in range(n // 128 if CUT > 2 else 0):
                        vo = vout[cnt1 % 2]
                        cnt1 += 1

                        def mm_v(tt=tt):
                            ins = None
                            for kc in range(NKC):
                                ins = nc.tensor.matmul(pv0.t[:, 0:512], lhsT=hb.t[:, kc, tt * 128:(tt + 1) * 128],
                                                       rhs=W1.t[:, kc, 3072:3584], start=(kc == 0), stop=(kc == NKC - 1))
                            for kc in range(NKC):
                                ins = nc.tensor.matmul(pss.t[:, 0:128], lhsT=hb.t[:, kc, tt * 128:(tt + 1) * 128],
                                                       rhs=W1.t[:, kc, 3584:3712], start=(kc == 0), stop=(kc == NKC - 1))
                            return ins
                        op(pe, mm_v, rd=[hb, W1], wr=[pv0, pss])
                        op(dve, lambda vo=vo: nc.vector.tensor_copy(
                            out=vo.t[:, 0:8, 0:64], in_=pv0.t[:, 0:512].rearrange("p (v c) -> p v c", c=64)),
                           rd=[pv0], wr=[vo])
                        op(dve, lambda vo=vo: nc.vector.tensor_copy(
                            out=vo.t[:, 8:10, 0:64], in_=pss.t[:, 0:128].rearrange("p (v c) -> p v c", c=64)),
                           rd=[pss], wr=[vo])
                        tk = t0 + tt * 128
                        dma_store(k, sp, vs[:, tk:tk + 128, :].rearrange("v p c -> p v c"), vo, vo.t[:])

                    for (sm, sw, rope, dd, is_q, dests) in (jobs1 if CUT > 3 else []):
                        i2 = cnt1 % 2
                        cnt1 += 1
                        pmb, pwb, pqb = pm[i2], pw[i2], pq[i2]
                        sqb, rqb, t1b, t2b = sqq[i2], rq[i2], t1[i2], t2[i2]

                        def mm_slab(slab, dst):
                            def f():
                                ins = None
                                for kc in range(NKC):
                                    ins = nc.tensor.matmul(dst.t[:, 0:n], lhsT=W1.t[:, kc, slab * 128:(slab + 1) * 128],
                                                           rhs=hb.t[:, kc, 0:n], start=(kc == 0), stop=(kc == NKC - 1))
                                return ins
                            return f
                        op(pe, mm_slab(sm, pmb), rd=[W1, hb], wr=[pmb])
                        if sw is not None:
                            op(pe, mm_slab(sw, pwb), rd=[W1, hb], wr=[pwb])
                        op(act, lambda: nc.scalar.activation(out=sqb.t[:, 0:n], in_=pmb.t[:, 0:n], func=AF.Square),
                           rd=[pmb], wr=[sqb])
                        blk = blk64 if dd == 64 else blk32
                        op(pe, lambda: nc.tensor.matmul(pqb.t[:, 0:n], lhsT=blk.t[:], rhs=sqb.t[:, 0:n], start=True, stop=True),
                           rd=[blk, sqb], wr=[pqb])
                        if is_q:
                            a, b = 1.0, dd * EPS
                        else:
                            a, b = 1.0 / dd, EPS
                        op(act, lambda: nc.scalar.activation(out=rqb.t[:, 0:n], in_=pqb.t[:, 0:n], func=AF.Ln, scale=a, bias=b),
                           rd=[pqb], wr=[rqb])
                        op(act, lambda: nc.scalar.activation(out=rqb.t[:, 0:n], in_=rqb.t[:, 0:n], func=AF.Exp, scale=-0.5),
                           rd=[rqb], wr=[rqb])
                        ob = outb[cnto % 3]
                        cnto += 1
                        gcol = PT_GTAB + sm
                        if CUT < 5:
                            continue
                        if os.environ.get('KSKIPB') and rope is None:
                            continue
                        if os.environ.get('KSKIPR') and rope is not None:
                            continue
                        if rope is not None:
                            cosb, sinb = (rp[0], rp[1]) if rope == 'A' else (rp[2], rp[3])
                            gcs = PT_GTAB + sw
                            op(dve, lambda: nc.vector.scalar_tensor_tensor(
                                out=t1b.t[:, 0:n], in0=cosb.t[:, 0:n], scalar=ptb[l].t[:, gcol:gcol + 1],
                                in1=pmb.t[:, 0:n], op0=ALU.mult, op1=ALU.mult), rd=[pmb, ptb[l], cosb], wr=[t1b])
                            op(dve, lambda: nc.vector.scalar_tensor_tensor(
                                out=t2b.t[:, 0:n], in0=sinb.t[:, 0:n], scalar=ptb[l].t[:, gcs:gcs + 1],
                                in1=pwb.t[:, 0:n], op0=ALU.mult, op1=ALU.mult), rd=[pwb, ptb[l], sinb], wr=[t2b])
                            if CUT < 6:
                                continue
                            op(pool, lambda: nc.gpsimd.tensor_tensor(out=t1b.t[:, 0:n], in0=t1b.t[:, 0:n], in1=t2b.t[:, 0:n],
                                                                     op=ALU.add), rd=[t1b, t2b], wr=[t1b])
                            op(pool, lambda: nc.gpsimd.tensor_tensor(out=ob.t[:, 0:n], in0=t1b.t[:, 0:n], in1=rqb.t[:, 0:n],
                                                                     op=ALU.mult), rd=[t1b, rqb], wr=[ob])
                        else:
                            op(dve, lambda: nc.vector.scalar_tensor_tensor(
                                out=ob.t[:, 0:n], in0=rqb.t[:, 0:n], scalar=ptb[l].t[:, gcol:gcol + 1],
                                in1=pmb.t[:, 0:n], op0=ALU.mult, op1=ALU.mult), rd=[pmb, ptb[l], rqb], wr=[ob])
                        for (dap, r0, r1) in (dests if CUT > 6 else []):
                            dma_store(k, sp, dap[r0:r1, t0:t0 + n], ob, ob.t[r0:r1, 0:n])
                barrier(k)
            if stop_after == (l, 1):
                break

            with ExitStack() as p2:
                NS, NP, LA = 3, 4, 2
                Kb = [[sb(p2, "Kb%d_%d" % (i, j), [128, T], BF16) for j in range(2)] for i in range(2)]
                Vb = [sb(p2, "Vb%d" % i, [128, 34, 128], BF16) for i in range(2)]
                Qb = [sb(p2, "Qb%d" % i, [128, T], BF16) for i in range(2)]
                Bb = [sb(p2, "Bb%d" % i, [128, 25, 128], F32) for i in range(2)]
                Pb = [sb(p2, "Pb%d" % i, [128, 512], BF16) for i in range(NP)]
                rb = [sb(p2, "rb%d" % i, [128, 512], F32) for i in range(2)]
                ab = [sb(p2, "ab%d" % i, [128, 512], F32) for i in range(2)]
                sqc = sb(p2, "sqc", [128, 512], BF16)
                rsc = sb(p2, "rsc", [128, 512], F32)
                osb = [sb(p2, "osb%d" % i, [128, 512], BF16) for i in range(3)]
                Sp = [ps(p2, "Sp%d" % i) for i in range(NS)]
                Op = [ps(p2, "Op%d" % i) for i in range(4)]
                Mp = ps(p2, "Mp")

                jobs = []
                for h in range(8):
                    jobs.append(dict(kind='A', kv=[2 * (h // 4) + (h % 2)], vh=h // 4, q=h // 2, row=64 * h))
                for h in range(4):
                    jobs.append(dict(kind='B', kv=[4 + h], vh=2 + h, q=4 + h // 2, row=512 + 64 * h, bh=h))
                for h in range(4):
                    jobs.append(dict(kind='C', kv=[8 + 2 * h, 9 + 2 * h], vh=6 + h, q=6 + h // 2, row=768 + 64 * h))

                if os.environ.get('KJOBS'):
                    jobs = [jb for jb in jobs if jb['kind'] in os.environ['KJOBS']]

                def load_job(ji):
                    jb = jobs[ji]
                    st = ji % 2
                    for m, kv in enumerate(jb['kv']):
                        dma_load(k, sp, Kb[st][m], Kb[st][m].t[:], ks[kv])
                    dma_load(k, sp, Qb[st], Qb[st].t[:], qs[jb['q']])
                    dma_load(k, sp, Vb[st], Vb[st].t[:], vs[jb['vh']].rearrange("(kt p) c -> p kt c", p=128))
                    if jb['kind'] == 'B':
                        dma_load(k, sp, Bb[st], Bb[st].t[:], biasB[l, jb['bh']])

                steps = []
                bc = 0
                ocnt = [0]
                for ji, jb in enumerate(jobs):
                    st = ji % 2
                    qblocks = [(512 * i, 512, False) for i in range(8)]
                    if with_ctx:
                        qblocks.append((L, LC, True))
                    nmap = len(jb['kv'])
                    first_of_job = True
                    for (t0, n, isctx) in qblocks:
                        Os = [Op[(nmap * bc + m) % 4] for m in range(nmap)]
                        bc += 1
                        blk_steps = []
                        Qt, Vt = Qb[st], Vb[st]
                        if jb['kind'] == 'B' and not isctx:
                            qb = t0 // 512
                            for s in range(5):
                                def S_fn(Sb, s=s, qb=qb, Kt=Kb[st][0], Qt=Qt, Bt=Bb[st]):
                                    ins = None
                                    for sbk in range(4):
                                        i = 4 * qb + sbk
                                        kt = min(max(i - 2, 0), 27) + s
                                        pat = s if 2 <= i <= 29 else {0: 5, 1: 10, 30: 15, 31: 20}[i] + s
                                        nc.tensor.matmul(Sb.t[:, sbk * 128:(sbk + 1) * 128], lhsT=Kt.t[:, kt * 128:(kt + 1) * 128],
                                                         rhs=Qt.t[:, i * 128:(i + 1) * 128], start=True, stop=False)
                                        ins = nc.tensor.matmul(Sb.t[:, sbk * 128:(sbk + 1) * 128], lhsT=ident.t[:],
                                                               rhs=Bt.t[:, pat, :], start=False, stop=True)
                                    return ins

                                def PV_fn(Pt, O, s=s, qb=qb, Vt=Vt, Qt=Qt, t0=t0):
                                    ins = None
                                    if s == 0:
                                        nc.tensor.matmul(O.t[:, 0:512], lhsT=zbf.t[:], rhs=Qt.t[:, t0:t0 + 512],
                                                         start=True, stop=False)
                                    for sbk in range(4):
                                        i = 4 * qb + sbk
                                        kt = min(max(i - 2, 0), 27) + s
                                        ins = nc.tensor.matmul(O.t[:, sbk * 128:(sbk + 1) * 128], lhsT=Vt.t[:, kt, :],
                                                               rhs=Pt.t[:, sbk * 128:(sbk + 1) * 128], start=False, stop=False)
                                    return ins
                                blk_steps.append(dict(S=S_fn, PV=PV_fn, O=Os[0], n=512, rdS=[Kb[st][0], Qt, Bb[st], ident],
                                                      rdV=[Vt, zbf, Qt]))
                            for j in range(2):
                                kt = 32 + j

                                def S_fn(Sb, kt=kt, Kt=Kb[st][0], Qt=Qt, t0=t0):
                                    return nc.tensor.matmul(Sb.t[:, 0:512], lhsT=Kt.t[:, kt * 128:(kt + 1) * 128],
                                                            rhs=Qt.t[:, t0:t0 + 512], start=True, stop=True)

                                def PV_fn(Pt, O, kt=kt, Vt=Vt, j=j):
                                    return nc.tensor.matmul(O.t[:, 0:512], lhsT=Vt.t[:, kt, :], rhs=Pt.t[:, 0:512],
                                                            start=False, stop=(j == 1))
                                blk_steps.append(dict(S=S_fn, PV=PV_fn, O=Os[0], n=512, rdS=[Kb[st][0], Qt], rdV=[Vt]))
                        else:
                            kts = [32, 33] if isctx else list(range(34))
                            for ki, kt in enumerate(kts):
                                for m in range(nmap):
                                    def S_fn(Sb, kt=kt, Kt=Kb[st][m], Qt=Qt, t0=t0, n=n):
                                        return nc.tensor.matmul(Sb.t[:, 0:n], lhsT=Kt.t[:, kt * 128:(kt + 1) * 128],
                                                                rhs=Qt.t[:, t0:t0 + n], start=True, stop=True)

                                    def PV_fn(Pt, O, kt=kt, Vt=Vt, n=n, ki=ki, nk=len(kts)):
                                        return nc.tensor.matmul(O.t[:, 0:n], lhsT=Vt.t[:, kt, :], rhs=Pt.t[:, 0:n],
                                                                start=(ki == 0), stop=(ki == nk - 1))
                                    blk_steps.append(dict(S=S_fn, PV=PV_fn, O=Os[m], n=n, rdS=[Kb[st][m], Qt], rdV=[Vt]))
                        if first_of_job:
                            blk_steps[0]['pre'] = ji
                            first_of_job = False
                        blk_steps[-1]['post'] = (jb, t0, n, Os)
                        steps.extend(blk_steps)

                def post_block(jb, t0, n, Os):
                    ob = osb[ocnt[0] % 3]
                    ocnt[0] += 1
                    if jb['kind'] != 'C':
                        O = Os[0]
                        r = rb[0]
                        op(dve, lambda: nc.vector.reciprocal(out=r.t[0:64, 0:n], in_=O.t[64:128, 0:n]), rd=[O], wr=[r])
                        op(dve, lambda: nc.vector.tensor_tensor(out=ob.t[0:64, 0:n], in0=O.t[0:64, 0:n], in1=r.t[0:64, 0:n],
                                                                op=ALU.mult), rd=[O, r], wr=[ob])
                    else:
                        O1, O2 = Os
                        op(dve, lambda: nc.vector.reciprocal(out=rb[0].t[0:64, 0:n], in_=O1.t[64:128, 0:n]), rd=[O1], wr=[rb[0]])
                        op(dve, lambda: nc.vector.reciprocal(out=rb[1].t[0:64, 0:n], in_=O2.t[64:128, 0:n]), rd=[O2], wr=[rb[1]])
                        op(dve, lambda: nc.vector.tensor_tensor(out=ab[0].t[0:64, 0:n], in0=O1.t[0:64, 0:n],
                                                                in1=rb[0].t[0:64, 0:n], op=ALU.mult), rd=[O1, rb[0]], wr=[ab[0]])
                        op(dve, lambda: nc.vector.scalar_tensor_tensor(
                            out=ab[1].t[0:64, 0:n], in0=rb[1].t[0:64, 0:n], scalar=lamt[l].t[0:64, 0:1],
                            in1=O2.t[0:64, 0:n], op0=ALU.mult, op1=ALU.mult), rd=[O2, lamt[l], rb[1]], wr=[ab[1]])
                        op(pool, lambda: nc.gpsimd.tensor_tensor(out=ab[0].t[0:64, 0:n], in0=ab[0].t[0:64, 0:n],
                                                                 in1=ab[1].t[0:64, 0:n], op=ALU.add), rd=[ab[0], ab[1]], wr=[ab[0]])
                        op(act, lambda: nc.scalar.activation(out=sqc.t[0:64, 0:n], in_=ab[0].t[0:64, 0:n], func=AF.Square),
                           rd=[ab[0]], wr=[sqc])
                        op(pe, lambda: nc.tensor.matmul(Mp.t[0:64, 0:n], lhsT=ones_bf.t[0:64, 0:64], rhs=sqc.t[0:64, 0:n],
                                                        start=True, stop=True), rd=[ones_bf, sqc], wr=[Mp])
                        op(act, lambda: nc.scalar.activation(out=rsc.t[0:64, 0:n], in_=Mp.t[0:64, 0:n], func=AF.Ln,
                                                             scale=1.0 / 64, bias=EPS), rd=[Mp], wr=[rsc])
                        op(act, lambda: nc.scalar.activation(out=rsc.t[0:64, 0:n], in_=rsc.t[0:64, 0:n], func=AF.Exp, scale=-0.5),
                           rd=[rsc], wr=[rsc])
                        op(dve, lambda: nc.vector.scalar_tensor_tensor(
                            out=ob.t[0:64, 0:n], in0=ab[0].t[0:64, 0:n], scalar=lamt[l].t[0:64, 1:2],
                            in1=rsc.t[0:64, 0:n], op0=ALU.mult, op1=ALU.mult), rd=[ab[0], lamt[l], rsc], wr=[ob])
                    dma_store(k, sp, attn[jb['row']:jb['row'] + 64, t0:t0 + n], ob, ob.t[0:64, 0:n])

                load_job(0)
                ns = len(steps)
                for i in range(ns + LA):
                    if i < ns:
                        stp = steps[i]
                        Sb, Pt = Sp[i % NS], Pb[i % NP]
                        op(pe, lambda: stp['S'](Sb), rd=stp['rdS'], wr=[Sb])
                        nn = stp['n']
                        op(act, lambda: nc.scalar.activation(out=Pt.t[:, 0:nn], in_=Sb.t[:, 0:nn], func=AF.Exp),
                           rd=[Sb], wr=[Pt])
                    if i >= LA:
                        j = i - LA
                        stj = steps[j]
                        Pj = Pb[j % NP]
                        if 'pre' in stj and stj['pre'] + 1 < len(jobs):
                            load_job(stj['pre'] + 1)
                        op(pe, lambda: stj['PV'](Pj, stj['O']), rd=[Pj] + stj['rdV'], wr=[stj['O']])
                        if 'post' in stj:
                            post_block(*stj['post'])
                barrier(k)
            if stop_after == (l, 2):
                break

            blocks3 = blocks if with_ctx else blocks[:8]
            with ExitStack() as p3:
                Wo = sb(p3, "Wo", [128, NKC, D], BF16)
                dma_load(k, pool, Wo, Wo.t[:], w_out[l].rearrange("(k p) c -> p k c", p=128))
                at = [sb(p3, "at%d" % i, [128, NKC, 512], BF16) for i in range(2)]
                xa = [sb(p3, "xa%d" % i, [128, NKC, 512], F32) for i in range(2)]
                po = [ps(p3, "po%d" % i) for i in range(4)]
                tmp3 = [sb(p3, "tmp3_%d" % i, [128, 512], F32) for i in range(2)]
                for bi, (t0, n) in enumerate(blocks3):
                    tcol = 0 if t0 < L else 1
                    a, x = at[bi % 2], xa[bi % 2]
                    dma_load(k, sp, a, a.t[:, :, 0:n], attn[:, t0:t0 + n].rearrange("(c p) t -> p c t", p=128))
                    dma_load(k, sp, x, x.t[:, :, 0:n], src[:, t0:t0 + n].rearrange("(c p) t -> p c t", p=128))
                    for oc in range(NKC):
                        pb = po[oc % 4]

                        def mm_o(oc=oc, pb=pb):
                            ins = None
                            for kc in range(NKC):
                                ins = nc.tensor.matmul(pb.t[:, 0:n], lhsT=Wo.t[:, kc, oc * 128:(oc + 1) * 128],
                                                       rhs=a.t[:, kc, 0:n], start=(kc == 0), stop=(kc == NKC - 1))
                            return ins
                        op(pe, mm_o, rd=[Wo, a], wr=[pb])
                        tb_ = tmp3[oc % 2]
                        op(act, lambda oc=oc, pb=pb, tb_=tb_: nc.scalar.activation(
                            out=tb_.t[:, 0:n], in_=pb.t[:, 0:n], func=AF.Copy,
                            scale=mods[l].t[:, 16 + oc, tcol:tcol + 1]), rd=[pb, mods[l]], wr=[tb_])
                        op(dve, lambda oc=oc, tb_=tb_: nc.vector.tensor_tensor(
                            out=x.t[:, oc, 0:n], in0=x.t[:, oc, 0:n], in1=tb_.t[:, 0:n], op=ALU.add), rd=[tb_, x], wr=[x])
                    dma_store(k, sp, xs_a[:, t0:t0 + n].rearrange("(c p) t -> p c t", p=128), x, x.t[:, :, 0:n])
                barrier(k)
            if stop_after == (l, 3):
                break

            with ExitStack() as p4:
                Wu = sb(p4, "Wu", [128, NKC, 2 * FFN], BF16)
                Wd = sb(p4, "Wd", [128, NJ, D], BF16)
                for c0 in range(0, 2 * FFN, 1408):
                    dma_load(k, pool, Wu, Wu.t[:, :, c0:c0 + 1408],
                             w_up[l, :, c0:c0 + 1408].rearrange("(k p) c -> p k c", p=128), part=(c0 > 0))
                for j0 in range(0, NJ, 11):
                    dma_load(k, pool, Wd, Wd.t[:, j0:j0 + 11, :],
                             w_down[l, j0 * 128:(j0 + 11) * 128, :].rearrange("(j p) c -> p j c", p=128), part=(j0 > 0))
                NW = TB + 2
                xw = sb(p4, "xw", [128, NKC, NW], F32)
                h2 = sb(p4, "h2", [128, NKC, NW], BF16)
                actb = sb(p4, "actb", [128, NJ, NW], BF16)
                rs2 = sb(p4, "rs2", [128, NW], F32)
                tf2 = [sb(p4, "tf2_%d" % i, [128, NW], F32) for i in range(2)]
                tu = [sb(p4, "tu%d" % i, [128, NW], F32) for i in range(2)]
                tg = [sb(p4, "tg%d" % i, [128, NW], F32) for i in range(2)]
                tm = [sb(p4, "tm%d" % i, [128, NW], F32) for i in range(3)]
                tmc = [0]
                xc = [sb(p4, "xc%d" % i, [128, NW], F32) for i in range(2)]
                pu = [ps(p4, "pu%d" % i) for i in range(2)]
                pg = [ps(p4, "pg%d" % i) for i in range(2)]
                pd = [ps(p4, "pd%d" % i) for i in range(3)]
                pr = ps(p4, "pr")
                fblocks = []
                for s0 in range(0, L, TB):
                    fblocks.append((0, L, s0, min(TB, L - s0)))
                if with_ctx:
                    fblocks.append((L, LC, 0, LC))
                dst_all = yT if last else xs_b
                for bi, (base, slen, s0, tn) in enumerate(fblocks):
                    tcol = 0 if base < L else 1
                    nw = tn + 2
                    lo = s0 - 1
                    hi = s0 + tn + 1
                    c_lo = max(lo, 0)
                    c_hi = min(hi, slen)
                    if c_lo > lo:
                        op(pool, lambda: nc.gpsimd.memset(xw.t[:, :, 0:1], 1.0), wr=[xw])
                    if c_hi < hi:
                        op(pool, lambda: nc.gpsimd.memset(xw.t[:, :, nw - 1:nw], 1.0), wr=[xw])
                    dma_load(k, sp, xw, xw.t[:, :, c_lo - lo:c_hi - lo],
                             xs_a[:, base + c_lo:base + c_hi].rearrange("(c p) t -> p c t", p=128))
                    op(act, lambda: nc.scalar.activation(out=actb.t[:, 0:NKC, 0:nw], in_=xw.t[:, :, 0:nw], func=AF.Square),
                       rd=[xw], wr=[actb])

                    def mm_ss2():
                        ins = None
                        for c in range(NKC):
                            ins = nc.tensor.matmul(pr.t[:, 0:nw], lhsT=ones_bf.t[:], rhs=actb.t[:, c, 0:nw],
                                                   start=(c == 0), stop=(c == NKC - 1))
                        return ins
                    op(pe, mm_ss2, rd=[ones_bf, actb], wr=[pr])
                    op(act, lambda: nc.scalar.activation(out=rs2.t[:, 0:nw], in_=pr.t[:, 0:nw], func=AF.Ln,
                                                         scale=1.0 / D, bias=EPS), rd=[pr], wr=[rs2])
                    op(act, lambda: nc.scalar.activation(out=rs2.t[:, 0:nw], in_=rs2.t[:, 0:nw], func=AF.Exp, scale=-0.5),
                       rd=[rs2], wr=[rs2])
                    for c in range(NKC):
                        tf = tf2[c % 2]
                        op(dve, lambda c=c, tf=tf: nc.vector.scalar_tensor_tensor(
                            out=tf.t[:, 0:nw], in0=xw.t[:, c, 0:nw], scalar=gm[l].t[:, 1, c, tcol:tcol + 1],
                            in1=rs2.t[:, 0:nw], op0=ALU.mult, op1=ALU.mult), rd=[xw, gm[l], rs2], wr=[tf])
                        op(act, lambda c=c, tf=tf: nc.scalar.activation(
                            out=h2.t[:, c, 0:nw], in_=tf.t[:, 0:nw], func=AF.Identity,
                            bias=mods[l].t[:, 24 + c, tcol:tcol + 1], scale=1.0), rd=[tf, mods[l]], wr=[h2])
                    if c_lo > lo:
                        op(pool, lambda: nc.gpsimd.memset(h2.t[:, :, 0:1], 0.0), wr=[h2])
                    if c_hi < hi:
                        op(pool, lambda: nc.gpsimd.memset(h2.t[:, :, nw - 1:nw], 0.0), wr=[h2])
                    for j in range(NJ):
                        i2 = j % 2
                        pub, pgb, tub, tgb = pu[i2], pg[i2], tu[i2], tg[i2]

                        def mm_up(col, dst):
                            def f():
                                ins = None
                                for kc in range(NKC):
                                    ins = nc.tensor.matmul(dst.t[:, 0:nw], lhsT=Wu.t[:, kc, col * 128:(col + 1) * 128],
                                                           rhs=h2.t[:, kc, 0:nw], start=(kc == 0), stop=(kc == NKC - 1))
                                return ins
                            return f
                        op(pe, mm_up(j, pub), rd=[Wu, h2], wr=[pub])
                        op(pe, mm_up(NJ + j, pgb), rd=[Wu, h2], wr=[pgb])
                        for ei, (col, pp, tt_) in enumerate(((j, pub, tub), (NJ + j, pgb, tgb))):
                            cw0 = PT_CW + 3 * col
                            cb0 = PT_CB + col
                            op(act, lambda pp=pp, tt_=tt_, cw0=cw0, cb0=cb0: nc.scalar.activation(
                                out=tt_.t[:, 0:tn], in_=pp.t[:, 0:tn], func=AF.Identity,
                                scale=ptb[l].t[:, cw0:cw0 + 1], bias=ptb[l].t[:, cb0:cb0 + 1]), rd=[pp, ptb[l]], wr=[tt_])
                            for tap in (1, 2):
                                tmb = tm[tmc[0] % 3]
                                tmc[0] += 1
                                op(act, lambda pp=pp, tmb=tmb, cw0=cw0, tap=tap: nc.scalar.activation(
                                    out=tmb.t[:, 0:tn], in_=pp.t[:, tap:tap + tn], func=AF.Copy,
                                    scale=ptb[l].t[:, cw0 + tap:cw0 + tap + 1]), rd=[pp, ptb[l]], wr=[tmb])
                                if ei == 0:
                                    op(dve, lambda tt_=tt_, tmb=tmb: nc.vector.tensor_tensor(
                                        out=tt_.t[:, 0:tn], in0=tt_.t[:, 0:tn], in1=tmb.t[:, 0:tn], op=ALU.add),
                                       rd=[tmb, tt_], wr=[tt_])
                                else:
                                    op(pool, lambda tt_=tt_, tmb=tmb: nc.gpsimd.tensor_tensor(
                                        out=tt_.t[:, 0:tn], in0=tt_.t[:, 0:tn], in1=tmb.t[:, 0:tn], op=ALU.add),
                                       rd=[tmb, tt_], wr=[tt_])
                        op(act, lambda: nc.scalar.activation(out=tgb.t[:, 0:tn], in_=tgb.t[:, 0:tn], func=AF.Silu),
                           rd=[tgb], wr=[tgb])
                        op(dve, lambda j=j: nc.vector.tensor_tensor(out=actb.t[:, j, 0:tn], in0=tub.t[:, 0:tn],
                                                                    in1=tgb.t[:, 0:tn], op=ALU.mult),
                           rd=[tub, tgb], wr=[actb])
                    for oc in range(NKC):
                        pdb = pd[oc % 3]
                        xcb = xc[oc % 2]
                        dma_load(k, sp, xcb, xcb.t[:, 0:tn], xs_a[oc * 128:(oc + 1) * 128, base + s0:base + s0 + tn])

                        def mm_dn(oc=oc, pdb=pdb):
                            ins = None
                            for j in range(NJ):
                                ins = nc.tensor.matmul(pdb.t[:, 0:tn], lhsT=Wd.t[:, j, oc * 128:(oc + 1) * 128],
                                                       rhs=actb.t[:, j, 0:tn], start=(j == 0), stop=(j == NJ - 1))
                            return ins
                        op(pe, mm_dn, rd=[Wd, actb], wr=[pdb])
                        tmb = tm[tmc[0] % 3]
                        tmc[0] += 1
                        op(act, lambda oc=oc, pdb=pdb, tmb=tmb: nc.scalar.activation(
                            out=tmb.t[:, 0:tn], in_=pdb.t[:, 0:tn], func=AF.Copy,
                            scale=mods[l].t[:, 40 + oc, tcol:tcol + 1]), rd=[pdb, mods[l]], wr=[tmb])
                        op(dve, lambda xcb=xcb, tmb=tmb: nc.vector.tensor_tensor(
                            out=xcb.t[:, 0:tn], in0=xcb.t[:, 0:tn], in1=tmb.t[:, 0:tn], op=ALU.add), rd=[tmb, xcb], wr=[xcb])
                        if last:
                            dma_store(k, sp, yT[oc * 128:(oc + 1) * 128, s0:s0 + tn], xcb, xcb.t[:, 0:tn])
                        else:
                            dma_store(k, sp, xs_b[oc * 128:(oc + 1) * 128, base + s0:base + s0 + tn], xcb, xcb.t[:, 0:tn])
                barrier(k)
        barrier(k)
    return nc


def _rope_tables():
    def ang(dim):
        half = dim // 2
        inv = (10000.0 ** (-np.arange(0, half, 2, dtype=np.float32) / half)).astype(np.float32)
        t = np.arange(L)
        row = (t // 64).astype(np.float32)
        col = (t % 64).astype(np.float32)
        return np.concatenate([row[:, None] * inv, col[:, None] * inv], axis=-1).astype(np.float32)

    out = []
    for dim in (64, 32):
        a = ang(dim)
        cos = np.cos(a).astype(np.float32)
        sin = np.sin(a).astype(np.float32)
        d = np.arange(dim)
        c = cos[:, d // 2].T
        s = sin[:, d // 2].T * np.where(d % 2 == 0, -1.0, 1.0)[:, None]
        tab = np.zeros((2, 128, T), np.float32)
        rep = 128 // dim
        tab[0, :, :L] = np.tile(c, (rep, 1))
        tab[1, :, :L] = np.tile(s, (rep, 1))
        tab[0, :, L:] = 1.0
        out.append(tab)
    return out


def _bias_patterns():
    classes = [2, 0, 1, 30, 31]
    valid = np.zeros((25, 128, 128), bool)
    ri = np.zeros((25, 128, 128), np.int64)
    ci = np.zeros((25, 128, 128), np.int64)
    kp = np.arange(128)[:, None]
    qf = np.arange(128)[None, :]
    for cidx, i in enumerate(classes):
        for s in range(5):
            kt = min(max(i - 2, 0), 27) + s
            qr = 2 * i + qf // 64
            qc = qf % 64
            kr = 2 * kt + kp // 64
            kc = kp % 64
            rs_ = np.clip(qr - 4, 0, 56)
            cs_ = np.clip(qc - 8, 0, 48)
            v = (kr >= rs_) & (kr < rs_ + 8) & (kc >= cs_) & (kc < cs_ + 16)
            p = cidx * 5 + s
            valid[p] = v
            ri[p] = np.where(v, kr - qr + 7, 0)
            ci[p] = np.where(v, kc - qc + 15, 0)
    return valid, ri, ci


def _prep_shared(inp):
    f = lambda a: np.ascontiguousarray(np.asarray(a, dtype=np.float32))
    w_in = f(inp["w_in"])
    sw = np.arange(64) ^ 1
    cols = []
    qa = np.arange(0, 512)
    cols += [qa]
    cols += [(qa // 64) * 64 + sw[qa % 64]]
    ka = [512 + 64 * g + np.arange(64) for g in range(2)]
    cols += [np.concatenate([ka[0], ka[0]]), np.concatenate([ka[1], ka[1]])]
    cols += [np.concatenate([ka[0][sw], ka[0][sw]]), np.concatenate([ka[1][sw], ka[1][sw]])]
    cols += [768 + np.arange(256)]
    cols += [1024 + np.arange(256)]
    qc = 1536 + np.arange(256)
    kc = 1792 + np.arange(256)
    cols += [qc, qc ^ 1, kc, kc ^ 1]
    cols += [640 + np.arange(128), 1280 + np.arange(256), 2048 + np.arange(256)]
    cols = np.concatenate(cols)
    assert cols.shape[0] == W1C
    w1 = np.ascontiguousarray(w_in[:, :, cols])

    ptab = np.zeros((DEPTH, 128, NPT), np.float32)
    p = np.arange(128)
    for l in range(DEPTH):
        ptab[l, :, PT_BADA:PT_BADA + 48] = f(inp["b_ada"])[l].reshape(48, 128).T
        ptab[l, :, PT_G1:PT_G1 + 8] = f(inp["g_norm1"])[l].reshape(8, 128).T
        ptab[l, :, PT_G2:PT_G2 + 8] = f(inp["g_norm2"])[l].reshape(8, 128).T
        gqa, gka = f(inp["gq_a"])[l], f(inp["gk_a"])[l]
        gqb, gkb = f(inp["gq_b"])[l], f(inp["gk_b"])[l]
        gqc, gkc = f(inp["gq_c"])[l], f(inp["gk_c"])[l]
        g = np.zeros((128, NSLAB), np.float32)
        for s in range(4):
            g[:, s] = gqa[p % 64]
            g[:, 4 + s] = gqa[(p % 64) ^ 1]
        for s in range(2):
            g[:, 8 + s] = gka[p % 64]
            g[:, 10 + s] = gka[(p % 64) ^ 1]
            g[:, 12 + s] = gqb[p % 64]
            g[:, 14 + s] = gkb[p % 64]
            g[:, 16 + s] = gqc[p % 32]
            g[:, 18 + s] = gqc[(p % 32) ^ 1]
            g[:, 20 + s] = gkc[p % 32]
            g[:, 22 + s] = gkc[(p % 32) ^ 1]
        ptab[l, :, PT_GTAB:PT_GTAB + NSLAB] = g
        cw = f(inp["conv_w"])[l]
        ptab[l, :, PT_CW:PT_CW + 132] = cw.reshape(3, 44, 128).transpose(2, 1, 0).reshape(128, 132)
        ptab[l, :, PT_CB:PT_CB + 44] = f(inp["conv_b"])[l].reshape(44, 128).T
        ptab[l, :, PT_GSUB] = f(inp["g_subln"])[l][p % 64]
        for i, nm in enumerate(("lambda_q1", "lambda_k1", "lambda_q2", "lambda_k2")):
            ptab[l, :, PT_LAM + 32 * i:PT_LAM + 32 * i + 32] = f(inp[nm])[l][None, :]

    ropeA, ropeC = _rope_tables()
    valid, ri, ci = _bias_patterns()
    rpb = f(inp["rpb_b"])
    bias = np.where(valid[None, None], rpb[:, :, ri, ci], np.float32(NEG)).astype(np.float32)
    biasB = np.ascontiguousarray(bias.transpose(0, 1, 3, 2, 4))
    consts = np.zeros((3, 128, 128), np.float32)
    consts[0] = np.eye(128, dtype=np.float32)
    consts[1] = np.kron(np.eye(2, dtype=np.float32), np.ones((64, 64), np.float32))
    consts[2] = np.kron(np.eye(4, dtype=np.float32), np.ones((32, 32), np.float32))
    return dict(consts=consts, w_ada=f(inp["w_ada"]), ptab=ptab, w1=w1, ropeA=ropeA, ropeC=ropeC, biasB=biasB,
                w_out=f(inp["w_out"]), w_up=f(inp["w_up"]), w_down=f(inp["w_down"]))


def _prep_core(inp, b):
    x = np.asarray(inp["x"][b], np.float32)
    ctx = np.asarray(inp["ctx"][b], np.float32)
    xT = np.ascontiguousarray(np.concatenate([x.T, ctx.T], axis=1))
    cv = np.stack([np.asarray(inp["c"][b], np.float32), np.asarray(inp["c_ctx"], np.float32)], axis=-1)
    cvec = np.ascontiguousarray(cv.reshape(8, 128, 2).transpose(1, 0, 2))
    return dict(xT=xT, cvec=cvec)


_NC_CACHE = {}


def kernel(**inputs):
    if "nc" not in _NC_CACHE:
        _NC_CACHE["nc"] = build_nc()
    nc = _NC_CACHE["nc"]
    shared = _prep_shared(inputs)
    in_maps = []
    for b in range(8):
        m = dict(shared)
        m.update(_prep_core(inputs, b))
        in_maps.append(m)
    outs = []
    for g0 in range(0, 8, GROUP):
        res = run_bass_kernel_spmd(nc, in_maps[g0:g0 + GROUP], core_ids=list(range(GROUP)))
        outs.extend(np.ascontiguousarray(res.results[b]["yT"].T) for b in range(GROUP))
    return np.stack(outs, axis=0).astype(np.float32)
```

```python
import math
import os
CUT = int(os.environ.get('KCUT', '99'))
from contextlib import ExitStack

import numpy as np
import concourse.bass as bass
import concourse.mybir as mybir
from concourse.bass_utils import run_bass_kernel_spmd

F32 = mybir.dt.float32
BF16 = mybir.dt.bfloat16
ALU = mybir.AluOpType
AF = mybir.ActivationFunctionType
AX = mybir.AxisListType

D = 1024
L = 4096
LC = 256
T = L + LC
DEPTH = 2
NKC = 8
FFN = 2816
NJ = FFN // 128
EPS = 1e-6
NEG = -30000.0
NSLAB = 24
W1C = NSLAB * 128 + 640
GROUP = 4
TB = 410

PT_BADA = 0
PT_G1 = 48
PT_G2 = 56
PT_GTAB = 64
PT_CW = 88
PT_CB = 220
PT_GSUB = 264
PT_LAM = 265
NPT = 393


class Eng:
    def __init__(self, nc, es, eng, name, skip_own=False):
        self.e = eng
        self.sem = es.enter_context(nc.semaphore(name))
        self.cnt = 0
        self.seen = {}
        self.skip_own = skip_own

    def wait(self, toks):
        best = {}
        for t in toks:
            if t is None:
                continue
            s, v = t
            if self.skip_own and s is self.sem:
                continue
            k = id(s)
            if self.seen.get(k, 0) >= v:
                continue
            if k not in best or best[k][1] < v:
                best[k] = (s, v)
        for k, (s, v) in best.items():
            self.e.wait_ge(s, v)
            self.seen[k] = v

    def sig(self, ins):
        self.cnt += 1
        ins.then_inc(self.sem, 1)
        return (self.sem, self.cnt)


class Buf:
    def __init__(self, t):
        self.t = t
        self.wr = None
        self.rd = {}
        self.dsem = None
        self.dcnt = 0
        self.excl = False

    def add_rd(self, tok):
        k = id(tok[0])
        if k not in self.rd or self.rd[k][1] < tok[1]:
            self.rd[k] = tok


class K:
    pass


def op(E, fn, rd=(), wr=()):
    toks = [b.wr for b in rd]
    for b in rd:
        if b.excl:
            toks.extend(b.rd.values())
    for b in wr:
        toks.append(b.wr)
        toks.extend(b.rd.values())
    E.wait(toks)
    tok = E.sig(fn())
    for b in rd:
        b.add_rd(tok)
    for b in wr:
        b.wr = tok
        b.rd = {}
    return tok


def _dsem(k, b):
    if b.dsem is None:
        k.nsem += 1
        b.dsem = k.es.enter_context(k.nc.semaphore("d%d" % k.nsem))
    return b.dsem


def dma_load(k, Q, b, dst_ap, src_ap, part=False):
    if not part:
        toks = [b.wr] + list(b.rd.values())
        Q.wait(toks)
    s = _dsem(k, b)
    ins = Q.e.dma_start(out=dst_ap, in_=src_ap)
    b.dcnt += 16
    ins.then_inc(s, 16)
    b.wr = (s, b.dcnt)
    b.rd = {}


def dma_store(k, Q, dst_ap, b, src_ap):
    Q.wait([b.wr])
    s = _dsem(k, b)
    ins = Q.e.dma_start(out=dst_ap, in_=src_ap)
    b.dcnt += 16
    ins.then_inc(s, 16)
    tok = (s, b.dcnt)
    b.add_rd(tok)
    k.stores[id(s)] = tok


def barrier(k):
    toks = [(E.sem, E.cnt) for E in k.engs if E.cnt > 0] + list(k.stores.values())
    for E in k.engs + [k.sp]:
        E.wait(toks)
    k.stores = {}


def build_nc(dbg=False, nlayers=DEPTH, stop_after=None):
    nc = bass.Bass("TRN2", target_bir_lowering=False)
    k = K()
    k.nc = nc
    k.nsem = 0
    k.stores = {}

    def din(name, shape, dt=F32):
        return nc.dram_tensor(name, list(shape), dt, kind="ExternalInput").ap()

    xT = din("xT", [D, T])
    cvec = din("cvec", [128, NKC, 2])
    w_ada = din("w_ada", [DEPTH, D, 6 * D])
    ptab = din("ptab", [DEPTH, 128, NPT])
    w1 = din("w1", [DEPTH, D, W1C])
    ropeA = din("ropeA", [2, 128, T])
    ropeC = din("ropeC", [2, 128, T])
    biasB = din("biasB", [DEPTH, 4, 128, 25, 128])
    w_out = din("w_out", [DEPTH, D, D])
    w_up = din("w_up", [DEPTH, D, 2 * FFN])
    w_down = din("w_down", [DEPTH, FFN, D])
    consts = din("consts", [3, 128, 128])
    yT = nc.dram_tensor("yT", [D, L], F32, kind="ExternalOutput").ap()

    skind = "ExternalOutput" if dbg else "Internal"

    def dscr(name, shape, dt):
        return nc.dram_tensor(name, list(shape), dt, kind=skind).ap()

    xs_a = dscr("xs_a", [D, T], F32)
    xs_b = dscr("xs_b", [D, T], F32)
    qs = dscr("qs", [8, 128, T], BF16)
    ks = dscr("ks", [16, 128, T], BF16)
    vs = dscr("vs", [10, T, 128], BF16)
    attn = dscr("attn", [D, T], BF16)
    modd = dscr("modd", [DEPTH, 128, 96], F32) if dbg else None

    with ExitStack() as es:
        k.es = es
        pe = Eng(nc, es, nc.tensor, "s_pe", skip_own=True)
        act = Eng(nc, es, nc.scalar, "s_act")
        dve = Eng(nc, es, nc.vector, "s_dve")
        pool = Eng(nc, es, nc.gpsimd, "s_pool")
        sp = Eng(nc, es, nc.sync, "s_sp")
        k.engs = [pe, act, dve, pool]
        k.sp = sp

        uniq = [0]

        def sb(st, name, shape, dt):
            uniq[0] += 1
            return Buf(st.enter_context(nc.sbuf_tensor("%s_u%d" % (name, uniq[0]), list(shape), dt)))

        def ps(st, name, shape=(128, 512), dt=F32):
            uniq[0] += 1
            b = Buf(st.enter_context(nc.psum_tensor("%s_u%d" % (name, uniq[0]), list(shape), dt)))
            b.excl = True
            return b

        ones_bf = sb(es, "ones_bf", [128, 128], BF16)
        blk64 = sb(es, "blk64", [128, 128], BF16)
        blk32 = sb(es, "blk32", [128, 128], BF16)
        ident = sb(es, "ident", [128, 128], F32)
        ptb = [sb(es, "ptb%d" % l, [128, NPT], F32) for l in range(DEPTH)]
        mods = [sb(es, "mods%d" % l, [128, 48, 2], F32) for l in range(DEPTH)]
        gm = [sb(es, "gm%d" % l, [128, 2, NKC, 2], F32) for l in range(DEPTH)]
        lamt = [sb(es, "lamt%d" % l, [128, 4], F32) for l in range(DEPTH)]

        op(pool, lambda: nc.gpsimd.memset(ones_bf.t[:], 1.0), wr=[ones_bf])
        zbf = sb(es, "zbf", [128, 128], BF16)
        op(pool, lambda: nc.gpsimd.memset(zbf.t[:], 0.0), wr=[zbf])
        dma_load(k, sp, ident, ident.t[:], consts[0])
        dma_load(k, pool, blk64, blk64.t[:], consts[1])
        dma_load(k, pool, blk32, blk32.t[:], consts[2])
        for l in range(DEPTH):
            dma_load(k, sp, ptb[l], ptb[l].t[:], ptab[l])

        with ExitStack() as p0:
            csb = sb(p0, "csb", [128, NKC, 2], F32)
            scs = sb(p0, "scs", [128, NKC, 2], F32)
            wa = [sb(p0, "wa%d" % i, [128, NKC, 768], F32) for i in range(2)]
            modp = ps(p0, "modp", [128, 512])
            tmpl = sb(p0, "tmpl", [128, 32], F32)
            zer_bf = sb(p0, "zer_bf", [128, T], BF16)
            op(pool, lambda: nc.gpsimd.memset(zer_bf.t[:], 0.0), wr=[zer_bf])
            for v in range(16):
                dma_store(k, sp, ks[v], zer_bf, zer_bf.t[:])
            dma_load(k, sp, csb, csb.t[:], cvec)
            op(act, lambda: nc.scalar.activation(out=scs.t[:], in_=csb.t[:], func=AF.Silu), rd=[csb], wr=[scs])
            for l in range(nlayers):
                for pc in range(8):
                    wb = wa[pc % 2]
                    dma_load(k, sp, wb, wb.t[:],
                             w_ada[l, :, pc * 768:(pc + 1) * 768].rearrange("(k p) c -> p k c", p=128))

                    def mm(wb=wb, pc=pc):
                        ins = None
                        for jl in range(6):
                            j = pc * 6 + jl
                            for kc in range(NKC):
                                ins = nc.tensor.matmul(modp.t[:, 2 * j:2 * j + 2], lhsT=wb.t[:, kc, jl * 128:(jl + 1) * 128],
                                                       rhs=scs.t[:, kc, :], start=(kc == 0), stop=(kc == NKC - 1))
                        return ins
                    op(pe, mm, rd=[wb, scs], wr=[modp])
                mv = modp.t[:, 0:96].rearrange("p (j t) -> p j t", t=2)
                for t in range(2):
                    op(dve, lambda t=t: nc.vector.tensor_tensor(out=mods[l].t[:, :, t], in0=mv[:, :, t],
                                                                in1=ptb[l].t[:, PT_BADA:PT_BADA + 48], op=ALU.add),
                       rd=[modp, ptb[l]], wr=[mods[l]])
                for n, (sc0, g0) in enumerate(((8, PT_G1), (32, PT_G2))):
                    for t in range(2):
                        op(dve, lambda n=n, sc0=sc0, g0=g0, t=t: nc.vector.scalar_tensor_tensor(
                            out=gm[l].t[:, n, :, t], in0=mods[l].t[:, sc0:sc0 + 8, t], scalar=1.0,
                            in1=ptb[l].t[:, g0:g0 + 8], op0=ALU.add, op1=ALU.mult),
                           rd=[mods[l], ptb[l]], wr=[gm[l]])
                lam_init = 0.8 - 0.6 * math.exp(-0.3 * l)
                for i in range(2):
                    a0 = PT_LAM + 64 * i
                    op(dve, lambda a0=a0: nc.vector.tensor_tensor(out=tmpl.t[:], in0=ptb[l].t[:, a0:a0 + 32],
                                                                  in1=ptb[l].t[:, a0 + 32:a0 + 64], op=ALU.mult),
                       rd=[ptb[l]], wr=[tmpl])
                    op(dve, lambda i=i: nc.vector.reduce_sum(out=lamt[l].t[:, 2 + i:3 + i], in_=tmpl.t[:], axis=AX.X),
                       rd=[tmpl], wr=[lamt[l]])
                op(act, lambda: nc.scalar.activation(out=lamt[l].t[:, 2:4], in_=lamt[l].t[:, 2:4], func=AF.Exp),
                   rd=[lamt[l]], wr=[lamt[l]])
                op(dve, lambda: nc.vector.scalar_tensor_tensor(out=lamt[l].t[:, 0:1], in0=lamt[l].t[:, 3:4], scalar=-lam_init,
                                                               in1=lamt[l].t[:, 2:3], op0=ALU.add, op1=ALU.subtract),
                   rd=[lamt[l]], wr=[lamt[l]])
                op(dve, lambda: nc.vector.tensor_scalar(out=lamt[l].t[:, 1:2], in0=ptb[l].t[:, PT_GSUB:PT_GSUB + 1],
                                                        scalar1=1.0 - lam_init, scalar2=None, op0=ALU.mult),
                   rd=[ptb[l]], wr=[lamt[l]])
                if dbg:
                    dma_store(k, sp, modd[l].rearrange("p (j t) -> p j t", t=2), mods[l], mods[l].t[:])
            barrier(k)

        blocks = [(512 * i, 512) for i in range(8)] + [(L, LC)]

        for l in range(nlayers if stop_after != 'p0' else 0):
            last = (l == DEPTH - 1)
            with_ctx = not last
            src = xT if l == 0 else xs_b

            with ExitStack() as p1:
                W1 = sb(p1, "W1", [128, NKC, W1C], BF16)
                for c0 in range(0, W1C, 1856):
                    dma_load(k, pool, W1, W1.t[:, :, c0:c0 + 1856],
                             w1[l, :, c0:c0 + 1856].rearrange("(k p) c -> p k c", p=128), part=(c0 > 0))
                xin = [sb(p1, "xin%d" % i, [128, NKC, 512], F32) for i in range(2)]
                sq = sb(p1, "sq", [128, NKC, 512], BF16)
                rs = sb(p1, "rs", [128, 512], F32)
                tmpf = [sb(p1, "tmpf%d" % i, [128, 512], F32) for i in range(2)]
                hT = [sb(p1, "hT%d" % i, [128, NKC, 512], BF16) for i in range(2)]
                rop = [[sb(p1, "rop%d_%d" % (i, j), [128, 512], F32) for j in range(4)] for i in range(2)]
                sqq = [sb(p1, "sqq%d" % i, [128, 512], BF16) for i in range(2)]
                rq = [sb(p1, "rq%d" % i, [128, 512], F32) for i in range(2)]
                t1 = [sb(p1, "t1_%d" % i, [128, 512], F32) for i in range(2)]
                t2 = [sb(p1, "t2_%d" % i, [128, 512], F32) for i in range(2)]
                outb = [sb(p1, "outb%d" % i, [128, 512], BF16) for i in range(3)]
                vout = [sb(p1, "vout%d" % i, [128, 10, 128], BF16) for i in range(2)]
                pss = ps(p1, "pss")
                pv0 = ps(p1, "pv0")
                pm = [ps(p1, "pm%d" % i) for i in range(2)]
                pw = [ps(p1, "pw%d" % i) for i in range(2)]
                pq = [ps(p1, "pq%d" % i) for i in range(2)]
                for i in range(2):
                    op(pool, lambda i=i: nc.gpsimd.memset(vout[i].t[:], 1.0), wr=[vout[i]])

                jobs1 = []
                for s in range(4):
                    jobs1.append((s, 4 + s, 'A', 64, True, [(qs[s], 0, 128)]))
                for g in range(2):
                    jobs1.append((8 + g, 10 + g, 'A', 64, False, [(ks[2 * g], 0, 64), (ks[2 * g + 1], 64, 128)]))
                for s in range(2):
                    jobs1.append((12 + s, None, None, 64, True, [(qs[4 + s], 0, 128)]))
                for s in range(2):
                    jobs1.append((14 + s, None, None, 64, False, [(ks[4 + 2 * s], 0, 64), (ks[5 + 2 * s], 64, 128)]))
                for s in range(2):
                    jobs1.append((16 + s, 18 + s, 'C', 32, True, [(qs[6 + s], 0, 128)]))
                for s in range(2):
                    jobs1.append((20 + s, 22 + s, 'C', 32, False,
                                  [(ks[8 + 4 * s + j], 32 * j, 32 * j + 32) for j in range(4)]))

                cnt1 = 0
                cnto = 0
                for bi, (t0, n) in enumerate(blocks if CUT > 1 else []):
                    if CUT < 10 and bi > 0:
                        break
                    tcol = 0 if t0 < L else 1
                    xb = xin[bi % 2]
                    hb = hT[bi % 2]
                    rp = rop[bi % 2]
                    dma_load(k, sp, xb, xb.t[:, :, 0:n], src[:, t0:t0 + n].rearrange("(c p) t -> p c t", p=128))
                    for j, (tab, idx) in enumerate(((ropeA, 0), (ropeA, 1), (ropeC, 0), (ropeC, 1))):
                        dma_load(k, sp, rp[j], rp[j].t[:, 0:n], tab[idx, :, t0:t0 + n])
                    op(act, lambda: nc.scalar.activation(out=sq.t[:, :, 0:n], in_=xb.t[:, :, 0:n], func=AF.Square),
                       rd=[xb], wr=[sq])

                    def mm_ss():
                        ins = None
                        for c in range(NKC):
                            ins = nc.tensor.matmul(pss.t[:, 0:n], lhsT=ones_bf.t[:], rhs=sq.t[:, c, 0:n],
                                                   start=(c == 0), stop=(c == NKC - 1))
                        return ins
                    op(pe, mm_ss, rd=[ones_bf, sq], wr=[pss])
                    op(act, lambda: nc.scalar.activation(out=rs.t[:, 0:n], in_=pss.t[:, 0:n], func=AF.Ln,
                                                         scale=1.0 / D, bias=EPS), rd=[pss], wr=[rs])
                    op(act, lambda: nc.scalar.activation(out=rs.t[:, 0:n], in_=rs.t[:, 0:n], func=AF.Exp, scale=-0.5),
                       rd=[rs], wr=[rs])
                    for c in range(NKC):
                        tf = tmpf[c % 2]
                        op(dve, lambda c=c, tf=tf: nc.vector.scalar_tensor_tensor(
                            out=tf.t[:, 0:n], in0=xb.t[:, c, 0:n], scalar=gm[l].t[:, 0, c, tcol:tcol + 1],
                            in1=rs.t[:, 0:n], op0=ALU.mult, op1=ALU.mult), rd=[xb, gm[l], rs], wr=[tf])
                        op(act, lambda c=c, tf=tf: nc.scalar.activation(
                            out=hb.t[:, c, 0:n], in_=tf.t[:, 0:n], func=AF.Identity,
                            bias=mods[l].t[:, c, tcol:tcol + 1], scale=1.0), rd=[tf, mods[l]], wr=[hb])

                    for tt in range(n // 128 if CUT > 2 else 0):
                        vo = vout[cnt1 % 2]
                        cnt1 += 1

                        def mm_v(tt=tt):
                            ins = None
                            for kc in range(NKC):
                                ins = nc.tensor.matmul(pv0.t[:, 0:512], lhsT=hb.t[:, kc, tt * 128:(tt + 1) * 128],
                                                       rhs=W1.t[:, kc, 3072:3584], start=(kc == 0), stop=(kc == NKC - 1))
                            for kc in range(NKC):
                                ins = nc.tensor.matmul(pss.t[:, 0:128], lhsT=hb.t[:, kc, tt * 128:(tt + 1) * 128],
                                                       rhs=W1.t[:, kc, 3584:3712], start=(kc == 0), stop=(kc == NKC - 1))
                            return ins
                        op(pe, mm_v, rd=[hb, W1], wr=[pv0, pss])
                        op(dve, lambda vo=vo: nc.vector.tensor_copy(
                            out=vo.t[:, 0:8, 0:64], in_=pv0.t[:, 0:512].rearrange("p (v c) -> p v c", c=64)),
                           rd=[pv0], wr=[vo])
                        op(dve, lambda vo=vo: nc.vector.tensor_copy(
                            out=vo.t[:, 8:10, 0:64], in_=pss.t[:, 0:128].rearrange("p (v c) -> p v c", c=64)),
                           rd=[pss], wr=[vo])
                        tk = t0 + tt * 128
                        dma_store(k, sp, vs[:, tk:tk + 128, :].rearrange("v p c -> p v c"), vo, vo.t[:])

                    for (sm, sw, rope, dd, is_q, dests) in (jobs1 if CUT > 3 else []):
                        i2 = cnt1 % 2
                        cnt1 += 1
                        pmb, pwb, pqb = pm[i2], pw[i2], pq[i2]
                        sqb, rqb, t1b, t2b = sqq[i2], rq[i2], t1[i2], t2[i2]

                        def mm_slab(slab, dst):
                            def f():
                                ins = None
                                for kc in range(NKC):
                                    ins = nc.tensor.matmul(dst.t[:, 0:n], lhsT=W1.t[:, kc, slab * 128:(slab + 1) * 128],
                                                           rhs=hb.t[:, kc, 0:n], start=(kc == 0), stop=(kc == NKC - 1))
                                return ins
                            return f
                        op(pe, mm_slab(sm, pmb), rd=[W1, hb], wr=[pmb])
                        if sw is not None:
                            op(pe, mm_slab(sw, pwb), rd=[W1, hb], wr=[pwb])
                        op(act, lambda: nc.scalar.activation(out=sqb.t[:, 0:n], in_=pmb.t[:, 0:n], func=AF.Square),
                           rd=[pmb], wr=[sqb])
                        blk = blk64 if dd == 64 else blk32
                        op(pe, lambda: nc.tensor.matmul(pqb.t[:, 0:n], lhsT=blk.t[:], rhs=sqb.t[:, 0:n], start=True, stop=True),
                           rd=[blk, sqb], wr=[pqb])
                        if is_q:
                            a, b = 1.0, dd * EPS
                        else:
                            a, b = 1.0 / dd, EPS
                        op(act, lambda: nc.scalar.activation(out=rqb.t[:, 0:n], in_=pqb.t[:, 0:n], func=AF.Ln, scale=a, bias=b),
                           rd=[pqb], wr=[rqb])
                        op(act, lambda: nc.scalar.activation(out=rqb.t[:, 0:n], in_=rqb.t[:, 0:n], func=AF.Exp, scale=-0.5),
                           rd=[rqb], wr=[rqb])
                        ob = outb[cnto % 3]
                        cnto += 1
                        gcol = PT_GTAB + sm
                        if CUT < 5:
                            continue
                        if os.environ.get('KSKIPB') and rope is None:
                            continue
                        if os.environ.get('KSKIPR') and rope is not None:
                            continue
                        if rope is not None:
                            cosb, sinb = (rp[0], rp[1]) if rope == 'A' else (rp[2], rp[3])
                            gcs = PT_GTAB + sw
                            op(dve, lambda: nc.vector.scalar_tensor_tensor(
                                out=t1b.t[:, 0:n], in0=cosb.t[:, 0:n], scalar=ptb[l].t[:, gcol:gcol + 1],
                                in1=pmb.t[:, 0:n], op0=ALU.mult, op1=ALU.mult), rd=[pmb, ptb[l], cosb], wr=[t1b])
                            op(dve, lambda: nc.vector.scalar_tensor_tensor(
                                out=t2b.t[:, 0:n], in0=sinb.t[:, 0:n], scalar=ptb[l].t[:, gcs:gcs + 1],
                                in1=pwb.t[:, 0:n], op0=ALU.mult, op1=ALU.mult), rd=[pwb, ptb[l], sinb], wr=[t2b])
                            if CUT < 6:
                                continue
                            op(pool, lambda: nc.gpsimd.tensor_tensor(out=t1b.t[:, 0:n], in0=t1b.t[:, 0:n], in1=t2b.t[:, 0:n],
                                                                     op=ALU.add), rd=[t1b, t2b], wr=[t1b])
                            op(pool, lambda: nc.gpsimd.tensor_tensor(out=ob.t[:, 0:n], in0=t1b.t[:, 0:n], in1=rqb.t[:, 0:n],
                                                                     op=ALU.mult), rd=[t1b, rqb], wr=[ob])
                        else:
                            op(dve, lambda: nc.vector.scalar_tensor_tensor(
                                out=ob.t[:, 0:n], in0=rqb.t[:, 0:n], scalar=ptb[l].t[:, gcol:gcol + 1],
                                in1=pmb.t[:, 0:n], op0=ALU.mult, op1=ALU.mult), rd=[pmb, ptb[l], rqb], wr=[ob])
                        for (dap, r0, r1) in (dests if CUT > 6 else []):
                            dma_store(k, sp, dap[r0:r1, t0:t0 + n], ob, ob.t[r0:r1, 0:n])
                barrier(k)
            if stop_after == (l, 1):
                break

            with ExitStack() as p2:
                NS, NP, LA = 3, 4, 2
                Kb = [[sb(p2, "Kb%d_%d" % (i, j), [128, T], BF16) for j in range(2)] for i in range(2)]
                Vb = [sb(p2, "Vb%d" % i, [128, 34, 128], BF16) for i in range(2)]
                Qb = [sb(p2, "Qb%d" % i, [128, T], BF16) for i in range(2)]
                Bb = [sb(p2, "Bb%d" % i, [128, 25, 128], F32) for i in range(2)]
                Pb = [sb(p2, "Pb%d" % i, [128, 512], BF16) for i in range(NP)]
                rb = [sb(p2, "rb%d" % i, [128, 512], F32) for i in range(2)]
                ab = [sb(p2, "ab%d" % i, [128, 512], F32) for i in range(2)]
                sqc = sb(p2, "sqc", [128, 512], BF16)
                rsc = sb(p2, "rsc", [128, 512], F32)
                osb = [sb(p2, "osb%d" % i, [128, 512], BF16) for i in range(3)]
                Sp = [ps(p2, "Sp%d" % i) for i in range(NS)]
                Op = [ps(p2, "Op%d" % i) for i in range(4)]
                Mp = ps(p2, "Mp")

                jobs = []
                for h in range(8):
                    jobs.append(dict(kind='A', kv=[2 * (h // 4) + (h % 2)], vh=h // 4, q=h // 2, row=64 * h))
                for h in range(4):
                    jobs.append(dict(kind='B', kv=[4 + h], vh=2 + h, q=4 + h // 2, row=512 + 64 * h, bh=h))
                for h in range(4):
                    jobs.append(dict(kind='C', kv=[8 + 2 * h, 9 + 2 * h], vh=6 + h, q=6 + h // 2, row=768 + 64 * h))

                if os.environ.get('KJOBS'):
                    jobs = [jb for jb in jobs if jb['kind'] in os.environ['KJOBS']]

                def load_job(ji):
                    jb = jobs[ji]
                    st = ji % 2
                    for m, kv in enumerate(jb['kv']):
                        dma_load(k, sp, Kb[st][m], Kb[st][m].t[:], ks[kv])
                    dma_load(k, sp, Qb[st], Qb[st].t[:], qs[jb['q']])
                    dma_load(k, sp, Vb[st], Vb[st].t[:], vs[jb['vh']].rearrange("(kt p) c -> p kt c", p=128))
                    if jb['kind'] == 'B':
                        dma_load(k, sp, Bb[st], Bb[st].t[:], biasB[l, jb['bh']])

                steps = []
                bc = 0
                ocnt = [0]
                for ji, jb in enumerate(jobs):
                    st = ji % 2
                    qblocks = [(512 * i, 512, False) for i in range(8)]
                    if with_ctx:
                        qblocks.append((L, LC, True))
                    nmap = len(jb['kv'])
                    first_of_job = True
                    for (t0, n, isctx) in qblocks:
                        Os = [Op[(nmap * bc + m) % 4] for m in range(nmap)]
                        bc += 1
                        blk_steps = []
                        Qt, Vt = Qb[st], Vb[st]
                        if jb['kind'] == 'B' and not isctx:
                            qb = t0 // 512
                            for s in range(5):
                                def S_fn(Sb, s=s, qb=qb, Kt=Kb[st][0], Qt=Qt, Bt=Bb[st]):
                                    ins = None
                                    for sbk in range(4):
                                        i = 4 * qb + sbk
                                        kt = min(max(i - 2, 0), 27) + s
                                        pat = s if 2 <= i <= 29 else {0: 5, 1: 10, 30: 15, 31: 20}[i] + s
                                        nc.tensor.matmul(Sb.t[:, sbk * 128:(sbk + 1) * 128], lhsT=Kt.t[:, kt * 128:(kt + 1) * 128],
                                                         rhs=Qt.t[:, i * 128:(i + 1) * 128], start=True, stop=False)
                                        ins = nc.tensor.matmul(Sb.t[:, sbk * 128:(sbk + 1) * 128], lhsT=ident.t[:],
                                                               rhs=Bt.t[:, pat, :], start=False, stop=True)
                                    return ins

                                def PV_fn(Pt, O, s=s, qb=qb, Vt=Vt, Qt=Qt, t0=t0):
                                    ins = None
                                    if s == 0:
                                        nc.tensor.matmul(O.t[:, 0:512], lhsT=zbf.t[:], rhs=Qt.t[:, t0:t0 + 512],
                                                         start=True, stop=False)
                                    for sbk in range(4):
                                        i = 4 * qb + sbk
                                        kt = min(max(i - 2, 0), 27) + s
                                        ins = nc.tensor.matmul(O.t[:, sbk * 128:(sbk + 1) * 128], lhsT=Vt.t[:, kt, :],
                                                               rhs=Pt.t[:, sbk * 128:(sbk + 1) * 128], start=False, stop=False)
                                    return ins
                                blk_steps.append(dict(S=S_fn, PV=PV_fn, O=Os[0], n=512, rdS=[Kb[st][0], Qt, Bb[st], ident],
                                                      rdV=[Vt, zbf, Qt]))
                            for j in range(2):
                                kt = 32 + j

                                def S_fn(Sb, kt=kt, Kt=Kb[st][0], Qt=Qt, t0=t0):
                                    return nc.tensor.matmul(Sb.t[:, 0:512], lhsT=Kt.t[:, kt * 128:(kt + 1) * 128],
                                                            rhs=Qt.t[:, t0:t0 + 512], start=True, stop=True)

                                def PV_fn(Pt, O, kt=kt, Vt=Vt, j=j):
                                    return nc.tensor.matmul(O.t[:, 0:512], lhsT=Vt.t[:, kt, :], rhs=Pt.t[:, 0:512],
                                                            start=False, stop=(j == 1))
                                blk_steps.append(dict(S=S_fn, PV=PV_fn, O=Os[0], n=512, rdS=[Kb[st][0], Qt], rdV=[Vt]))
                        else:
                            kts = [32, 33] if isctx else list(range(34))
                            for ki, kt in enumerate(kts):
                                for m in range(nmap):
                                    def S_fn(Sb, kt=kt, Kt=Kb[st][m], Qt=Qt, t0=t0, n=n):
                                        return nc.tensor.matmul(Sb.t[:, 0:n], lhsT=Kt.t[:, kt * 128:(kt + 1) * 128],
                                                                rhs=Qt.t[:, t0:t0 + n], start=True, stop=True)

                                    def PV_fn(Pt, O, kt=kt, Vt=Vt, n=n, ki=ki, nk=len(kts)):
                                        return nc.tensor.matmul(O.t[:, 0:n], lhsT=Vt.t[:, kt, :], rhs=Pt.t[:, 0:n],
                                                                start=(ki == 0), stop=(ki == nk - 1))
                                    blk_steps.append(dict(S=S_fn, PV=PV_fn, O=Os[m], n=n, rdS=[Kb[st][m], Qt], rdV=[Vt]))
                        if first_of_job:
                            blk_steps[0]['pre'] = ji
                            first_of_job = False
                        blk_steps[-1]['post'] = (jb, t0, n, Os)
                        steps.extend(blk_steps)

                def post_block(jb, t0, n, Os):
                    ob = osb[ocnt[0] % 3]
                    ocnt[0] += 1
                    if jb['kind'] != 'C':
                        O = Os[0]
                        r = rb[0]
                        op(dve, lambda: nc.vector.reciprocal(out=r.t[0:64, 0:n], in_=O.t[64:128, 0:n]), rd=[O], wr=[r])
                        op(dve, lambda: nc.vector.tensor_tensor(out=ob.t[0:64, 0:n], in0=O.t[0:64, 0:n], in1=r.t[0:64, 0:n],
                                                                op=ALU.mult), rd=[O, r], wr=[ob])
                    else:
                        O1, O2 = Os
                        op(dve, lambda: nc.vector.reciprocal(out=rb[0].t[0:64, 0:n], in_=O1.t[64:128, 0:n]), rd=[O1], wr=[rb[0]])
                        op(dve, lambda: nc.vector.reciprocal(out=rb[1].t[0:64, 0:n], in_=O2.t[64:128, 0:n]), rd=[O2], wr=[rb[1]])
                        op(dve, lambda: nc.vector.tensor_tensor(out=ab[0].t[0:64, 0:n], in0=O1.t[0:64, 0:n],
                                                                in1=rb[0].t[0:64, 0:n], op=ALU.mult), rd=[O1, rb[0]], wr=[ab[0]])
                        op(dve, lambda: nc.vector.scalar_tensor_tensor(
                            out=ab[1].t[0:64, 0:n], in0=rb[1].t[0:64, 0:n], scalar=lamt[l].t[0:64, 0:1],
                            in1=O2.t[0:64, 0:n], op0=ALU.mult, op1=ALU.mult), rd=[O2, lamt[l], rb[1]], wr=[ab[1]])
                        op(pool, lambda: nc.gpsimd.tensor_tensor(out=ab[0].t[0:64, 0:n], in0=ab[0].t[0:64, 0:n],
                                                                 in1=ab[1].t[0:64, 0:n], op=ALU.add), rd=[ab[0], ab[1]], wr=[ab[0]])
                        op(act, lambda: nc.scalar.activation(out=sqc.t[0:64, 0:n], in_=ab[0].t[0:64, 0:n], func=AF.Square),
                           rd=[ab[0]], wr=[sqc])
                        op(pe, lambda: nc.tensor.matmul(Mp.t[0:64, 0:n], lhsT=ones_bf.t[0:64, 0:64], rhs=sqc.t[0:64, 0:n],
                                                        start=True, stop=True), rd=[ones_bf, sqc], wr=[Mp])
                        op(act, lambda: nc.scalar.activation(out=rsc.t[0:64, 0:n], in_=Mp.t[0:64, 0:n], func=AF.Ln,
                                                             scale=1.0 / 64, bias=EPS), rd=[Mp], wr=[rsc])
                        op(act, lambda: nc.scalar.activation(out=rsc.t[0:64, 0:n], in_=rsc.t[0:64, 0:n], func=AF.Exp, scale=-0.5),
                           rd=[rsc], wr=[rsc])
                        op(dve, lambda: nc.vector.scalar_tensor_tensor(
                            out=ob.t[0:64, 0:n], in0=ab[0].t[0:64, 0:n], scalar=lamt[l].t[0:64, 1:2],
                            in1=rsc.t[0:64, 0:n], op0=ALU.mult, op1=ALU.mult), rd=[ab[0], lamt[l], rsc], wr=[ob])
                    dma_store(k, sp, attn[jb['row']:jb['row'] + 64, t0:t0 + n], ob, ob.t[0:64, 0:n])

                load_job(0)
                ns = len(steps)
                for i in range(ns + LA):
                    if i < ns:
                        stp = steps[i]
                        Sb, Pt = Sp[i % NS], Pb[i % NP]
                        op(pe, lambda: stp['S'](Sb), rd=stp['rdS'], wr=[Sb])
                        nn = stp['n']
                        op(act, lambda: nc.scalar.activation(out=Pt.t[:, 0:nn], in_=Sb.t[:, 0:nn], func=AF.Exp),
                           rd=[Sb], wr=[Pt])
                    if i >= LA:
                        j = i - LA
                        stj = steps[j]
                        Pj = Pb[j % NP]
                        if 'pre' in stj and stj['pre'] + 1 < len(jobs):
                            load_job(stj['pre'] + 1)
                        op(pe, lambda: stj['PV'](Pj, stj['O']), rd=[Pj] + stj['rdV'], wr=[stj['O']])
                        if 'post' in stj:
                            post_block(*stj['post'])
                barrier(k)
            if stop_after == (l, 2):
                break

            blocks3 = blocks if with_ctx else blocks[:8]
            with ExitStack() as p3:
                Wo = sb(p3, "Wo", [128, NKC, D], BF16)
                dma_load(k, pool, Wo, Wo.t[:], w_out[l].rearrange("(k p) c -> p k c", p=128))
                at = [sb(p3, "at%d" % i, [128, NKC, 512], BF16) for i in range(2)]
                xa = [sb(p3, "xa%d" % i, [128, NKC, 512], F32) for i in range(2)]
                po = [ps(p3, "po%d" % i) for i in range(4)]
                tmp3 = [sb(p3, "tmp3_%d" % i, [128, 512], F32) for i in range(2)]
                for bi, (t0, n) in enumerate(blocks3):
                    tcol = 0 if t0 < L else 1
                    a, x = at[bi % 2], xa[bi % 2]
                    dma_load(k, sp, a, a.t[:, :, 0:n], attn[:, t0:t0 + n].rearrange("(c p) t -> p c t", p=128))
                    dma_load(k, sp, x, x.t[:, :, 0:n], src[:, t0:t0 + n].rearrange("(c p) t -> p c t", p=128))
                    for oc in range(NKC):
                        pb = po[oc % 4]

                        def mm_o(oc=oc, pb=pb):
                            ins = None
                            for kc in range(NKC):
                                ins = nc.tensor.matmul(pb.t[:, 0:n], lhsT=Wo.t[:, kc, oc * 128:(oc + 1) * 128],
                                                       rhs=a.t[:, kc, 0:n], start=(kc == 0), stop=(kc == NKC - 1))
                            return ins
                        op(pe, mm_o, rd=[Wo, a], wr=[pb])
                        tb_ = tmp3[oc % 2]
                        op(act, lambda oc=oc, pb=pb, tb_=tb_: nc.scalar.activation(
                            out=tb_.t[:, 0:n], in_=pb.t[:, 0:n], func=AF.Copy,
                            scale=mods[l].t[:, 16 + oc, tcol:tcol + 1]), rd=[pb, mods[l]], wr=[tb_])
                        op(dve, lambda oc=oc, tb_=tb_: nc.vector.tensor_tensor(
                            out=x.t[:, oc, 0:n], in0=x.t[:, oc, 0:n], in1=tb_.t[:, 0:n], op=ALU.add), rd=[tb_, x], wr=[x])
                    dma_store(k, sp, xs_a[:, t0:t0 + n].rearrange("(c p) t -> p c t", p=128), x, x.t[:, :, 0:n])
                barrier(k)
            if stop_after == (l, 3):
                break

            with ExitStack() as p4:
                Wu = sb(p4, "Wu", [128, NKC, 2 * FFN], BF16)
                Wd = sb(p4, "Wd", [128, NJ, D], BF16)
                for c0 in range(0, 2 * FFN, 1408):
                    dma_load(k, pool, Wu, Wu.t[:, :, c0:c0 + 1408],
                             w_up[l, :, c0:c0 + 1408].rearrange("(k p) c -> p k c", p=128), part=(c0 > 0))
                for j0 in range(0, NJ, 11):
                    dma_load(k, pool, Wd, Wd.t[:, j0:j0 + 11, :],
                             w_down[l, j0 * 128:(j0 + 11) * 128, :].rearrange("(j p) c -> p j c", p=128), part=(j0 > 0))
                NW = TB + 2
                xw = sb(p4, "xw", [128, NKC, NW], F32)
                h2 = sb(p4, "h2", [128, NKC, NW], BF16)
                actb = sb(p4, "actb", [128, NJ, NW], BF16)
                rs2 = sb(p4, "rs2", [128, NW], F32)
                tf2 = [sb(p4, "tf2_%d" % i, [128, NW], F32) for i in range(2)]
                tu = [sb(p4, "tu%d" % i, [128, NW], F32) for i in range(2)]
                tg = [sb(p4, "tg%d" % i, [128, NW], F32) for i in range(2)]
                tm = [sb(p4, "tm%d" % i, [128, NW], F32) for i in range(3)]
                tmc = [0]
                xc = [sb(p4, "xc%d" % i, [128, NW], F32) for i in range(2)]
                pu = [ps(p4, "pu%d" % i) for i in range(2)]
                pg = [ps(p4, "pg%d" % i) for i in range(2)]
                pd = [ps(p4, "pd%d" % i) for i in range(3)]
                pr = ps(p4, "pr")
                fblocks = []
                for s0 in range(0, L, TB):
                    fblocks.append((0, L, s0, min(TB, L - s0)))
                if with_ctx:
                    fblocks.append((L, LC, 0, LC))
                dst_all = yT if last else xs_b
                for bi, (base, slen, s0, tn) in enumerate(fblocks):
                    tcol = 0 if base < L else 1
                    nw = tn + 2
                    lo = s0 - 1
                    hi = s0 + tn + 1
                    c_lo = max(lo, 0)
                    c_hi = min(hi, slen)
                    if c_lo > lo:
                        op(pool, lambda: nc.gpsimd.memset(xw.t[:, :, 0:1], 1.0), wr=[xw])
                    if c_hi < hi:
                        op(pool, lambda: nc.gpsimd.memset(xw.t[:, :, nw - 1:nw], 1.0), wr=[xw])
                    dma_load(k, sp, xw, xw.t[:, :, c_lo - lo:c_hi - lo],
                             xs_a[:, base + c_lo:base + c_hi].rearrange("(c p) t -> p c t", p=128))
                    op(act, lambda: nc.scalar.activation(out=actb.t[:, 0:NKC, 0:nw], in_=xw.t[:, :, 0:nw], func=AF.Square),
                       rd=[xw], wr=[actb])

                    def mm_ss2():
                        ins = None
                        for c in range(NKC):
                            ins = nc.tensor.matmul(pr.t[:, 0:nw], lhsT=ones_bf.t[:], rhs=actb.t[:, c, 0:nw],
                                                   start=(c == 0), stop=(c == NKC - 1))
                        return ins
                    op(pe, mm_ss2, rd=[ones_bf, actb], wr=[pr])
                    op(act, lambda: nc.scalar.activation(out=rs2.t[:, 0:nw], in_=pr.t[:, 0:nw], func=AF.Ln,
                                                         scale=1.0 / D, bias=EPS), rd=[pr], wr=[rs2])
                    op(act, lambda: nc.scalar.activation(out=rs2.t[:, 0:nw], in_=rs2.t[:, 0:nw], func=AF.Exp, scale=-0.5),
                       rd=[rs2], wr=[rs2])
                    for c in range(NKC):
                        tf = tf2[c % 2]
                        op(dve, lambda c=c, tf=tf: nc.vector.scalar_tensor_tensor(
                            out=tf.t[:, 0:nw], in0=xw.t[:, c, 0:nw], scalar=gm[l].t[:, 1, c, tcol:tcol + 1],
                            in1=rs2.t[:, 0:nw], op0=ALU.mult, op1=ALU.mult), rd=[xw, gm[l], rs2], wr=[tf])
                        op(act, lambda c=c, tf=tf: nc.scalar.activation(
                            out=h2.t[:, c, 0:nw], in_=tf.t[:, 0:nw], func=AF.Identity,
                            bias=mods[l].t[:, 24 + c, tcol:tcol + 1], scale=1.0), rd=[tf, mods[l]], wr=[h2])
                    if c_lo > lo:
                        op(pool, lambda: nc.gpsimd.memset(h2.t[:, :, 0:1], 0.0), wr=[h2])
                    if c_hi < hi:
                        op(pool, lambda: nc.gpsimd.memset(h2.t[:, :, nw - 1:nw], 0.0), wr=[h2])
                    for j in range(NJ):
                        i2 = j % 2
                        pub, pgb, tub, tgb = pu[i2], pg[i2], tu[i2], tg[i2]

                        def mm_up(col, dst):
                            def f():
                                ins = None
                                for kc in range(NKC):
                                    ins = nc.tensor.matmul(dst.t[:, 0:nw], lhsT=Wu.t[:, kc, col * 128:(col + 1) * 128],
                                                           rhs=h2.t[:, kc, 0:nw], start=(kc == 0), stop=(kc == NKC - 1))
                                return ins
                            return f
                        op(pe, mm_up(j, pub), rd=[Wu, h2], wr=[pub])
                        op(pe, mm_up(NJ + j, pgb), rd=[Wu, h2], wr=[pgb])
                        for ei, (col, pp, tt_) in enumerate(((j, pub, tub), (NJ + j, pgb, tgb))):
                            cw0 = PT_CW + 3 * col
                            cb0 = PT_CB + col
                            op(act, lambda pp=pp, tt_=tt_, cw0=cw0, cb0=cb0: nc.scalar.activation(
                                out=tt_.t[:, 0:tn], in_=pp.t[:, 0:tn], func=AF.Identity,
                                scale=ptb[l].t[:, cw0:cw0 + 1], bias=ptb[l].t[:, cb0:cb0 + 1]), rd=[pp, ptb[l]], wr=[tt_])
                            for tap in (1, 2):
                                tmb = tm[tmc[0] % 3]
                                tmc[0] += 1
                                op(act, lambda pp=pp, tmb=tmb, cw0=cw0, tap=tap: nc.scalar.activation(
                                    out=tmb.t[:, 0:tn], in_=pp.t[:, tap:tap + tn], func=AF.Copy,
                                    scale=ptb[l].t[:, cw0 + tap:cw0 + tap + 1]), rd=[pp, ptb[l]], wr=[tmb])
                                if ei == 0:
                                    op(dve, lambda tt_=tt_, tmb=tmb: nc.vector.tensor_tensor(
                                        out=tt_.t[:, 0:tn], in0=tt_.t[:, 0:tn], in1=tmb.t[:, 0:tn], op=ALU.add),
                                       rd=[tmb, tt_], wr=[tt_])
                                else:
                                    op(pool, lambda tt_=tt_, tmb=tmb: nc.gpsimd.tensor_tensor(
                                        out=tt_.t[:, 0:tn], in0=tt_.t[:, 0:tn], in1=tmb.t[:, 0:tn], op=ALU.add),
                                       rd=[tmb, tt_], wr=[tt_])
                        op(act, lambda: nc.scalar.activation(out=tgb.t[:, 0:tn], in_=tgb.t[:, 0:tn], func=AF.Silu),
                           rd=[tgb], wr=[tgb])
                        op(dve, lambda j=j: nc.vector.tensor_tensor(out=actb.t[:, j, 0:tn], in0=tub.t[:, 0:tn],
                                                                    in1=tgb.t[:, 0:tn], op=ALU.mult),
                           rd=[tub, tgb], wr=[actb])
                    for oc in range(NKC):
                        pdb = pd[oc % 3]
                        xcb = xc[oc % 2]
                        dma_load(k, sp, xcb, xcb.t[:, 0:tn], xs_a[oc * 128:(oc + 1) * 128, base + s0:base + s0 + tn])

                        def mm_dn(oc=oc, pdb=pdb):
                            ins = None
                            for j in range(NJ):
                                ins = nc.tensor.matmul(pdb.t[:, 0:tn], lhsT=Wd.t[:, j, oc * 128:(oc + 1) * 128],
                                                       rhs=actb.t[:, j, 0:tn], start=(j == 0), stop=(j == NJ - 1))
                            return ins
                        op(pe, mm_dn, rd=[Wd, actb], wr=[pdb])
                        tmb = tm[tmc[0] % 3]
                        tmc[0] += 1
                        op(act, lambda oc=oc, pdb=pdb, tmb=tmb: nc.scalar.activation(
                            out=tmb.t[:, 0:tn], in_=pdb.t[:, 0:tn], func=AF.Copy,
                            scale=mods[l].t[:, 40 + oc, tcol:tcol + 1]), rd=[pdb, mods[l]], wr=[tmb])
                        op(dve, lambda xcb=xcb, tmb=tmb: nc.vector.tensor_tensor(
                            out=xcb.t[:, 0:tn], in0=xcb.t[:, 0:tn], in1=tmb.t[:, 0:tn], op=ALU.add), rd=[tmb, xcb], wr=[xcb])
                        if last:
                            dma_store(k, sp, yT[oc * 128:(oc + 1) * 128, s0:s0 + tn], xcb, xcb.t[:, 0:tn])
                        else:
                            dma_store(k, sp, xs_b[oc * 128:(oc + 1) * 128, base + s0:base + s0 + tn], xcb, xcb.t[:, 0:tn])
                barrier(k)
        barrier(k)
    return nc


def _rope_tables():
    def ang(dim):
        half = dim // 2
        inv = (10000.0 ** (-np.arange(0, half, 2, dtype=np.float32) / half)).astype(np.float32)
        t = np.arange(L)
        row = (t // 64).astype(np.float32)
        col = (t % 64).astype(np.float32)
        return np.concatenate([row[:, None] * inv, col[:, None] * inv], axis=-1).astype(np.float32)

    out = []
    for dim in (64, 32):
        a = ang(dim)
        cos = np.cos(a).astype(np.float32)
        sin = np.sin(a).astype(np.float32)
        d = np.arange(dim)
        c = cos[:, d // 2].T
        s = sin[:, d // 2].T * np.where(d % 2 == 0, -1.0, 1.0)[:, None]
        tab = np.zeros((2, 128, T), np.float32)
        rep = 128 // dim
        tab[0, :, :L] = np.tile(c, (rep, 1))
        tab[1, :, :L] = np.tile(s, (rep, 1))
        tab[0, :, L:] = 1.0
        out.append(tab)
    return out


def _bias_patterns():
    classes = [2, 0, 1, 30, 31]
    valid = np.zeros((25, 128, 128), bool)
    ri = np.zeros((25, 128, 128), np.int64)
    ci = np.zeros((25, 128, 128), np.int64)
    kp = np.arange(128)[:, None]
    qf = np.arange(128)[None, :]
    for cidx, i in enumerate(classes):
        for s in range(5):
            kt = min(max(i - 2, 0), 27) + s
            qr = 2 * i + qf // 64
            qc = qf % 64
            kr = 2 * kt + kp // 64
            kc = kp % 64
            rs_ = np.clip(qr - 4, 0, 56)
            cs_ = np.clip(qc - 8, 0, 48)
            v = (kr >= rs_) & (kr < rs_ + 8) & (kc >= cs_) & (kc < cs_ + 16)
            p = cidx * 5 + s
            valid[p] = v
            ri[p] = np.where(v, kr - qr + 7, 0)
            ci[p] = np.where(v, kc - qc + 15, 0)
    return valid, ri, ci


def _prep_shared(inp):
    f = lambda a: np.ascontiguousarray(np.asarray(a, dtype=np.float32))
    w_in = f(inp["w_in"])
    sw = np.arange(64) ^ 1
    cols = []
    qa = np.arange(0, 512)
    cols += [qa]
    cols += [(qa // 64) * 64 + sw[qa % 64]]
    ka = [512 + 64 * g + np.arange(64) for g in range(2)]
    cols += [np.concatenate([ka[0], ka[0]]), np.concatenate([ka[1], ka[1]])]
    cols += [np.concatenate([ka[0][sw], ka[0][sw]]), np.concatenate([ka[1][sw], ka[1][sw]])]
    cols += [768 + np.arange(256)]
    cols += [1024 + np.arange(256)]
    qc = 1536 + np.arange(256)
    kc = 1792 + np.arange(256)
    cols += [qc, qc ^ 1, kc, kc ^ 1]
    cols += [640 + np.arange(128), 1280 + np.arange(256), 2048 + np.arange(256)]
    cols = np.concatenate(cols)
    assert cols.shape[0] == W1C
    w1 = np.ascontiguousarray(w_in[:, :, cols])

    ptab = np.zeros((DEPTH, 128, NPT), np.float32)
    p = np.arange(128)
    for l in range(DEPTH):
        ptab[l, :, PT_BADA:PT_BADA + 48] = f(inp["b_ada"])[l].reshape(48, 128).T
        ptab[l, :, PT_G1:PT_G1 + 8] = f(inp["g_norm1"])[l].reshape(8, 128).T
        ptab[l, :, PT_G2:PT_G2 + 8] = f(inp["g_norm2"])[l].reshape(8, 128).T
        gqa, gka = f(inp["gq_a"])[l], f(inp["gk_a"])[l]
        gqb, gkb = f(inp["gq_b"])[l], f(inp["gk_b"])[l]
        gqc, gkc = f(inp["gq_c"])[l], f(inp["gk_c"])[l]
        g = np.zeros((128, NSLAB), np.float32)
        for s in range(4):
            g[:, s] = gqa[p % 64]
            g[:, 4 + s] = gqa[(p % 64) ^ 1]
        for s in range(2):
            g[:, 8 + s] = gka[p % 64]
            g[:, 10 + s] = gka[(p % 64) ^ 1]
            g[:, 12 + s] = gqb[p % 64]
            g[:, 14 + s] = gkb[p % 64]
            g[:, 16 + s] = gqc[p % 32]
            g[:, 18 + s] = gqc[(p % 32) ^ 1]
            g[:, 20 + s] = gkc[p % 32]
            g[:, 22 + s] = gkc[(p % 32) ^ 1]
        ptab[l, :, PT_GTAB:PT_GTAB + NSLAB] = g
        cw = f(inp["conv_w"])[l]
        ptab[l, :, PT_CW:PT_CW + 132] = cw.reshape(3, 44, 128).transpose(2, 1, 0).reshape(128, 132)
        ptab[l, :, PT_CB:PT_CB + 44] = f(inp["conv_b"])[l].reshape(44, 128).T
        ptab[l, :, PT_GSUB] = f(inp["g_subln"])[l][p % 64]
        for i, nm in enumerate(("lambda_q1", "lambda_k1", "lambda_q2", "lambda_k2")):
            ptab[l, :, PT_LAM + 32 * i:PT_LAM + 32 * i + 32] = f(inp[nm])[l][None, :]

    ropeA, ropeC = _rope_tables()
    valid, ri, ci = _bias_patterns()
    rpb = f(inp["rpb_b"])
    bias = np.where(valid[None, None], rpb[:, :, ri, ci], np.float32(NEG)).astype(np.float32)
    biasB = np.ascontiguousarray(bias.transpose(0, 1, 3, 2, 4))
    consts = np.zeros((3, 128, 128), np.float32)
    consts[0] = np.eye(128, dtype=np.float32)
    consts[1] = np.kron(np.eye(2, dtype=np.float32), np.ones((64, 64), np.float32))
    consts[2] = np.kron(np.eye(4, dtype=np.float32), np.ones((32, 32), np.float32))
    return dict(consts=consts, w_ada=f(inp["w_ada"]), ptab=ptab, w1=w1, ropeA=ropeA, ropeC=ropeC, biasB=biasB,
                w_out=f(inp["w_out"]), w_up=f(inp["w_up"]), w_down=f(inp["w_down"]))


def _prep_core(inp, b):
    x = np.asarray(inp["x"][b], np.float32)
    ctx = np.asarray(inp["ctx"][b], np.float32)
    xT = np.ascontiguousarray(np.concatenate([x.T, ctx.T], axis=1))
    cv = np.stack([np.asarray(inp["c"][b], np.float32), np.asarray(inp["c_ctx"], np.float32)], axis=-1)
    cvec = np.ascontiguousarray(cv.reshape(8, 128, 2).transpose(1, 0, 2))
    return dict(xT=xT, cvec=cvec)


_NC_CACHE = {}


def kernel(**inputs):
    if "nc" not in _NC_CACHE:
        _NC_CACHE["nc"] = build_nc()
    nc = _NC_CACHE["nc"]
    shared = _prep_shared(inputs)
    in_maps = []
    for b in range(8):
        m = dict(shared)
        m.update(_prep_core(inputs, b))
        in_maps.append(m)
    outs = []
    for g0 in range(0, 8, GROUP):
        res = run_bass_kernel_spmd(nc, in_maps[g0:g0 + GROUP], core_ids=list(range(GROUP)))
        outs.extend(np.ascontiguousarray(res.results[b]["yT"].T) for b in range(GROUP))
    return np.stack(outs, axis=0).astype(np.float32)
```

```python
import math
import os
CUT = int(os.environ.get('KCUT', '99'))
from contextlib import ExitStack

import numpy as np
import concourse.bass as bass
import concourse.mybir as mybir
from concourse.bass_utils import run_bass_kernel_spmd

F32 = mybir.dt.float32
BF16 = mybir.dt.bfloat16
ALU = mybir.AluOpType
AF = mybir.ActivationFunctionType
AX = mybir.AxisListType

D = 1024
L = 4096
LC = 256
T = L + LC
DEPTH = 2
NKC = 8
FFN = 2816
NJ = FFN // 128
EPS = 1e-6
NEG = -30000.0
NSLAB = 24
W1C = NSLAB * 128 + 640
GROUP = 8
TB = 410

PT_BADA = 0
PT_G1 = 48
PT_G2 = 56
PT_GTAB = 64
PT_CW = 88
PT_CB = 220
PT_GSUB = 264
PT_LAM = 265
NPT = 393


class Eng:
    def __init__(self, nc, es, eng, name, skip_own=False):
        self.e = eng
        self.sem = es.enter_context(nc.semaphore(name))
        self.cnt = 0
        self.seen = {}
        self.skip_own = skip_own

    def wait(self, toks):
        best = {}
        for t in toks:
            if t is None:
                continue
            s, v = t
            if self.skip_own and s is self.sem:
                continue
            k = id(s)
            if self.seen.get(k, 0) >= v:
                continue
            if k not in best or best[k][1] < v:
                best[k] = (s, v)
        for k, (s, v) in best.items():
            self.e.wait_ge(s, v)
            self.seen[k] = v

    def sig(self, ins):
        self.cnt += 1
        ins.then_inc(self.sem, 1)
        return (self.sem, self.cnt)


class Buf:
    def __init__(self, t):
        self.t = t
        self.wr = None
        self.rd = {}
        self.dsem = None
        self.dcnt = 0
        self.excl = False

    def add_rd(self, tok):
        k = id(tok[0])
        if k not in self.rd or self.rd[k][1] < tok[1]:
            self.rd[k] = tok


class K:
    pass


def op(E, fn, rd=(), wr=()):
    toks = [b.wr for b in rd]
    for b in rd:
        if b.excl:
            toks.extend(b.rd.values())
    for b in wr:
        toks.append(b.wr)
        toks.extend(b.rd.values())
    E.wait(toks)
    tok = E.sig(fn())
    for b in rd:
        b.add_rd(tok)
    for b in wr:
        b.wr = tok
        b.rd = {}
    return tok


def _dsem(k, b):
    if b.dsem is None:
        k.nsem += 1
        b.dsem = k.es.enter_context(k.nc.semaphore("d%d" % k.nsem))
    return b.dsem


def dma_load(k, Q, b, dst_ap, src_ap, part=False):
    if not part:
        toks = [b.wr] + list(b.rd.values())
        Q.wait(toks)
    s = _dsem(k, b)
    ins = Q.e.dma_start(out=dst_ap, in_=src_ap)
    b.dcnt += 16
    ins.then_inc(s, 16)
    b.wr = (s, b.dcnt)
    b.rd = {}


def dma_store(k, Q, dst_ap, b, src_ap):
    Q.wait([b.wr])
    s = _dsem(k, b)
    ins = Q.e.dma_start(out=dst_ap, in_=src_ap)
    b.dcnt += 16
    ins.then_inc(s, 16)
    tok = (s, b.dcnt)
    b.add_rd(tok)
    k.stores[id(s)] = tok


def barrier(k):
    toks = [(E.sem, E.cnt) for E in k.engs if E.cnt > 0] + list(k.stores.values())
    for E in k.engs + [k.sp]:
        E.wait(toks)
    k.stores = {}


def build_nc(dbg=False, nlayers=DEPTH, stop_after=None):
    nc = bass.Bass("TRN2", target_bir_lowering=False)
    k = K()
    k.nc = nc
    k.nsem = 0
    k.stores = {}

    def din(name, shape, dt=F32):
        return nc.dram_tensor(name, list(shape), dt, kind="ExternalInput").ap()

    xT = din("xT", [D, T])
    cvec = din("cvec", [128, NKC, 2])
    w_ada = din("w_ada", [DEPTH, D, 6 * D])
    ptab = din("ptab", [DEPTH, 128, NPT])
    w1 = din("w1", [DEPTH, D, W1C])
    ropeA = din("ropeA", [2, 128, T])
    ropeC = din("ropeC", [2, 128, T])
    biasB = din("biasB", [DEPTH, 4, 128, 25, 128])
    w_out = din("w_out", [DEPTH, D, D])
    w_up = din("w_up", [DEPTH, D, 2 * FFN])
    w_down = din("w_down", [DEPTH, FFN, D])
    consts = din("consts", [3, 128, 128])
    yTh = [nc.dram_tensor("yT%d" % i, [D // 2, L], F32, kind="ExternalOutput").ap() for i in range(2)]

    skind = "ExternalOutput" if dbg else "Internal"

    def dscr(name, shape, dt):
        return nc.dram_tensor(name, list(shape), dt, kind=skind).ap()

    xs_a = dscr("xs_a", [D, T], F32)
    xs_b = dscr("xs_b", [D, T], F32)
    qs = dscr("qs", [8, 128, T], BF16)
    ks = dscr("ks", [16, 128, T], BF16)
    vs = dscr("vs", [10, T, 128], BF16)
    attn = dscr("attn", [D, T], BF16)
    modd = dscr("modd", [DEPTH, 128, 96], F32) if dbg else None

    with ExitStack() as es:
        k.es = es
        pe = Eng(nc, es, nc.tensor, "s_pe", skip_own=True)
        act = Eng(nc, es, nc.scalar, "s_act")
        dve = Eng(nc, es, nc.vector, "s_dve")
        pool = Eng(nc, es, nc.gpsimd, "s_pool")
        sp = Eng(nc, es, nc.sync, "s_sp")
        k.engs = [pe, act, dve, pool]
        k.sp = sp

        uniq = [0]

        def sb(st, name, shape, dt):
            uniq[0] += 1
            return Buf(st.enter_context(nc.sbuf_tensor("%s_u%d" % (name, uniq[0]), list(shape), dt)))

        def ps(st, name, shape=(128, 512), dt=F32):
            uniq[0] += 1
            b = Buf(st.enter_context(nc.psum_tensor("%s_u%d" % (name, uniq[0]), list(shape), dt)))
            b.excl = True
            return b

        ones_bf = sb(es, "ones_bf", [128, 128], BF16)
        blk64 = sb(es, "blk64", [128, 128], BF16)
        blk32 = sb(es, "blk32", [128, 128], BF16)
        ident = sb(es, "ident", [128, 128], F32)
        ptb = [sb(es, "ptb%d" % l, [128, NPT], F32) for l in range(DEPTH)]
        mods = [sb(es, "mods%d" % l, [128, 48, 2], F32) for l in range(DEPTH)]
        gm = [sb(es, "gm%d" % l, [128, 2, NKC, 2], F32) for l in range(DEPTH)]
        lamt = [sb(es, "lamt%d" % l, [128, 4], F32) for l in range(DEPTH)]

        op(pool, lambda: nc.gpsimd.memset(ones_bf.t[:], 1.0), wr=[ones_bf])
        zbf = sb(es, "zbf", [128, 128], BF16)
        op(pool, lambda: nc.gpsimd.memset(zbf.t[:], 0.0), wr=[zbf])
        dma_load(k, sp, ident, ident.t[:], consts[0])
        dma_load(k, pool, blk64, blk64.t[:], consts[1])
        dma_load(k, pool, blk32, blk32.t[:], consts[2])
        for l in range(DEPTH):
            dma_load(k, sp, ptb[l], ptb[l].t[:], ptab[l])

        with ExitStack() as p0:
            csb = sb(p0, "csb", [128, NKC, 2], F32)
            scs = sb(p0, "scs", [128, NKC, 2], F32)
            wa = [sb(p0, "wa%d" % i, [128, NKC, 768], F32) for i in range(2)]
            modp = ps(p0, "modp", [128, 512])
            tmpl = sb(p0, "tmpl", [128, 32], F32)
            zer_bf = sb(p0, "zer_bf", [128, T], BF16)
            op(pool, lambda: nc.gpsimd.memset(zer_bf.t[:], 0.0), wr=[zer_bf])
            for v in range(16):
                dma_store(k, sp, ks[v], zer_bf, zer_bf.t[:])
            dma_load(k, sp, csb, csb.t[:], cvec)
            op(act, lambda: nc.scalar.activation(out=scs.t[:], in_=csb.t[:], func=AF.Silu), rd=[csb], wr=[scs])
            for l in range(nlayers):
                for pc in range(8):
                    wb = wa[pc % 2]
                    dma_load(k, sp, wb, wb.t[:],
                             w_ada[l, :, pc * 768:(pc + 1) * 768].rearrange("(k p) c -> p k c", p=128))

                    def mm(wb=wb, pc=pc):
                        ins = None
                        for jl in range(6):
                            j = pc * 6 + jl
                            for kc in range(NKC):
                                ins = nc.tensor.matmul(modp.t[:, 2 * j:2 * j + 2], lhsT=wb.t[:, kc, jl * 128:(jl + 1) * 128],
                                                       rhs=scs.t[:, kc, :], start=(kc == 0), stop=(kc == NKC - 1))
                        return ins
                    op(pe, mm, rd=[wb, scs], wr=[modp])
                mv = modp.t[:, 0:96].rearrange("p (j t) -> p j t", t=2)
                for t in range(2):
                    op(dve, lambda t=t: nc.vector.tensor_tensor(out=mods[l].t[:, :, t], in0=mv[:, :, t],
                                                                in1=ptb[l].t[:, PT_BADA:PT_BADA + 48], op=ALU.add),
                       rd=[modp, ptb[l]], wr=[mods[l]])
                for n, (sc0, g0) in enumerate(((8, PT_G1), (32, PT_G2))):
                    for t in range(2):
                        op(dve, lambda n=n, sc0=sc0, g0=g0, t=t: nc.vector.scalar_tensor_tensor(
                            out=gm[l].t[:, n, :, t], in0=mods[l].t[:, sc0:sc0 + 8, t], scalar=1.0,
                            in1=ptb[l].t[:, g0:g0 + 8], op0=ALU.add, op1=ALU.mult),
                           rd=[mods[l], ptb[l]], wr=[gm[l]])
                lam_init = 0.8 - 0.6 * math.exp(-0.3 * l)
                for i in range(2):
                    a0 = PT_LAM + 64 * i
                    op(dve, lambda a0=a0: nc.vector.tensor_tensor(out=tmpl.t[:], in0=ptb[l].t[:, a0:a0 + 32],
                                                                  in1=ptb[l].t[:, a0 + 32:a0 + 64], op=ALU.mult),
                       rd=[ptb[l]], wr=[tmpl])
                    op(dve, lambda i=i: nc.vector.reduce_sum(out=lamt[l].t[:, 2 + i:3 + i], in_=tmpl.t[:], axis=AX.X),
                       rd=[tmpl], wr=[lamt[l]])
                op(act, lambda: nc.scalar.activation(out=lamt[l].t[:, 2:4], in_=lamt[l].t[:, 2:4], func=AF.Exp),
                   rd=[lamt[l]], wr=[lamt[l]])
                op(dve, lambda: nc.vector.scalar_tensor_tensor(out=lamt[l].t[:, 0:1], in0=lamt[l].t[:, 3:4], scalar=-lam_init,
                                                               in1=lamt[l].t[:, 2:3], op0=ALU.add, op1=ALU.subtract),
                   rd=[lamt[l]], wr=[lamt[l]])
                op(dve, lambda: nc.vector.tensor_scalar(out=lamt[l].t[:, 1:2], in0=ptb[l].t[:, PT_GSUB:PT_GSUB + 1],
                                                        scalar1=1.0 - lam_init, scalar2=None, op0=ALU.mult),
                   rd=[ptb[l]], wr=[lamt[l]])
                if dbg:
                    dma_store(k, sp, modd[l].rearrange("p (j t) -> p j t", t=2), mods[l], mods[l].t[:])
            barrier(k)

        blocks = [(512 * i, 512) for i in range(8)] + [(L, LC)]

        for l in range(nlayers if stop_after != 'p0' else 0):
            last = (l == DEPTH - 1)
            with_ctx = not last
            src = xT if l == 0 else xs_b

            with ExitStack() as p1:
                W1 = sb(p1, "W1", [128, NKC, W1C], BF16)
                for c0 in range(0, W1C, 1856):
                    dma_load(k, pool, W1, W1.t[:, :, c0:c0 + 1856],
                             w1[l, :, c0:c0 + 1856].rearrange("(k p) c -> p k c", p=128), part=(c0 > 0))
                xin = [sb(p1, "xin%d" % i, [128, NKC, 512], F32) for i in range(2)]
                sq = sb(p1, "sq", [128, NKC, 512], BF16)
                rs = sb(p1, "rs", [128, 512], F32)
                tmpf = [sb(p1, "tmpf%d" % i, [128, 512], F32) for i in range(2)]
                hT = [sb(p1, "hT%d" % i, [128, NKC, 512], BF16) for i in range(2)]
                rop = [[sb(p1, "rop%d_%d" % (i, j), [128, 512], F32) for j in range(4)] for i in range(2)]
                sqq = [sb(p1, "sqq%d" % i, [128, 512], BF16) for i in range(2)]
                rq = [sb(p1, "rq%d" % i, [128, 512], F32) for i in range(2)]
                t1 = [sb(p1, "t1_%d" % i, [128, 512], F32) for i in range(2)]
                t2 = [sb(p1, "t2_%d" % i, [128, 512], F32) for i in range(2)]
                outb = [sb(p1, "outb%d" % i, [128, 512], BF16) for i in range(3)]
                vout = [sb(p1, "vout%d" % i, [128, 10, 128], BF16) for i in range(2)]
                pss = ps(p1, "pss")
                pv0 = ps(p1, "pv0")
                pm = [ps(p1, "pm%d" % i) for i in range(2)]
                pw = [ps(p1, "pw%d" % i) for i in range(2)]
                pq = [ps(p1, "pq%d" % i) for i in range(2)]
                for i in range(2):
                    op(pool, lambda i=i: nc.gpsimd.memset(vout[i].t[:], 1.0), wr=[vout[i]])

                jobs1 = []
                for s in range(4):
                    jobs1.append((s, 4 + s, 'A', 64, True, [(qs[s], 0, 128)]))
                for g in range(2):
                    jobs1.append((8 + g, 10 + g, 'A', 64, False, [(ks[2 * g], 0, 64), (ks[2 * g + 1], 64, 128)]))
                for s in range(2):
                    jobs1.append((12 + s, None, None, 64, True, [(qs[4 + s], 0, 128)]))
                for s in range(2):
                    jobs1.append((14 + s, None, None, 64, False, [(ks[4 + 2 * s], 0, 64), (ks[5 + 2 * s], 64, 128)]))
                for s in range(2):
                    jobs1.append((16 + s, 18 + s, 'C', 32, True, [(qs[6 + s], 0, 128)]))
                for s in range(2):
                    jobs1.append((20 + s, 22 + s, 'C', 32, False,
                                  [(ks[8 + 4 * s + j], 32 * j, 32 * j + 32) for j in range(4)]))

                cnt1 = 0
                cnto = 0
                for bi, (t0, n) in enumerate(blocks if CUT > 1 else []):
                    if CUT < 10 and bi > 0:
                        break
                    tcol = 0 if t0 < L else 1
                    xb = xin[bi % 2]
                    hb = hT[bi % 2]
                    rp = rop[bi % 2]
                    dma_load(k, sp, xb, xb.t[:, :, 0:n], src[:, t0:t0 + n].rearrange("(c p) t -> p c t", p=128))
                    for j, (tab, idx) in enumerate(((ropeA, 0), (ropeA, 1), (ropeC, 0), (ropeC, 1))):
                        dma_load(k, sp, rp[j], rp[j].t[:, 0:n], tab[idx, :, t0:t0 + n])
                    op(act, lambda: nc.scalar.activation(out=sq.t[:, :, 0:n], in_=xb.t[:, :, 0:n], func=AF.Square),
                       rd=[xb], wr=[sq])

                    def mm_ss():
                        ins = None
                        for c in range(NKC):
                            ins = nc.tensor.matmul(pss.t[:, 0:n], lhsT=ones_bf.t[:], rhs=sq.t[:, c, 0:n],
                                                   start=(c == 0), stop=(c == NKC - 1))
                        return ins
                    op(pe, mm_ss, rd=[ones_bf, sq], wr=[pss])
                    op(act, lambda: nc.scalar.activation(out=rs.t[:, 0:n], in_=pss.t[:, 0:n], func=AF.Ln,
                                                         scale=1.0 / D, bias=EPS), rd=[pss], wr=[rs])
                    op(act, lambda: nc.scalar.activation(out=rs.t[:, 0:n], in_=rs.t[:, 0:n], func=AF.Exp, scale=-0.5),
                       rd=[rs], wr=[rs])
                    for c in range(NKC):
                        tf = tmpf[c % 2]
                        op(dve, lambda c=c, tf=tf: nc.vector.scalar_tensor_tensor(
                            out=tf.t[:, 0:n], in0=xb.t[:, c, 0:n], scalar=gm[l].t[:, 0, c, tcol:tcol + 1],
                            in1=rs.t[:, 0:n], op0=ALU.mult, op1=ALU.mult), rd=[xb, gm[l], rs], wr=[tf])
                        op(act, lambda c=c, tf=tf: nc.scalar.activation(
                            out=hb.t[:, c, 0:n], in_=tf.t[:, 0:n], func=AF.Identity,
                            bias=mods[l].t[:, c, tcol:tcol + 1], scale=1.0), rd=[tf, mods[l]], wr=[hb])

                    for tt in range(n // 128 if CUT > 2 else 0):
                        vo = vout[cnt1 % 2]
                        cnt1 += 1

                        def mm_v(tt=tt):
                            ins = None
                            for kc in range(NKC):
                                ins = nc.tensor.matmul(pv0.t[:, 0:512], lhsT=hb.t[:, kc, tt * 128:(tt + 1) * 128],
                                                       rhs=W1.t[:, kc, 3072:3584], start=(kc == 0), stop=(kc == NKC - 1))
                            for kc in range(NKC):
                                ins = nc.tensor.matmul(pss.t[:, 0:128], lhsT=hb.t[:, kc, tt * 128:(tt + 1) * 128],
                                                       rhs=W1.t[:, kc, 3584:3712], start=(kc == 0), stop=(kc == NKC - 1))
                            return ins
                        op(pe, mm_v, rd=[hb, W1], wr=[pv0, pss])
                        op(dve, lambda vo=vo: nc.vector.tensor_copy(
                            out=vo.t[:, 0:8, 0:64], in_=pv0.t[:, 0:512].rearrange("p (v c) -> p v c", c=64)),
                           rd=[pv0], wr=[vo])
                        op(dve, lambda vo=vo: nc.vector.tensor_copy(
                            out=vo.t[:, 8:10, 0:64], in_=pss.t[:, 0:128].rearrange("p (v c) -> p v c", c=64)),
                           rd=[pss], wr=[vo])
                        tk = t0 + tt * 128
                        dma_store(k, sp, vs[:, tk:tk + 128, :].rearrange("v p c -> p v c"), vo, vo.t[:])

                    for (sm, sw, rope, dd, is_q, dests) in (jobs1 if CUT > 3 else []):
                        i2 = cnt1 % 2
                        cnt1 += 1
                        pmb, pwb, pqb = pm[i2], pw[i2], pq[i2]
                        sqb, rqb, t1b, t2b = sqq[i2], rq[i2], t1[i2], t2[i2]

                        def mm_slab(slab, dst):
                            def f():
                                ins = None
                                for kc in range(NKC):
                                    ins = nc.tensor.matmul(dst.t[:, 0:n], lhsT=W1.t[:, kc, slab * 128:(slab + 1) * 128],
                                                           rhs=hb.t[:, kc, 0:n], start=(kc == 0), stop=(kc == NKC - 1))
                                return ins
                            return f
                        op(pe, mm_slab(sm, pmb), rd=[W1, hb], wr=[pmb])
                        if sw is not None:
                            op(pe, mm_slab(sw, pwb), rd=[W1, hb], wr=[pwb])
                        op(act, lambda: nc.scalar.activation(out=sqb.t[:, 0:n], in_=pmb.t[:, 0:n], func=AF.Square),
                           rd=[pmb], wr=[sqb])
                        blk = blk64 if dd == 64 else blk32
                        op(pe, lambda: nc.tensor.matmul(pqb.t[:, 0:n], lhsT=blk.t[:], rhs=sqb.t[:, 0:n], start=True, stop=True),
                           rd=[blk, sqb], wr=[pqb])
                        if is_q:
                            a, b = 1.0, dd * EPS
                        else:
                            a, b = 1.0 / dd, EPS
                        op(act, lambda: nc.scalar.activation(out=rqb.t[:, 0:n], in_=pqb.t[:, 0:n], func=AF.Ln, scale=a, bias=b),
                           rd=[pqb], wr=[rqb])
                        op(act, lambda: nc.scalar.activation(out=rqb.t[:, 0:n], in_=rqb.t[:, 0:n], func=AF.Exp, scale=-0.5),
                           rd=[rqb], wr=[rqb])
                        ob = outb[cnto % 3]
                        cnto += 1
                        gcol = PT_GTAB + sm
                        if CUT < 5:
                            continue
                        if os.environ.get('KSKIPB') and rope is None:
                            continue
                        if os.environ.get('KSKIPR') and rope is not None:
                            continue
                        if rope is not None:
                            cosb, sinb = (rp[0], rp[1]) if rope == 'A' else (rp[2], rp[3])
                            gcs = PT_GTAB + sw
                            op(dve, lambda: nc.vector.scalar_tensor_tensor(
                                out=t1b.t[:, 0:n], in0=cosb.t[:, 0:n], scalar=ptb[l].t[:, gcol:gcol + 1],
                                in1=pmb.t[:, 0:n], op0=ALU.mult, op1=ALU.mult), rd=[pmb, ptb[l], cosb], wr=[t1b])
                            op(dve, lambda: nc.vector.scalar_tensor_tensor(
                                out=t2b.t[:, 0:n], in0=sinb.t[:, 0:n], scalar=ptb[l].t[:, gcs:gcs + 1],
                                in1=pwb.t[:, 0:n], op0=ALU.mult, op1=ALU.mult), rd=[pwb, ptb[l], sinb], wr=[t2b])
                            if CUT < 6:
                                continue
                            op(pool, lambda: nc.gpsimd.tensor_tensor(out=t1b.t[:, 0:n], in0=t1b.t[:, 0:n], in1=t2b.t[:, 0:n],
                                                                     op=ALU.add), rd=[t1b, t2b], wr=[t1b])
                            op(pool, lambda: nc.gpsimd.tensor_tensor(out=ob.t[:, 0:n], in0=t1b.t[:, 0:n], in1=rqb.t[:, 0:n],
                                                                     op=ALU.mult), rd=[t1b, rqb], wr=[ob])
                        else:
                            op(dve, lambda: nc.vector.scalar_tensor_tensor(
                                out=ob.t[:, 0:n], in0=rqb.t[:, 0:n], scalar=ptb[l].t[:, gcol:gcol + 1],
                                in1=pmb.t[:, 0:n], op0=ALU.mult, op1=ALU.mult), rd=[pmb, ptb[l], rqb], wr=[ob])
                        for (dap, r0, r1) in (dests if CUT > 6 else []):
                            dma_store(k, sp, dap[r0:r1, t0:t0 + n], ob, ob.t[r0:r1, 0:n])
                barrier(k)
            if stop_after == (l, 1):
                break

            with ExitStack() as p2:
                NS, NP, LA = 3, 4, 2
                Kb = [[sb(p2, "Kb%d_%d" % (i, j), [128, T], BF16) for j in range(2)] for i in range(2)]
                Vb = [sb(p2, "Vb%d" % i, [128, 34, 128], BF16) for i in range(2)]
                Qb = [sb(p2, "Qb%d" % i, [128, T], BF16) for i in range(2)]
                Bb = [sb(p2, "Bb%d" % i, [128, 25, 128], F32) for i in range(2)]
                Pb = [sb(p2, "Pb%d" % i, [128, 512], BF16) for i in range(NP)]
                rb = [sb(p2, "rb%d" % i, [128, 512], F32) for i in range(2)]
                ab = [sb(p2, "ab%d" % i, [128, 512], F32) for i in range(2)]
                sqc = sb(p2, "sqc", [128, 512], BF16)
                rsc = sb(p2, "rsc", [128, 512], F32)
                osb = [sb(p2, "osb%d" % i, [128, 512], BF16) for i in range(3)]
                Sp = [ps(p2, "Sp%d" % i) for i in range(NS)]
                Op = [ps(p2, "Op%d" % i) for i in range(4)]
                Mp = ps(p2, "Mp")

                jobs = []
                for h in range(8):
                    jobs.append(dict(kind='A', kv=[2 * (h // 4) + (h % 2)], vh=h // 4, q=h // 2, row=64 * h))
                for h in range(4):
                    jobs.append(dict(kind='B', kv=[4 + h], vh=2 + h, q=4 + h // 2, row=512 + 64 * h, bh=h))
                for h in range(4):
                    jobs.append(dict(kind='C', kv=[8 + 2 * h, 9 + 2 * h], vh=6 + h, q=6 + h // 2, row=768 + 64 * h))

                if os.environ.get('KJOBS'):
                    jobs = [jb for jb in jobs if jb['kind'] in os.environ['KJOBS']]

                def load_job(ji):
                    jb = jobs[ji]
                    st = ji % 2
                    for m, kv in enumerate(jb['kv']):
                        dma_load(k, sp, Kb[st][m], Kb[st][m].t[:], ks[kv])
                    dma_load(k, sp, Qb[st], Qb[st].t[:], qs[jb['q']])
                    dma_load(k, sp, Vb[st], Vb[st].t[:], vs[jb['vh']].rearrange("(kt p) c -> p kt c", p=128))
                    if jb['kind'] == 'B':
                        dma_load(k, sp, Bb[st], Bb[st].t[:], biasB[l, jb['bh']])

                steps = []
                bc = 0
                ocnt = [0]
                for ji, jb in enumerate(jobs):
                    st = ji % 2
                    qblocks = [(512 * i, 512, False) for i in range(8)]
                    if with_ctx:
                        qblocks.append((L, LC, True))
                    nmap = len(jb['kv'])
                    first_of_job = True
                    for (t0, n, isctx) in qblocks:
                        Os = [Op[(nmap * bc + m) % 4] for m in range(nmap)]
                        bc += 1
                        blk_steps = []
                        Qt, Vt = Qb[st], Vb[st]
                        if jb['kind'] == 'B' and not isctx:
                            qb = t0 // 512
                            for s in range(5):
                                def S_fn(Sb, s=s, qb=qb, Kt=Kb[st][0], Qt=Qt, Bt=Bb[st]):
                                    ins = None
                                    for sbk in range(4):
                                        i = 4 * qb + sbk
                                        kt = min(max(i - 2, 0), 27) + s
                                        pat = s if 2 <= i <= 29 else {0: 5, 1: 10, 30: 15, 31: 20}[i] + s
                                        nc.tensor.matmul(Sb.t[:, sbk * 128:(sbk + 1) * 128], lhsT=Kt.t[:, kt * 128:(kt + 1) * 128],
                                                         rhs=Qt.t[:, i * 128:(i + 1) * 128], start=True, stop=False)
                                        ins = nc.tensor.matmul(Sb.t[:, sbk * 128:(sbk + 1) * 128], lhsT=ident.t[:],
                                                               rhs=Bt.t[:, pat, :], start=False, stop=True)
                                    return ins

                                def PV_fn(Pt, O, s=s, qb=qb, Vt=Vt, Qt=Qt, t0=t0):
                                    ins = None
                                    if s == 0:
                                        nc.tensor.matmul(O.t[:, 0:512], lhsT=zbf.t[:], rhs=Qt.t[:, t0:t0 + 512],
                                                         start=True, stop=False)
                                    for sbk in range(4):
                                        i = 4 * qb + sbk
                                        kt = min(max(i - 2, 0), 27) + s
                                        ins = nc.tensor.matmul(O.t[:, sbk * 128:(sbk + 1) * 128], lhsT=Vt.t[:, kt, :],
                                                               rhs=Pt.t[:, sbk * 128:(sbk + 1) * 128], start=False, stop=False)
                                    return ins
                                blk_steps.append(dict(S=S_fn, PV=PV_fn, O=Os[0], n=512, rdS=[Kb[st][0], Qt, Bb[st], ident],
                                                      rdV=[Vt, zbf, Qt]))
                            for j in range(2):
                                kt = 32 + j

                                def S_fn(Sb, kt=kt, Kt=Kb[st][0], Qt=Qt, t0=t0):
                                    return nc.tensor.matmul(Sb.t[:, 0:512], lhsT=Kt.t[:, kt * 128:(kt + 1) * 128],
                                                            rhs=Qt.t[:, t0:t0 + 512], start=True, stop=True)

                                def PV_fn(Pt, O, kt=kt, Vt=Vt, j=j):
                                    return nc.tensor.matmul(O.t[:, 0:512], lhsT=Vt.t[:, kt, :], rhs=Pt.t[:, 0:512],
                                                            start=False, stop=(j == 1))
                                blk_steps.append(dict(S=S_fn, PV=PV_fn, O=Os[0], n=512, rdS=[Kb[st][0], Qt], rdV=[Vt]))
                        else:
                            kts = [32, 33] if isctx else list(range(34))
                            for ki, kt in enumerate(kts):
                                for m in range(nmap):
                                    def S_fn(Sb, kt=kt, Kt=Kb[st][m], Qt=Qt, t0=t0, n=n):
                                        return nc.tensor.matmul(Sb.t[:, 0:n], lhsT=Kt.t[:, kt * 128:(kt + 1) * 128],
                                                                rhs=Qt.t[:, t0:t0 + n], start=True, stop=True)

                                    def PV_fn(Pt, O, kt=kt, Vt=Vt, n=n, ki=ki, nk=len(kts)):
                                        return nc.tensor.matmul(O.t[:, 0:n], lhsT=Vt.t[:, kt, :], rhs=Pt.t[:, 0:n],
                                                                start=(ki == 0), stop=(ki == nk - 1))
                                    blk_steps.append(dict(S=S_fn, PV=PV_fn, O=Os[m], n=n, rdS=[Kb[st][m], Qt], rdV=[Vt]))
                        if first_of_job:
                            blk_steps[0]['pre'] = ji
                            first_of_job = False
                        blk_steps[-1]['post'] = (jb, t0, n, Os)
                        steps.extend(blk_steps)

                def post_block(jb, t0, n, Os):
                    ob = osb[ocnt[0] % 3]
                    ocnt[0] += 1
                    if jb['kind'] != 'C':
                        O = Os[0]
                        r = rb[0]
                        op(dve, lambda: nc.vector.reciprocal(out=r.t[0:64, 0:n], in_=O.t[64:128, 0:n]), rd=[O], wr=[r])
                        op(dve, lambda: nc.vector.tensor_tensor(out=ob.t[0:64, 0:n], in0=O.t[0:64, 0:n], in1=r.t[0:64, 0:n],
                                                                op=ALU.mult), rd=[O, r], wr=[ob])
                    else:
                        O1, O2 = Os
                        op(dve, lambda: nc.vector.reciprocal(out=rb[0].t[0:64, 0:n], in_=O1.t[64:128, 0:n]), rd=[O1], wr=[rb[0]])
                        op(dve, lambda: nc.vector.reciprocal(out=rb[1].t[0:64, 0:n], in_=O2.t[64:128, 0:n]), rd=[O2], wr=[rb[1]])
                        op(dve, lambda: nc.vector.tensor_tensor(out=ab[0].t[0:64, 0:n], in0=O1.t[0:64, 0:n],
                                                                in1=rb[0].t[0:64, 0:n], op=ALU.mult), rd=[O1, rb[0]], wr=[ab[0]])
                        op(dve, lambda: nc.vector.scalar_tensor_tensor(
                            out=ab[1].t[0:64, 0:n], in0=rb[1].t[0:64, 0:n], scalar=lamt[l].t[0:64, 0:1],
                            in1=O2.t[0:64, 0:n], op0=ALU.mult, op1=ALU.mult), rd=[O2, lamt[l], rb[1]], wr=[ab[1]])
                        op(pool, lambda: nc.gpsimd.tensor_tensor(out=ab[0].t[0:64, 0:n], in0=ab[0].t[0:64, 0:n],
                                                                 in1=ab[1].t[0:64, 0:n], op=ALU.add), rd=[ab[0], ab[1]], wr=[ab[0]])
                        op(act, lambda: nc.scalar.activation(out=sqc.t[0:64, 0:n], in_=ab[0].t[0:64, 0:n], func=AF.Square),
                           rd=[ab[0]], wr=[sqc])
                        op(pe, lambda: nc.tensor.matmul(Mp.t[0:64, 0:n], lhsT=ones_bf.t[0:64, 0:64], rhs=sqc.t[0:64, 0:n],
                                                        start=True, stop=True), rd=[ones_bf, sqc], wr=[Mp])
                        op(act, lambda: nc.scalar.activation(out=rsc.t[0:64, 0:n], in_=Mp.t[0:64, 0:n], func=AF.Ln,
                                                             scale=1.0 / 64, bias=EPS), rd=[Mp], wr=[rsc])
                        op(act, lambda: nc.scalar.activation(out=rsc.t[0:64, 0:n], in_=rsc.t[0:64, 0:n], func=AF.Exp, scale=-0.5),
                           rd=[rsc], wr=[rsc])
                        op(dve, lambda: nc.vector.scalar_tensor_tensor(
                            out=ob.t[0:64, 0:n], in0=ab[0].t[0:64, 0:n], scalar=lamt[l].t[0:64, 1:2],
                            in1=rsc.t[0:64, 0:n], op0=ALU.mult, op1=ALU.mult), rd=[ab[0], lamt[l], rsc], wr=[ob])
                    dma_store(k, sp, attn[jb['row']:jb['row'] + 64, t0:t0 + n], ob, ob.t[0:64, 0:n])

                load_job(0)
                ns = len(steps)
                for i in range(ns + LA):
                    if i < ns:
                        stp = steps[i]
                        Sb, Pt = Sp[i % NS], Pb[i % NP]
                        op(pe, lambda: stp['S'](Sb), rd=stp['rdS'], wr=[Sb])
                        nn = stp['n']
                        op(act, lambda: nc.scalar.activation(out=Pt.t[:, 0:nn], in_=Sb.t[:, 0:nn], func=AF.Exp),
                           rd=[Sb], wr=[Pt])
                    if i >= LA:
                        j = i - LA
                        stj = steps[j]
                        Pj = Pb[j % NP]
                        if 'pre' in stj and stj['pre'] + 1 < len(jobs):
                            load_job(stj['pre'] + 1)
                        op(pe, lambda: stj['PV'](Pj, stj['O']), rd=[Pj] + stj['rdV'], wr=[stj['O']])
                        if 'post' in stj:
                            post_block(*stj['post'])
                barrier(k)
            if stop_after == (l, 2):
                break

            blocks3 = blocks if with_ctx else blocks[:8]
            with ExitStack() as p3:
                Wo = sb(p3, "Wo", [128, NKC, D], BF16)
                dma_load(k, pool, Wo, Wo.t[:], w_out[l].rearrange("(k p) c -> p k c", p=128))
                at = [sb(p3, "at%d" % i, [128, NKC, 512], BF16) for i in range(2)]
                xa = [sb(p3, "xa%d" % i, [128, NKC, 512], F32) for i in range(2)]
                po = [ps(p3, "po%d" % i) for i in range(4)]
                tmp3 = [sb(p3, "tmp3_%d" % i, [128, 512], F32) for i in range(2)]
                for bi, (t0, n) in enumerate(blocks3):
                    tcol = 0 if t0 < L else 1
                    a, x = at[bi % 2], xa[bi % 2]
                    dma_load(k, sp, a, a.t[:, :, 0:n], attn[:, t0:t0 + n].rearrange("(c p) t -> p c t", p=128))
                    dma_load(k, sp, x, x.t[:, :, 0:n], src[:, t0:t0 + n].rearrange("(c p) t -> p c t", p=128))
                    for oc in range(NKC):
                        pb = po[oc % 4]

                        def mm_o(oc=oc, pb=pb):
                            ins = None
                            for kc in range(NKC):
                                ins = nc.tensor.matmul(pb.t[:, 0:n], lhsT=Wo.t[:, kc, oc * 128:(oc + 1) * 128],
                                                       rhs=a.t[:, kc, 0:n], start=(kc == 0), stop=(kc == NKC - 1))
                            return ins
                        op(pe, mm_o, rd=[Wo, a], wr=[pb])
                        tb_ = tmp3[oc % 2]
                        op(act, lambda oc=oc, pb=pb, tb_=tb_: nc.scalar.activation(
                            out=tb_.t[:, 0:n], in_=pb.t[:, 0:n], func=AF.Copy,
                            scale=mods[l].t[:, 16 + oc, tcol:tcol + 1]), rd=[pb, mods[l]], wr=[tb_])
                        op(dve, lambda oc=oc, tb_=tb_: nc.vector.tensor_tensor(
                            out=x.t[:, oc, 0:n], in0=x.t[:, oc, 0:n], in1=tb_.t[:, 0:n], op=ALU.add), rd=[tb_, x], wr=[x])
                    dma_store(k, sp, xs_a[:, t0:t0 + n].rearrange("(c p) t -> p c t", p=128), x, x.t[:, :, 0:n])
                barrier(k)
            if stop_after == (l, 3):
                break

            with ExitStack() as p4:
                Wu = sb(p4, "Wu", [128, NKC, 2 * FFN], BF16)
                Wd = sb(p4, "Wd", [128, NJ, D], BF16)
                for c0 in range(0, 2 * FFN, 1408):
                    dma_load(k, pool, Wu, Wu.t[:, :, c0:c0 + 1408],
                             w_up[l, :, c0:c0 + 1408].rearrange("(k p) c -> p k c", p=128), part=(c0 > 0))
                for j0 in range(0, NJ, 11):
                    dma_load(k, pool, Wd, Wd.t[:, j0:j0 + 11, :],
                             w_down[l, j0 * 128:(j0 + 11) * 128, :].rearrange("(j p) c -> p j c", p=128), part=(j0 > 0))
                NW = TB + 2
                xw = sb(p4, "xw", [128, NKC, NW], F32)
                h2 = sb(p4, "h2", [128, NKC, NW], BF16)
                actb = sb(p4, "actb", [128, NJ, NW], BF16)
                rs2 = sb(p4, "rs2", [128, NW], F32)
                tf2 = [sb(p4, "tf2_%d" % i, [128, NW], F32) for i in range(2)]
                tu = [sb(p4, "tu%d" % i, [128, NW], F32) for i in range(2)]
                tg = [sb(p4, "tg%d" % i, [128, NW], F32) for i in range(2)]
                tm = [sb(p4, "tm%d" % i, [128, NW], F32) for i in range(3)]
                tmc = [0]
                xc = [sb(p4, "xc%d" % i, [128, NW], F32) for i in range(2)]
                pu = [ps(p4, "pu%d" % i) for i in range(2)]
                pg = [ps(p4, "pg%d" % i) for i in range(2)]
                pd = [ps(p4, "pd%d" % i) for i in range(3)]
                pr = ps(p4, "pr")
                fblocks = []
                for s0 in range(0, L, TB):
                    fblocks.append((0, L, s0, min(TB, L - s0)))
                if with_ctx:
                    fblocks.append((L, LC, 0, LC))
                for bi, (base, slen, s0, tn) in enumerate(fblocks):
                    tcol = 0 if base < L else 1
                    nw = tn + 2
                    lo = s0 - 1
                    hi = s0 + tn + 1
                    c_lo = max(lo, 0)
                    c_hi = min(hi, slen)
                    if c_lo > lo:
                        op(pool, lambda: nc.gpsimd.memset(xw.t[:, :, 0:1], 1.0), wr=[xw])
                    if c_hi < hi:
                        op(pool, lambda: nc.gpsimd.memset(xw.t[:, :, nw - 1:nw], 1.0), wr=[xw])
                    dma_load(k, sp, xw, xw.t[:, :, c_lo - lo:c_hi - lo],
                             xs_a[:, base + c_lo:base + c_hi].rearrange("(c p) t -> p c t", p=128))
                    op(act, lambda: nc.scalar.activation(out=actb.t[:, 0:NKC, 0:nw], in_=xw.t[:, :, 0:nw], func=AF.Square),
                       rd=[xw], wr=[actb])

                    def mm_ss2():
                        ins = None
                        for c in range(NKC):
                            ins = nc.tensor.matmul(pr.t[:, 0:nw], lhsT=ones_bf.t[:], rhs=actb.t[:, c, 0:nw],
                                                   start=(c == 0), stop=(c == NKC - 1))
                        return ins
                    op(pe, mm_ss2, rd=[ones_bf, actb], wr=[pr])
                    op(act, lambda: nc.scalar.activation(out=rs2.t[:, 0:nw], in_=pr.t[:, 0:nw], func=AF.Ln,
                                                         scale=1.0 / D, bias=EPS), rd=[pr], wr=[rs2])
                    op(act, lambda: nc.scalar.activation(out=rs2.t[:, 0:nw], in_=rs2.t[:, 0:nw], func=AF.Exp, scale=-0.5),
                       rd=[rs2], wr=[rs2])
                    for c in range(NKC):
                        tf = tf2[c % 2]
                        op(dve, lambda c=c, tf=tf: nc.vector.scalar_tensor_tensor(
                            out=tf.t[:, 0:nw], in0=xw.t[:, c, 0:nw], scalar=gm[l].t[:, 1, c, tcol:tcol + 1],
                            in1=rs2.t[:, 0:nw], op0=ALU.mult, op1=ALU.mult), rd=[xw, gm[l], rs2], wr=[tf])
                        op(act, lambda c=c, tf=tf: nc.scalar.activation(
                            out=h2.t[:, c, 0:nw], in_=tf.t[:, 0:nw], func=AF.Identity,
                            bias=mods[l].t[:, 24 + c, tcol:tcol + 1], scale=1.0), rd=[tf, mods[l]], wr=[h2])
                    if c_lo > lo:
                        op(pool, lambda: nc.gpsimd.memset(h2.t[:, :, 0:1], 0.0), wr=[h2])
                    if c_hi < hi:
                        op(pool, lambda: nc.gpsimd.memset(h2.t[:, :, nw - 1:nw], 0.0), wr=[h2])
                    for j in range(NJ):
                        i2 = j % 2
                        pub, pgb, tub, tgb = pu[i2], pg[i2], tu[i2], tg[i2]

                        def mm_up(col, dst):
                            def f():
                                ins = None
                                for kc in range(NKC):
                                    ins = nc.tensor.matmul(dst.t[:, 0:nw], lhsT=Wu.t[:, kc, col * 128:(col + 1) * 128],
                                                           rhs=h2.t[:, kc, 0:nw], start=(kc == 0), stop=(kc == NKC - 1))
                                return ins
                            return f
                        op(pe, mm_up(j, pub), rd=[Wu, h2], wr=[pub])
                        op(pe, mm_up(NJ + j, pgb), rd=[Wu, h2], wr=[pgb])
                        for ei, (col, pp, tt_) in enumerate(((j, pub, tub), (NJ + j, pgb, tgb))):
                            cw0 = PT_CW + 3 * col
                            cb0 = PT_CB + col
                            op(act, lambda pp=pp, tt_=tt_, cw0=cw0, cb0=cb0: nc.scalar.activation(
                                out=tt_.t[:, 0:tn], in_=pp.t[:, 0:tn], func=AF.Identity,
                                scale=ptb[l].t[:, cw0:cw0 + 1], bias=ptb[l].t[:, cb0:cb0 + 1]), rd=[pp, ptb[l]], wr=[tt_])
                            for tap in (1, 2):
                                tmb = tm[tmc[0] % 3]
                                tmc[0] += 1
                                op(act, lambda pp=pp, tmb=tmb, cw0=cw0, tap=tap: nc.scalar.activation(
                                    out=tmb.t[:, 0:tn], in_=pp.t[:, tap:tap + tn], func=AF.Copy,
                                    scale=ptb[l].t[:, cw0 + tap:cw0 + tap + 1]), rd=[pp, ptb[l]], wr=[tmb])
                                if ei == 0:
                                    op(dve, lambda tt_=tt_, tmb=tmb: nc.vector.tensor_tensor(
                                        out=tt_.t[:, 0:tn], in0=tt_.t[:, 0:tn], in1=tmb.t[:, 0:tn], op=ALU.add),
                                       rd=[tmb, tt_], wr=[tt_])
                                else:
                                    op(pool, lambda tt_=tt_, tmb=tmb: nc.gpsimd.tensor_tensor(
                                        out=tt_.t[:, 0:tn], in0=tt_.t[:, 0:tn], in1=tmb.t[:, 0:tn], op=ALU.add),
                                       rd=[tmb, tt_], wr=[tt_])
                        op(act, lambda: nc.scalar.activation(out=tgb.t[:, 0:tn], in_=tgb.t[:, 0:tn], func=AF.Silu),
                           rd=[tgb], wr=[tgb])
                        op(dve, lambda j=j: nc.vector.tensor_tensor(out=actb.t[:, j, 0:tn], in0=tub.t[:, 0:tn],
                                                                    in1=tgb.t[:, 0:tn], op=ALU.mult),
                           rd=[tub, tgb], wr=[actb])
                    for oc in range(NKC):
                        pdb = pd[oc % 3]
                        xcb = xc[oc % 2]
                        dma_load(k, sp, xcb, xcb.t[:, 0:tn], xs_a[oc * 128:(oc + 1) * 128, base + s0:base + s0 + tn])

                        def mm_dn(oc=oc, pdb=pdb):
                            ins = None
                            for j in range(NJ):
                                ins = nc.tensor.matmul(pdb.t[:, 0:tn], lhsT=Wd.t[:, j, oc * 128:(oc + 1) * 128],
                                                       rhs=actb.t[:, j, 0:tn], start=(j == 0), stop=(j == NJ - 1))
                            return ins
                        op(pe, mm_dn, rd=[Wd, actb], wr=[pdb])
                        tmb = tm[tmc[0] % 3]
                        tmc[0] += 1
                        op(act, lambda oc=oc, pdb=pdb, tmb=tmb: nc.scalar.activation(
                            out=tmb.t[:, 0:tn], in_=pdb.t[:, 0:tn], func=AF.Copy,
                            scale=mods[l].t[:, 40 + oc, tcol:tcol + 1]), rd=[pdb, mods[l]], wr=[tmb])
                        op(dve, lambda xcb=xcb, tmb=tmb: nc.vector.tensor_tensor(
                            out=xcb.t[:, 0:tn], in0=xcb.t[:, 0:tn], in1=tmb.t[:, 0:tn], op=ALU.add), rd=[tmb, xcb], wr=[xcb])
                        if last:
                            dma_store(k, sp, yTh[oc // 4][(oc % 4) * 128:(oc % 4 + 1) * 128, s0:s0 + tn], xcb, xcb.t[:, 0:tn])
                        else:
                            dma_store(k, sp, xs_b[oc * 128:(oc + 1) * 128, base + s0:base + s0 + tn], xcb, xcb.t[:, 0:tn])
                barrier(k)
        barrier(k)
    return nc


def _rope_tables():
    def ang(dim):
        half = dim // 2
        inv = (10000.0 ** (-np.arange(0, half, 2, dtype=np.float32) / half)).astype(np.float32)
        t = np.arange(L)
        row = (t // 64).astype(np.float32)
        col = (t % 64).astype(np.float32)
        return np.concatenate([row[:, None] * inv, col[:, None] * inv], axis=-1).astype(np.float32)

    out = []
    for dim in (64, 32):
        a = ang(dim)
        cos = np.cos(a).astype(np.float32)
        sin = np.sin(a).astype(np.float32)
        d = np.arange(dim)
        c = cos[:, d // 2].T
        s = sin[:, d // 2].T * np.where(d % 2 == 0, -1.0, 1.0)[:, None]
        tab = np.zeros((2, 128, T), np.float32)
        rep = 128 // dim
        tab[0, :, :L] = np.tile(c, (rep, 1))
        tab[1, :, :L] = np.tile(s, (rep, 1))
        tab[0, :, L:] = 1.0
        out.append(tab)
    return out


def _bias_patterns():
    classes = [2, 0, 1, 30, 31]
    valid = np.zeros((25, 128, 128), bool)
    ri = np.zeros((25, 128, 128), np.int64)
    ci = np.zeros((25, 128, 128), np.int64)
    kp = np.arange(128)[:, None]
    qf = np.arange(128)[None, :]
    for cidx, i in enumerate(classes):
        for s in range(5):
            kt = min(max(i - 2, 0), 27) + s
            qr = 2 * i + qf // 64
            qc = qf % 64
            kr = 2 * kt + kp // 64
            kc = kp % 64
            rs_ = np.clip(qr - 4, 0, 56)
            cs_ = np.clip(qc - 8, 0, 48)
            v = (kr >= rs_) & (kr < rs_ + 8) & (kc >= cs_) & (kc < cs_ + 16)
            p = cidx * 5 + s
            valid[p] = v
            ri[p] = np.where(v, kr - qr + 7, 0)
            ci[p] = np.where(v, kc - qc + 15, 0)
    return valid, ri, ci


def _prep_shared(inp):
    f = lambda a: np.ascontiguousarray(np.asarray(a, dtype=np.float32))
    w_in = f(inp["w_in"])
    sw = np.arange(64) ^ 1
    cols = []
    qa = np.arange(0, 512)
    cols += [qa]
    cols += [(qa // 64) * 64 + sw[qa % 64]]
    ka = [512 + 64 * g + np.arange(64) for g in range(2)]
    cols += [np.concatenate([ka[0], ka[0]]), np.concatenate([ka[1], ka[1]])]
    cols += [np.concatenate([ka[0][sw], ka[0][sw]]), np.concatenate([ka[1][sw], ka[1][sw]])]
    cols += [768 + np.arange(256)]
    cols += [1024 + np.arange(256)]
    qc = 1536 + np.arange(256)
    kc = 1792 + np.arange(256)
    cols += [qc, qc ^ 1, kc, kc ^ 1]
    cols += [640 + np.arange(128), 1280 + np.arange(256), 2048 + np.arange(256)]
    cols = np.concatenate(cols)
    assert cols.shape[0] == W1C
    w1 = np.ascontiguousarray(w_in[:, :, cols])

    ptab = np.zeros((DEPTH, 128, NPT), np.float32)
    p = np.arange(128)
    for l in range(DEPTH):
        ptab[l, :, PT_BADA:PT_BADA + 48] = f(inp["b_ada"])[l].reshape(48, 128).T
        ptab[l, :, PT_G1:PT_G1 + 8] = f(inp["g_norm1"])[l].reshape(8, 128).T
        ptab[l, :, PT_G2:PT_G2 + 8] = f(inp["g_norm2"])[l].reshape(8, 128).T
        gqa, gka = f(inp["gq_a"])[l], f(inp["gk_a"])[l]
        gqb, gkb = f(inp["gq_b"])[l], f(inp["gk_b"])[l]
        gqc, gkc = f(inp["gq_c"])[l], f(inp["gk_c"])[l]
        g = np.zeros((128, NSLAB), np.float32)
        for s in range(4):
            g[:, s] = gqa[p % 64]
            g[:, 4 + s] = gqa[(p % 64) ^ 1]
        for s in range(2):
            g[:, 8 + s] = gka[p % 64]
            g[:, 10 + s] = gka[(p % 64) ^ 1]
            g[:, 12 + s] = gqb[p % 64]
            g[:, 14 + s] = gkb[p % 64]
            g[:, 16 + s] = gqc[p % 32]
            g[:, 18 + s] = gqc[(p % 32) ^ 1]
            g[:, 20 + s] = gkc[p % 32]
            g[:, 22 + s] = gkc[(p % 32) ^ 1]
        ptab[l, :, PT_GTAB:PT_GTAB + NSLAB] = g
        cw = f(inp["conv_w"])[l]
        ptab[l, :, PT_CW:PT_CW + 132] = cw.reshape(3, 44, 128).transpose(2, 1, 0).reshape(128, 132)
        ptab[l, :, PT_CB:PT_CB + 44] = f(inp["conv_b"])[l].reshape(44, 128).T
        ptab[l, :, PT_GSUB] = f(inp["g_subln"])[l][p % 64]
        for i, nm in enumerate(("lambda_q1", "lambda_k1", "lambda_q2", "lambda_k2")):
            ptab[l, :, PT_LAM + 32 * i:PT_LAM + 32 * i + 32] = f(inp[nm])[l][None, :]

    ropeA, ropeC = _rope_tables()
    valid, ri, ci = _bias_patterns()
    rpb = f(inp["rpb_b"])
    bias = np.where(valid[None, None], rpb[:, :, ri, ci], np.float32(NEG)).astype(np.float32)
    biasB = np.ascontiguousarray(bias.transpose(0, 1, 3, 2, 4))
    consts = np.zeros((3, 128, 128), np.float32)
    consts[0] = np.eye(128, dtype=np.float32)
    consts[1] = np.kron(np.eye(2, dtype=np.float32), np.ones((64, 64), np.float32))
    consts[2] = np.kron(np.eye(4, dtype=np.float32), np.ones((32, 32), np.float32))
    return dict(consts=consts, w_ada=f(inp["w_ada"]), ptab=ptab, w1=w1, ropeA=ropeA, ropeC=ropeC, biasB=biasB,
                w_out=f(inp["w_out"]), w_up=f(inp["w_up"]), w_down=f(inp["w_down"]))


def _prep_core(inp, b):
    x = np.asarray(inp["x"][b], np.float32)
    ctx = np.asarray(inp["ctx"][b], np.float32)
    xT = np.ascontiguousarray(np.concatenate([x.T, ctx.T], axis=1))
    cv = np.stack([np.asarray(inp["c"][b], np.float32), np.asarray(inp["c_ctx"], np.float32)], axis=-1)
    cvec = np.ascontiguousarray(cv.reshape(8, 128, 2).transpose(1, 0, 2))
    return dict(xT=xT, cvec=cvec)


_NC_CACHE = {}


def kernel(**inputs):
    if "nc" not in _NC_CACHE:
        _NC_CACHE["nc"] = build_nc()
    nc = _NC_CACHE["nc"]
    shared = _prep_shared(inputs)
    in_maps = []
    for b in range(8):
        m = dict(shared)
        m.update(_prep_core(inputs, b))
        in_maps.append(m)
    outs = []
    for g0 in range(0, 8, GROUP):
        res = run_bass_kernel_spmd(nc, in_maps[g0:g0 + GROUP], core_ids=list(range(GROUP)))
        outs.extend(np.ascontiguousarray(np.concatenate([res.results[b]["yT0"], res.results[b]["yT1"]], axis=0).T)
                    for b in range(GROUP))
    return np.stack(outs, axis=0).astype(np.float32)
```

```python
import math
import os
CUT = int(os.environ.get('KCUT', '99'))
from contextlib import ExitStack

import numpy as np
import concourse.bass as bass
import concourse.mybir as mybir
from concourse.bass_utils import run_bass_kernel_spmd

F32 = mybir.dt.float32
BF16 = mybir.dt.bfloat16
ALU = mybir.AluOpType
AF = mybir.ActivationFunctionType
AX = mybir.AxisListType

D = 1024
L = 4096
LC = 256
T = L + LC
DEPTH = 2
NKC = 8
FFN = 2816
NJ = FFN // 128
EPS = 1e-6
NEG = -30000.0
NSLAB = 24
W1C = NSLAB * 128 + 640
GROUP = 8
TB = 410

PT_BADA = 0
PT_G1 = 48
PT_G2 = 56
PT_GTAB = 64
PT_CW = 88
PT_CB = 220
PT_GSUB = 264
PT_LAM = 265
NPT = 393


class Eng:
    def __init__(self, nc, es, eng, name, skip_own=False):
        self.e = eng
        self.sem = es.enter_context(nc.semaphore(name))
        self.cnt = 0
        self.seen = {}
        self.skip_own = skip_own

    def wait(self, toks):
        best = {}
        for t in toks:
            if t is None:
                continue
            s, v = t
            if self.skip_own and s is self.sem:
                continue
            k = id(s)
            if self.seen.get(k, 0) >= v:
                continue
            if k not in best or best[k][1] < v:
                best[k] = (s, v)
        for k, (s, v) in best.items():
            self.e.wait_ge(s, v)
            self.seen[k] = v

    def sig(self, ins):
        self.cnt += 1
        ins.then_inc(self.sem, 1)
        return (self.sem, self.cnt)


class Buf:
    def __init__(self, t):
        self.t = t
        self.wr = None
        self.rd = {}
        self.dsem = None
        self.dcnt = 0
        self.excl = False

    def add_rd(self, tok):
        k = id(tok[0])
        if k not in self.rd or self.rd[k][1] < tok[1]:
            self.rd[k] = tok


class K:
    pass


def op(E, fn, rd=(), wr=()):
    toks = [b.wr for b in rd]
    for b in rd:
        if b.excl:
            toks.extend(b.rd.values())
    for b in wr:
        toks.append(b.wr)
        toks.extend(b.rd.values())
    E.wait(toks)
    tok = E.sig(fn())
    for b in rd:
        b.add_rd(tok)
    for b in wr:
        b.wr = tok
        b.rd = {}
    return tok


def _dsem(k, b):
    if b.dsem is None:
        k.nsem += 1
        b.dsem = k.es.enter_context(k.nc.semaphore("d%d" % k.nsem))
    return b.dsem


def dma_load(k, Q, b, dst_ap, src_ap, part=False):
    if not part:
        toks = [b.wr] + list(b.rd.values())
        Q.wait(toks)
    s = _dsem(k, b)
    ins = Q.e.dma_start(out=dst_ap, in_=src_ap)
    b.dcnt += 16
    ins.then_inc(s, 16)
    b.wr = (s, b.dcnt)
    b.rd = {}


def dma_store(k, Q, dst_ap, b, src_ap):
    Q.wait([b.wr])
    s = _dsem(k, b)
    ins = Q.e.dma_start(out=dst_ap, in_=src_ap)
    b.dcnt += 16
    ins.then_inc(s, 16)
    tok = (s, b.dcnt)
    b.add_rd(tok)
    k.stores[id(s)] = tok


def barrier(k):
    toks = [(E.sem, E.cnt) for E in k.engs if E.cnt > 0] + list(k.stores.values())
    for E in k.engs + [k.sp]:
        E.wait(toks)
    k.stores = {}


def build_nc(dbg=False, nlayers=DEPTH, stop_after=None):
    nc = bass.Bass("TRN2", target_bir_lowering=False)
    k = K()
    k.nc = nc
    k.nsem = 0
    k.stores = {}

    def din(name, shape, dt=F32):
        return nc.dram_tensor(name, list(shape), dt, kind="ExternalInput").ap()

    xT = din("xT", [D, T])
    cvec = din("cvec", [128, NKC, 2])
    w_ada = din("w_ada", [DEPTH, D, 6 * D])
    ptab = din("ptab", [DEPTH, 128, NPT])
    w1 = din("w1", [DEPTH, D, W1C])
    ropeA = din("ropeA", [2, 128, T])
    ropeC = din("ropeC", [2, 128, T])
    biasB = din("biasB", [DEPTH, 4, 128, 25, 128])
    w_out = din("w_out", [DEPTH, D, D])
    w_up = din("w_up", [DEPTH, D, 2 * FFN])
    w_down = din("w_down", [DEPTH, FFN, D])
    consts = din("consts", [3, 128, 128])
    yTh = [nc.dram_tensor("yT%d" % i, [D // 2, L], F32, kind="ExternalOutput").ap() for i in range(2)]

    skind = "ExternalOutput" if dbg else "Internal"

    def dscr(name, shape, dt):
        return nc.dram_tensor(name, list(shape), dt, kind=skind).ap()

    xs_a = dscr("xs_a", [D, T], F32)
    xs_b = dscr("xs_b", [D, T], F32)
    qs = dscr("qs", [8, 128, T], BF16)
    ks = dscr("ks", [16, 128, T], BF16)
    vs = dscr("vs", [10, T, 128], BF16)
    attn = dscr("attn", [D, T], BF16)
    modd = dscr("modd", [DEPTH, 128, 96], F32) if dbg else None

    with ExitStack() as es:
        k.es = es
        pe = Eng(nc, es, nc.tensor, "s_pe", skip_own=True)
        act = Eng(nc, es, nc.scalar, "s_act")
        dve = Eng(nc, es, nc.vector, "s_dve")
        pool = Eng(nc, es, nc.gpsimd, "s_pool")
        sp = Eng(nc, es, nc.sync, "s_sp")
        k.engs = [pe, act, dve, pool]
        k.sp = sp

        uniq = [0]

        def sb(st, name, shape, dt):
            uniq[0] += 1
            return Buf(st.enter_context(nc.sbuf_tensor("%s_u%d" % (name, uniq[0]), list(shape), dt)))

        def ps(st, name, shape=(128, 512), dt=F32):
            uniq[0] += 1
            b = Buf(st.enter_context(nc.psum_tensor("%s_u%d" % (name, uniq[0]), list(shape), dt)))
            b.excl = True
            return b

        ones_bf = sb(es, "ones_bf", [128, 128], BF16)
        blk64 = sb(es, "blk64", [128, 128], BF16)
        blk32 = sb(es, "blk32", [128, 128], BF16)
        ident = sb(es, "ident", [128, 128], F32)
        ptb = [sb(es, "ptb%d" % l, [128, NPT], F32) for l in range(DEPTH)]
        mods = [sb(es, "mods%d" % l, [128, 48, 2], F32) for l in range(DEPTH)]
        gm = [sb(es, "gm%d" % l, [128, 2, NKC, 2], F32) for l in range(DEPTH)]
        lamt = [sb(es, "lamt%d" % l, [128, 4], F32) for l in range(DEPTH)]

        op(pool, lambda: nc.gpsimd.memset(ones_bf.t[:], 1.0), wr=[ones_bf])
        zbf = sb(es, "zbf", [128, 128], BF16)
        op(pool, lambda: nc.gpsimd.memset(zbf.t[:], 0.0), wr=[zbf])
        dma_load(k, sp, ident, ident.t[:], consts[0])
        dma_load(k, pool, blk64, blk64.t[:], consts[1])
        dma_load(k, pool, blk32, blk32.t[:], consts[2])
        for l in range(DEPTH):
            dma_load(k, sp, ptb[l], ptb[l].t[:], ptab[l])

        with ExitStack() as p0:
            csb = sb(p0, "csb", [128, NKC, 2], F32)
            scs = sb(p0, "scs", [128, NKC, 2], F32)
            wa = [sb(p0, "wa%d" % i, [128, NKC, 768], F32) for i in range(2)]
            modp = ps(p0, "modp", [128, 512])
            tmpl = sb(p0, "tmpl", [128, 32], F32)
            zer_bf = sb(p0, "zer_bf", [128, T], BF16)
            op(pool, lambda: nc.gpsimd.memset(zer_bf.t[:], 0.0), wr=[zer_bf])
            for v in range(16):
                dma_store(k, sp, ks[v], zer_bf, zer_bf.t[:])
            dma_load(k, sp, csb, csb.t[:], cvec)
            op(act, lambda: nc.scalar.activation(out=scs.t[:], in_=csb.t[:], func=AF.Silu), rd=[csb], wr=[scs])
            for l in range(nlayers):
                for pc in range(8):
                    wb = wa[pc % 2]
                    dma_load(k, sp, wb, wb.t[:],
                             w_ada[l, :, pc * 768:(pc + 1) * 768].rearrange("(k p) c -> p k c", p=128))

                    def mm(wb=wb, pc=pc):
                        ins = None
                        for jl in range(6):
                            j = pc * 6 + jl
                            for kc in range(NKC):
                                ins = nc.tensor.matmul(modp.t[:, 2 * j:2 * j + 2], lhsT=wb.t[:, kc, jl * 128:(jl + 1) * 128],
                                                       rhs=scs.t[:, kc, :], start=(kc == 0), stop=(kc == NKC - 1))
                        return ins
                    op(pe, mm, rd=[wb, scs], wr=[modp])
                mv = modp.t[:, 0:96].rearrange("p (j t) -> p j t", t=2)
                for t in range(2):
                    op(dve, lambda t=t: nc.vector.tensor_tensor(out=mods[l].t[:, :, t], in0=mv[:, :, t],
                                                                in1=ptb[l].t[:, PT_BADA:PT_BADA + 48], op=ALU.add),
                       rd=[modp, ptb[l]], wr=[mods[l]])
                for n, (sc0, g0) in enumerate(((8, PT_G1), (32, PT_G2))):
                    for t in range(2):
                        op(dve, lambda n=n, sc0=sc0, g0=g0, t=t: nc.vector.scalar_tensor_tensor(
                            out=gm[l].t[:, n, :, t], in0=mods[l].t[:, sc0:sc0 + 8, t], scalar=1.0,
                            in1=ptb[l].t[:, g0:g0 + 8], op0=ALU.add, op1=ALU.mult),
                           rd=[mods[l], ptb[l]], wr=[gm[l]])
                lam_init = 0.8 - 0.6 * math.exp(-0.3 * l)
                for i in range(2):
                    a0 = PT_LAM + 64 * i
                    op(dve, lambda a0=a0: nc.vector.tensor_tensor(out=tmpl.t[:], in0=ptb[l].t[:, a0:a0 + 32],
                                                                  in1=ptb[l].t[:, a0 + 32:a0 + 64], op=ALU.mult),
                       rd=[ptb[l]], wr=[tmpl])
                    op(dve, lambda i=i: nc.vector.reduce_sum(out=lamt[l].t[:, 2 + i:3 + i], in_=tmpl.t[:], axis=AX.X),
                       rd=[tmpl], wr=[lamt[l]])
                op(act, lambda: nc.scalar.activation(out=lamt[l].t[:, 2:4], in_=lamt[l].t[:, 2:4], func=AF.Exp),
                   rd=[lamt[l]], wr=[lamt[l]])
                op(dve, lambda: nc.vector.scalar_tensor_tensor(out=lamt[l].t[:, 0:1], in0=lamt[l].t[:, 3:4], scalar=-lam_init,
                                                               in1=lamt[l].t[:, 2:3], op0=ALU.add, op1=ALU.subtract),
                   rd=[lamt[l]], wr=[lamt[l]])
                op(dve, lambda: nc.vector.tensor_scalar(out=lamt[l].t[:, 1:2], in0=ptb[l].t[:, PT_GSUB:PT_GSUB + 1],
                                                        scalar1=1.0 - lam_init, scalar2=None, op0=ALU.mult),
                   rd=[ptb[l]], wr=[lamt[l]])
                if dbg:
                    dma_store(k, sp, modd[l].rearrange("p (j t) -> p j t", t=2), mods[l], mods[l].t[:])
            barrier(k)

        blocks = [(512 * i, 512) for i in range(8)] + [(L, LC)]

        for l in range(nlayers if stop_after != 'p0' else 0):
            last = (l == DEPTH - 1)
            with_ctx = not last
            src = xT if l == 0 else xs_b

            with ExitStack() as p1:
                W1 = sb(p1, "W1", [128, NKC, W1C], BF16)
                for c0 in range(0, W1C, 1856):
                    dma_load(k, pool, W1, W1.t[:, :, c0:c0 + 1856],
                             w1[l, :, c0:c0 + 1856].rearrange("(k p) c -> p k c", p=128), part=(c0 > 0))
                xin = [sb(p1, "xin%d" % i, [128, NKC, 512], F32) for i in range(2)]
                sq = sb(p1, "sq", [128, NKC, 512], BF16)
                rs = sb(p1, "rs", [128, 512], F32)
                tmpf = [sb(p1, "tmpf%d" % i, [128, 512], F32) for i in range(2)]
                hT = [sb(p1, "hT%d" % i, [128, NKC, 512], BF16) for i in range(2)]
                rop = [[sb(p1, "rop%d_%d" % (i, j), [128, 512], F32) for j in range(4)] for i in range(2)]
                sqq = [sb(p1, "sqq%d" % i, [128, 512], BF16) for i in range(2)]
                rq = [sb(p1, "rq%d" % i, [128, 512], F32) for i in range(2)]
                t1 = [sb(p1, "t1_%d" % i, [128, 512], F32) for i in range(2)]
                t2 = [sb(p1, "t2_%d" % i, [128, 512], F32) for i in range(2)]
                outb = [sb(p1, "outb%d" % i, [128, 512], BF16) for i in range(3)]
                vout = [sb(p1, "vout%d" % i, [128, 10, 128], BF16) for i in range(2)]
                pss = ps(p1, "pss")
                pv0 = ps(p1, "pv0")
                pm = [ps(p1, "pm%d" % i) for i in range(2)]
                pw = [ps(p1, "pw%d" % i) for i in range(2)]
                pq = [ps(p1, "pq%d" % i) for i in range(2)]
                for i in range(2):
                    op(pool, lambda i=i: nc.gpsimd.memset(vout[i].t[:], 1.0), wr=[vout[i]])

                jobs1 = []
                for s in range(4):
                    jobs1.append((s, 4 + s, 'A', 64, True, [(qs[s], 0, 128)]))
                for g in range(2):
                    jobs1.append((8 + g, 10 + g, 'A', 64, False, [(ks[2 * g], 0, 64), (ks[2 * g + 1], 64, 128)]))
                for s in range(2):
                    jobs1.append((12 + s, None, None, 64, True, [(qs[4 + s], 0, 128)]))
                for s in range(2):
                    jobs1.append((14 + s, None, None, 64, False, [(ks[4 + 2 * s], 0, 64), (ks[5 + 2 * s], 64, 128)]))
                for s in range(2):
                    jobs1.append((16 + s, 18 + s, 'C', 32, True, [(qs[6 + s], 0, 128)]))
                for s in range(2):
                    jobs1.append((20 + s, 22 + s, 'C', 32, False,
                                  [(ks[8 + 4 * s + j], 32 * j, 32 * j + 32) for j in range(4)]))

                cnt1 = 0
                cnto = 0
                for bi, (t0, n) in enumerate(blocks if CUT > 1 else []):
                    if CUT < 10 and bi > 0:
                        break
                    tcol = 0 if t0 < L else 1
                    xb = xin[bi % 2]
                    hb = hT[bi % 2]
                    rp = rop[bi % 2]
                    dma_load(k, sp, xb, xb.t[:, :, 0:n], src[:, t0:t0 + n].rearrange("(c p) t -> p c t", p=128))
                    for j, (tab, idx) in enumerate(((ropeA, 0), (ropeA, 1), (ropeC, 0), (ropeC, 1))):
                        dma_load(k, sp, rp[j], rp[j].t[:, 0:n], tab[idx, :, t0:t0 + n])
                    op(act, lambda: nc.scalar.activation(out=sq.t[:, :, 0:n], in_=xb.t[:, :, 0:n], func=AF.Square),
                       rd=[xb], wr=[sq])

                    def mm_ss():
                        ins = None
                        for c in range(NKC):
                            ins = nc.tensor.matmul(pss.t[:, 0:n], lhsT=ones_bf.t[:], rhs=sq.t[:, c, 0:n],
                                                   start=(c == 0), stop=(c == NKC - 1))
                        return ins
                    op(pe, mm_ss, rd=[ones_bf, sq], wr=[pss])
                    op(act, lambda: nc.scalar.activation(out=rs.t[:, 0:n], in_=pss.t[:, 0:n], func=AF.Ln,
                                                         scale=1.0 / D, bias=EPS), rd=[pss], wr=[rs])
                    op(act, lambda: nc.scalar.activation(out=rs.t[:, 0:n], in_=rs.t[:, 0:n], func=AF.Exp, scale=-0.5),
                       rd=[rs], wr=[rs])
                    for c in range(NKC):
                        tf = tmpf[c % 2]
                        op(dve, lambda c=c, tf=tf: nc.vector.scalar_tensor_tensor(
                            out=tf.t[:, 0:n], in0=xb.t[:, c, 0:n], scalar=gm[l].t[:, 0, c, tcol:tcol + 1],
                            in1=rs.t[:, 0:n], op0=ALU.mult, op1=ALU.mult), rd=[xb, gm[l], rs], wr=[tf])
                        op(act, lambda c=c, tf=tf: nc.scalar.activation(
                            out=hb.t[:, c, 0:n], in_=tf.t[:, 0:n], func=AF.Identity,
                            bias=mods[l].t[:, c, tcol:tcol + 1], scale=1.0), rd=[tf, mods[l]], wr=[hb])

                    for tt in range(n // 128 if CUT > 2 else 0):
                        vo = vout[cnt1 % 2]
                        cnt1 += 1

                        def mm_v(tt=tt):
                            ins = None
                            for kc in range(NKC):
                                ins = nc.tensor.matmul(pv0.t[:, 0:512], lhsT=hb.t[:, kc, tt * 128:(tt + 1) * 128],
                                                       rhs=W1.t[:, kc, 3072:3584], start=(kc == 0), stop=(kc == NKC - 1))
                            for kc in range(NKC):
                                ins = nc.tensor.matmul(pss.t[:, 0:128], lhsT=hb.t[:, kc, tt * 128:(tt + 1) * 128],
                                                       rhs=W1.t[:, kc, 3584:3712], start=(kc == 0), stop=(kc == NKC - 1))
                            return ins
                        op(pe, mm_v, rd=[hb, W1], wr=[pv0, pss])
                        op(dve, lambda vo=vo: nc.vector.tensor_copy(
                            out=vo.t[:, 0:8, 0:64], in_=pv0.t[:, 0:512].rearrange("p (v c) -> p v c", c=64)),
                           rd=[pv0], wr=[vo])
                        op(dve, lambda vo=vo: nc.vector.tensor_copy(
                            out=vo.t[:, 8:10, 0:64], in_=pss.t[:, 0:128].rearrange("p (v c) -> p v c", c=64)),
                           rd=[pss], wr=[vo])
                        tk = t0 + tt * 128
                        dma_store(k, sp, vs[:, tk:tk + 128, :].rearrange("v p c -> p v c"), vo, vo.t[:])

                    for (sm, sw, rope, dd, is_q, dests) in (jobs1 if CUT > 3 else []):
                        i2 = cnt1 % 2
                        cnt1 += 1
                        pmb, pwb, pqb = pm[i2], pw[i2], pq[i2]
                        sqb, rqb, t1b, t2b = sqq[i2], rq[i2], t1[i2], t2[i2]

                        def mm_slab(slab, dst):
                            def f():
                                ins = None
                                for kc in range(NKC):
                                    ins = nc.tensor.matmul(dst.t[:, 0:n], lhsT=W1.t[:, kc, slab * 128:(slab + 1) * 128],
                                                           rhs=hb.t[:, kc, 0:n], start=(kc == 0), stop=(kc == NKC - 1))
                                return ins
                            return f
                        op(pe, mm_slab(sm, pmb), rd=[W1, hb], wr=[pmb])
                        if sw is not None:
                            op(pe, mm_slab(sw, pwb), rd=[W1, hb], wr=[pwb])
                        op(act, lambda: nc.scalar.activation(out=sqb.t[:, 0:n], in_=pmb.t[:, 0:n], func=AF.Square),
                           rd=[pmb], wr=[sqb])
                        blk = blk64 if dd == 64 else blk32
                        op(pe, lambda: nc.tensor.matmul(pqb.t[:, 0:n], lhsT=blk.t[:], rhs=sqb.t[:, 0:n], start=True, stop=True),
                           rd=[blk, sqb], wr=[pqb])
                        if is_q:
                            a, b = 1.0, dd * EPS
                        else:
                            a, b = 1.0 / dd, EPS
                        op(act, lambda: nc.scalar.activation(out=rqb.t[:, 0:n], in_=pqb.t[:, 0:n], func=AF.Ln, scale=a, bias=b),
                           rd=[pqb], wr=[rqb])
                        op(act, lambda: nc.scalar.activation(out=rqb.t[:, 0:n], in_=rqb.t[:, 0:n], func=AF.Exp, scale=-0.5),
                           rd=[rqb], wr=[rqb])
                        ob = outb[cnto % 3]
                        cnto += 1
                        gcol = PT_GTAB + sm
                        if CUT < 5:
                            continue
                        if os.environ.get('KSKIPB') and rope is None:
                            continue
                        if os.environ.get('KSKIPR') and rope is not None:
                            continue
                        if rope is not None:
                            cosb, sinb = (rp[0], rp[1]) if rope == 'A' else (rp[2], rp[3])
                            gcs = PT_GTAB + sw
                            op(dve, lambda: nc.vector.scalar_tensor_tensor(
                                out=t1b.t[:, 0:n], in0=cosb.t[:, 0:n], scalar=ptb[l].t[:, gcol:gcol + 1],
                                in1=pmb.t[:, 0:n], op0=ALU.mult, op1=ALU.mult), rd=[pmb, ptb[l], cosb], wr=[t1b])
                            op(dve, lambda: nc.vector.scalar_tensor_tensor(
                                out=t2b.t[:, 0:n], in0=sinb.t[:, 0:n], scalar=ptb[l].t[:, gcs:gcs + 1],
                                in1=pwb.t[:, 0:n], op0=ALU.mult, op1=ALU.mult), rd=[pwb, ptb[l], sinb], wr=[t2b])
                            if CUT < 6:
                                continue
                            op(pool, lambda: nc.gpsimd.tensor_tensor(out=t1b.t[:, 0:n], in0=t1b.t[:, 0:n], in1=t2b.t[:, 0:n],
                                                                     op=ALU.add), rd=[t1b, t2b], wr=[t1b])
                            op(pool, lambda: nc.gpsimd.tensor_tensor(out=ob.t[:, 0:n], in0=t1b.t[:, 0:n], in1=rqb.t[:, 0:n],
                                                                     op=ALU.mult), rd=[t1b, rqb], wr=[ob])
                        else:
                            op(dve, lambda: nc.vector.scalar_tensor_tensor(
                                out=ob.t[:, 0:n], in0=rqb.t[:, 0:n], scalar=ptb[l].t[:, gcol:gcol + 1],
                                in1=pmb.t[:, 0:n], op0=ALU.mult, op1=ALU.mult), rd=[pmb, ptb[l], rqb], wr=[ob])
                        for (dap, r0, r1) in (dests if CUT > 6 else []):
                            dma_store(k, sp, dap[r0:r1, t0:t0 + n], ob, ob.t[r0:r1, 0:n])
                barrier(k)
            if stop_after == (l, 1):
                break

            with ExitStack() as p2:
                NS, NP, LA = 3, 4, 2
                Kb = [[sb(p2, "Kb%d_%d" % (i, j), [128, T], BF16) for j in range(2)] for i in range(2)]
                Vb = [sb(p2, "Vb%d" % i, [128, 34, 128], BF16) for i in range(2)]
                Qb = [sb(p2, "Qb%d" % i, [128, T], BF16) for i in range(2)]
                Bb = [sb(p2, "Bb%d" % i, [128, 25, 128], F32) for i in range(2)]
                Pb = [sb(p2, "Pb%d" % i, [128, 512], BF16) for i in range(NP)]
                rb = [sb(p2, "rb%d" % i, [128, 512], F32) for i in range(2)]
                ab = [sb(p2, "ab%d" % i, [128, 512], F32) for i in range(2)]
                sqc = sb(p2, "sqc", [128, 512], BF16)
                rsc = sb(p2, "rsc", [128, 512], F32)
                osb = [sb(p2, "osb%d" % i, [128, 512], BF16) for i in range(3)]
                Sp = [ps(p2, "Sp%d" % i) for i in range(NS)]
                Op = [ps(p2, "Op%d" % i) for i in range(4)]
                Mp = ps(p2, "Mp")

                jobs = []
                for h in range(8):
                    jobs.append(dict(kind='A', kv=[2 * (h // 4) + (h % 2)], vh=h // 4, q=h // 2, row=64 * h))
                for h in range(4):
                    jobs.append(dict(kind='B', kv=[4 + h], vh=2 + h, q=4 + h // 2, row=512 + 64 * h, bh=h))
                for h in range(4):
                    jobs.append(dict(kind='C', kv=[8 + 2 * h, 9 + 2 * h], vh=6 + h, q=6 + h // 2, row=768 + 64 * h))

                if os.environ.get('KJOBS'):
                    jobs = [jb for jb in jobs if jb['kind'] in os.environ['KJOBS']]

                def load_job(ji):
                    jb = jobs[ji]
                    st = ji % 2
                    for m, kv in enumerate(jb['kv']):
                        dma_load(k, sp, Kb[st][m], Kb[st][m].t[:], ks[kv])
                    dma_load(k, sp, Qb[st], Qb[st].t[:], qs[jb['q']])
                    dma_load(k, sp, Vb[st], Vb[st].t[:], vs[jb['vh']].rearrange("(kt p) c -> p kt c", p=128))
                    if jb['kind'] == 'B':
                        dma_load(k, sp, Bb[st], Bb[st].t[:], biasB[l, jb['bh']])

                steps = []
                bc = 0
                ocnt = [0]
                for ji, jb in enumerate(jobs):
                    st = ji % 2
                    qblocks = [(512 * i, 512, False) for i in range(8)]
                    if with_ctx:
                        qblocks.append((L, LC, True))
                    nmap = len(jb['kv'])
                    first_of_job = True
                    for (t0, n, isctx) in qblocks:
                        Os = [Op[(nmap * bc + m) % 4] for m in range(nmap)]
                        bc += 1
                        blk_steps = []
                        Qt, Vt = Qb[st], Vb[st]
                        if jb['kind'] == 'B' and not isctx:
                            qb = t0 // 512
                            for s in range(5):
                                def S_fn(Sb, s=s, qb=qb, Kt=Kb[st][0], Qt=Qt, Bt=Bb[st]):
                                    ins = None
                                    for sbk in range(4):
                                        i = 4 * qb + sbk
                                        kt = min(max(i - 2, 0), 27) + s
                                        pat = s if 2 <= i <= 29 else {0: 5, 1: 10, 30: 15, 31: 20}[i] + s
                                        nc.tensor.matmul(Sb.t[:, sbk * 128:(sbk + 1) * 128], lhsT=Kt.t[:, kt * 128:(kt + 1) * 128],
                                                         rhs=Qt.t[:, i * 128:(i + 1) * 128], start=True, stop=False)
                                        ins = nc.tensor.matmul(Sb.t[:, sbk * 128:(sbk + 1) * 128], lhsT=ident.t[:],
                                                               rhs=Bt.t[:, pat, :], start=False, stop=True)
                                    return ins

                                def PV_fn(Pt, O, s=s, qb=qb, Vt=Vt, Qt=Qt, t0=t0):
                                    ins = None
                                    if s == 0:
                                        nc.tensor.matmul(O.t[:, 0:512], lhsT=zbf.t[:], rhs=Qt.t[:, t0:t0 + 512],
                                                         start=True, stop=False)
                                    for sbk in range(4):
                                        i = 4 * qb + sbk
                                        kt = min(max(i - 2, 0), 27) + s
                                        ins = nc.tensor.matmul(O.t[:, sbk * 128:(sbk + 1) * 128], lhsT=Vt.t[:, kt, :],
                                                               rhs=Pt.t[:, sbk * 128:(sbk + 1) * 128], start=False, stop=False)
                                    return ins
                                blk_steps.append(dict(S=S_fn, PV=PV_fn, O=Os[0], n=512, rdS=[Kb[st][0], Qt, Bb[st], ident],
                                                      rdV=[Vt, zbf, Qt]))
                            for j in range(2):
                                kt = 32 + j

                                def S_fn(Sb, kt=kt, Kt=Kb[st][0], Qt=Qt, t0=t0):
                                    return nc.tensor.matmul(Sb.t[:, 0:512], lhsT=Kt.t[:, kt * 128:(kt + 1) * 128],
                                                            rhs=Qt.t[:, t0:t0 + 512], start=True, stop=True)

                                def PV_fn(Pt, O, kt=kt, Vt=Vt, j=j):
                                    return nc.tensor.matmul(O.t[:, 0:512], lhsT=Vt.t[:, kt, :], rhs=Pt.t[:, 0:512],
                                                            start=False, stop=(j == 1))
                                blk_steps.append(dict(S=S_fn, PV=PV_fn, O=Os[0], n=512, rdS=[Kb[st][0], Qt], rdV=[Vt]))
                        else:
                            kts = [32, 33] if isctx else list(range(34))
                            for ki, kt in enumerate(kts):
                                for m in range(nmap):
                                    def S_fn(Sb, kt=kt, Kt=Kb[st][m], Qt=Qt, t0=t0, n=n):
                                        return nc.tensor.matmul(Sb.t[:, 0:n], lhsT=Kt.t[:, kt * 128:(kt + 1) * 128],
                                                                rhs=Qt.t[:, t0:t0 + n], start=True, stop=True)

                                    def PV_fn(Pt, O, kt=kt, Vt=Vt, n=n, ki=ki, nk=len(kts)):
                                        return nc.tensor.matmul(O.t[:, 0:n], lhsT=Vt.t[:, kt, :], rhs=Pt.t[:, 0:n],
                                                                start=(ki == 0), stop=(ki == nk - 1))
                                    blk_steps.append(dict(S=S_fn, PV=PV_fn, O=Os[m], n=n, rdS=[Kb[st][m], Qt], rdV=[Vt]))
                        if first_of_job:
                            blk_steps[0]['pre'] = ji
                            first_of_job = False
                        blk_steps[-1]['post'] = (jb, t0, n, Os)
                        steps.extend(blk_steps)

                def post_block(jb, t0, n, Os):
                    ob = osb[ocnt[0] % 3]
                    ocnt[0] += 1
                    if jb['kind'] != 'C':
                        O = Os[0]
                        r = rb[0]
                        op(dve, lambda: nc.vector.reciprocal(out=r.t[0:64, 0:n], in_=O.t[64:128, 0:n]), rd=[O], wr=[r])
                        op(dve, lambda: nc.vector.tensor_tensor(out=ob.t[0:64, 0:n], in0=O.t[0:64, 0:n], in1=r.t[0:64, 0:n],
                                                                op=ALU.mult), rd=[O, r], wr=[ob])
                    else:
                        O1, O2 = Os
                        op(dve, lambda: nc.vector.reciprocal(out=rb[0].t[0:64, 0:n], in_=O1.t[64:128, 0:n]), rd=[O1], wr=[rb[0]])
                        op(dve, lambda: nc.vector.reciprocal(out=rb[1].t[0:64, 0:n], in_=O2.t[64:128, 0:n]), rd=[O2], wr=[rb[1]])
                        op(dve, lambda: nc.vector.tensor_tensor(out=ab[0].t[0:64, 0:n], in0=O1.t[0:64, 0:n],
                                                                in1=rb[0].t[0:64, 0:n], op=ALU.mult), rd=[O1, rb[0]], wr=[ab[0]])
                        op(dve, lambda: nc.vector.scalar_tensor_tensor(
                            out=ab[1].t[0:64, 0:n], in0=rb[1].t[0:64, 0:n], scalar=lamt[l].t[0:64, 0:1],
                            in1=O2.t[0:64, 0:n], op0=ALU.mult, op1=ALU.mult), rd=[O2, lamt[l], rb[1]], wr=[ab[1]])
                        op(pool, lambda: nc.gpsimd.tensor_tensor(out=ab[0].t[0:64, 0:n], in0=ab[0].t[0:64, 0:n],
                                                                 in1=ab[1].t[0:64, 0:n], op=ALU.add), rd=[ab[0], ab[1]], wr=[ab[0]])
                        op(act, lambda: nc.scalar.activation(out=sqc.t[0:64, 0:n], in_=ab[0].t[0:64, 0:n], func=AF.Square),
                           rd=[ab[0]], wr=[sqc])
                        op(pe, lambda: nc.tensor.matmul(Mp.t[0:64, 0:n], lhsT=ones_bf.t[0:64, 0:64], rhs=sqc.t[0:64, 0:n],
                                                        start=True, stop=True), rd=[ones_bf, sqc], wr=[Mp])
                        op(act, lambda: nc.scalar.activation(out=rsc.t[0:64, 0:n], in_=Mp.t[0:64, 0:n], func=AF.Ln,
                                                             scale=1.0 / 64, bias=EPS), rd=[Mp], wr=[rsc])
                        op(act, lambda: nc.scalar.activation(out=rsc.t[0:64, 0:n], in_=rsc.t[0:64, 0:n], func=AF.Exp, scale=-0.5),
                           rd=[rsc], wr=[rsc])
                        op(dve, lambda: nc.vector.scalar_tensor_tensor(
                            out=ob.t[0:64, 0:n], in0=ab[0].t[0:64, 0:n], scalar=lamt[l].t[0:64, 1:2],
                            in1=rsc.t[0:64, 0:n], op0=ALU.mult, op1=ALU.mult), rd=[ab[0], lamt[l], rsc], wr=[ob])
                    dma_store(k, sp, attn[jb['row']:jb['row'] + 64, t0:t0 + n], ob, ob.t[0:64, 0:n])

                load_job(0)
                ns = len(steps)
                for i in range(ns + LA):
                    if i < ns:
                        stp = steps[i]
                        Sb, Pt = Sp[i % NS], Pb[i % NP]
                        op(pe, lambda: stp['S'](Sb), rd=stp['rdS'], wr=[Sb])
                        nn = stp['n']
                        op(act, lambda: nc.scalar.activation(out=Pt.t[:, 0:nn], in_=Sb.t[:, 0:nn], func=AF.Exp),
                           rd=[Sb], wr=[Pt])
                    if i >= LA:
                        j = i - LA
                        stj = steps[j]
                        Pj = Pb[j % NP]
                        if 'pre' in stj and stj['pre'] + 1 < len(jobs):
                            load_job(stj['pre'] + 1)
                        op(pe, lambda: stj['PV'](Pj, stj['O']), rd=[Pj] + stj['rdV'], wr=[stj['O']])
                        if 'post' in stj:
                            post_block(*stj['post'])
                barrier(k)
            if stop_after == (l, 2):
                break

            blocks3 = blocks if with_ctx else blocks[:8]
            pwu = ExitStack()
            Wu = sb(pwu, "Wu", [128, NKC, 2 * FFN], BF16)
            with ExitStack() as p3:
                Wo = sb(p3, "Wo", [128, NKC, D], BF16)
                dma_load(k, pool, Wo, Wo.t[:], w_out[l].rearrange("(k p) c -> p k c", p=128))
                for c0 in range(0, 2 * FFN, 1408):
                    dma_load(k, pool, Wu, Wu.t[:, :, c0:c0 + 1408],
                             w_up[l, :, c0:c0 + 1408].rearrange("(k p) c -> p k c", p=128), part=(c0 > 0))
                at = [sb(p3, "at%d" % i, [128, NKC, 512], BF16) for i in range(2)]
                xa = [sb(p3, "xa%d" % i, [128, NKC, 512], F32) for i in range(2)]
                po = [ps(p3, "po%d" % i) for i in range(4)]
                tmp3 = [sb(p3, "tmp3_%d" % i, [128, 512], F32) for i in range(2)]
                for bi, (t0, n) in enumerate(blocks3):
                    tcol = 0 if t0 < L else 1
                    a, x = at[bi % 2], xa[bi % 2]
                    dma_load(k, sp, a, a.t[:, :, 0:n], attn[:, t0:t0 + n].rearrange("(c p) t -> p c t", p=128))
                    dma_load(k, sp, x, x.t[:, :, 0:n], src[:, t0:t0 + n].rearrange("(c p) t -> p c t", p=128))
                    for oc in range(NKC):
                        pb = po[oc % 4]

                        def mm_o(oc=oc, pb=pb):
                            ins = None
                            for kc in range(NKC):
                                ins = nc.tensor.matmul(pb.t[:, 0:n], lhsT=Wo.t[:, kc, oc * 128:(oc + 1) * 128],
                                                       rhs=a.t[:, kc, 0:n], start=(kc == 0), stop=(kc == NKC - 1))
                            return ins
                        op(pe, mm_o, rd=[Wo, a], wr=[pb])
                        tb_ = tmp3[oc % 2]
                        op(act, lambda oc=oc, pb=pb, tb_=tb_: nc.scalar.activation(
                            out=tb_.t[:, 0:n], in_=pb.t[:, 0:n], func=AF.Copy,
                            scale=mods[l].t[:, 16 + oc, tcol:tcol + 1]), rd=[pb, mods[l]], wr=[tb_])
                        op(dve, lambda oc=oc, tb_=tb_: nc.vector.tensor_tensor(
                            out=x.t[:, oc, 0:n], in0=x.t[:, oc, 0:n], in1=tb_.t[:, 0:n], op=ALU.add), rd=[tb_, x], wr=[x])
                    dma_store(k, sp, xs_a[:, t0:t0 + n].rearrange("(c p) t -> p c t", p=128), x, x.t[:, :, 0:n])
                barrier(k)
            if stop_after == (l, 3):
                pwu.close()
                break

            with ExitStack() as p4:
                Wd = sb(p4, "Wd", [128, NJ, D], BF16)
                for j0 in range(0, NJ, 11):
                    dma_load(k, pool, Wd, Wd.t[:, j0:j0 + 11, :],
                             w_down[l, j0 * 128:(j0 + 11) * 128, :].rearrange("(j p) c -> p j c", p=128), part=(j0 > 0))
                NW = TB + 2
                xw = sb(p4, "xw", [128, NKC, NW], F32)
                h2 = sb(p4, "h2", [128, NKC, NW], BF16)
                actb = sb(p4, "actb", [128, NJ, NW], BF16)
                rs2 = sb(p4, "rs2", [128, NW], F32)
                tf2 = [sb(p4, "tf2_%d" % i, [128, NW], F32) for i in range(2)]
                tu = [sb(p4, "tu%d" % i, [128, NW], F32) for i in range(2)]
                tg = [sb(p4, "tg%d" % i, [128, NW], F32) for i in range(2)]
                tm = [sb(p4, "tm%d" % i, [128, NW], F32) for i in range(3)]
                tmc = [0]
                xc = [sb(p4, "xc%d" % i, [128, NW], F32) for i in range(2)]
                pu = [ps(p4, "pu%d" % i) for i in range(2)]
                pg = [ps(p4, "pg%d" % i) for i in range(2)]
                pd = [ps(p4, "pd%d" % i) for i in range(3)]
                pr = ps(p4, "pr")
                fblocks = []
                for s0 in range(0, L, TB):
                    fblocks.append((0, L, s0, min(TB, L - s0)))
                if with_ctx:
                    fblocks.append((L, LC, 0, LC))
                for bi, (base, slen, s0, tn) in enumerate(fblocks):
                    tcol = 0 if base < L else 1
                    nw = tn + 2
                    lo = s0 - 1
                    hi = s0 + tn + 1
                    c_lo = max(lo, 0)
                    c_hi = min(hi, slen)
                    if c_lo > lo:
                        op(pool, lambda: nc.gpsimd.memset(xw.t[:, :, 0:1], 1.0), wr=[xw])
                    if c_hi < hi:
                        op(pool, lambda: nc.gpsimd.memset(xw.t[:, :, nw - 1:nw], 1.0), wr=[xw])
                    dma_load(k, sp, xw, xw.t[:, :, c_lo - lo:c_hi - lo],
                             xs_a[:, base + c_lo:base + c_hi].rearrange("(c p) t -> p c t", p=128))
                    op(act, lambda: nc.scalar.activation(out=actb.t[:, 0:NKC, 0:nw], in_=xw.t[:, :, 0:nw], func=AF.Square),
                       rd=[xw], wr=[actb])

                    def mm_ss2():
                        ins = None
                        for c in range(NKC):
                            ins = nc.tensor.matmul(pr.t[:, 0:nw], lhsT=ones_bf.t[:], rhs=actb.t[:, c, 0:nw],
                                                   start=(c == 0), stop=(c == NKC - 1))
                        return ins
                    op(pe, mm_ss2, rd=[ones_bf, actb], wr=[pr])
                    op(act, lambda: nc.scalar.activation(out=rs2.t[:, 0:nw], in_=pr.t[:, 0:nw], func=AF.Ln,
                                                         scale=1.0 / D, bias=EPS), rd=[pr], wr=[rs2])
                    op(act, lambda: nc.scalar.activation(out=rs2.t[:, 0:nw], in_=rs2.t[:, 0:nw], func=AF.Exp, scale=-0.5),
                       rd=[rs2], wr=[rs2])
                    for c in range(NKC):
                        tf = tf2[c % 2]
                        op(dve, lambda c=c, tf=tf: nc.vector.scalar_tensor_tensor(
                            out=tf.t[:, 0:nw], in0=xw.t[:, c, 0:nw], scalar=gm[l].t[:, 1, c, tcol:tcol + 1],
                            in1=rs2.t[:, 0:nw], op0=ALU.mult, op1=ALU.mult), rd=[xw, gm[l], rs2], wr=[tf])
                        op(act, lambda c=c, tf=tf: nc.scalar.activation(
                            out=h2.t[:, c, 0:nw], in_=tf.t[:, 0:nw], func=AF.Identity,
                            bias=mods[l].t[:, 24 + c, tcol:tcol + 1], scale=1.0), rd=[tf, mods[l]], wr=[h2])
                    if c_lo > lo:
                        op(pool, lambda: nc.gpsimd.memset(h2.t[:, :, 0:1], 0.0), wr=[h2])
                    if c_hi < hi:
                        op(pool, lambda: nc.gpsimd.memset(h2.t[:, :, nw - 1:nw], 0.0), wr=[h2])
                    for j in range(NJ):
                        i2 = j % 2
                        pub, pgb, tub, tgb = pu[i2], pg[i2], tu[i2], tg[i2]

                        def mm_up(col, dst):
                            def f():
                                ins = None
                                for kc in range(NKC):
                                    ins = nc.tensor.matmul(dst.t[:, 0:nw], lhsT=Wu.t[:, kc, col * 128:(col + 1) * 128],
                                                           rhs=h2.t[:, kc, 0:nw], start=(kc == 0), stop=(kc == NKC - 1))
                                return ins
                            return f
                        op(pe, mm_up(j, pub), rd=[Wu, h2], wr=[pub])
                        op(pe, mm_up(NJ + j, pgb), rd=[Wu, h2], wr=[pgb])
                        for ei, (col, pp, tt_) in enumerate(((j, pub, tub), (NJ + j, pgb, tgb))):
                            cw0 = PT_CW + 3 * col
                            cb0 = PT_CB + col
                            op(act, lambda pp=pp, tt_=tt_, cw0=cw0, cb0=cb0: nc.scalar.activation(
                                out=tt_.t[:, 0:tn], in_=pp.t[:, 0:tn], func=AF.Identity,
                                scale=ptb[l].t[:, cw0:cw0 + 1], bias=ptb[l].t[:, cb0:cb0 + 1]), rd=[pp, ptb[l]], wr=[tt_])
                            for tap in (1, 2):
                                tmb = tm[tmc[0] % 3]
                                tmc[0] += 1
                                op(act, lambda pp=pp, tmb=tmb, cw0=cw0, tap=tap: nc.scalar.activation(
                                    out=tmb.t[:, 0:tn], in_=pp.t[:, tap:tap + tn], func=AF.Copy,
                                    scale=ptb[l].t[:, cw0 + tap:cw0 + tap + 1]), rd=[pp, ptb[l]], wr=[tmb])
                                if ei == 0:
                                    op(dve, lambda tt_=tt_, tmb=tmb: nc.vector.tensor_tensor(
                                        out=tt_.t[:, 0:tn], in0=tt_.t[:, 0:tn], in1=tmb.t[:, 0:tn], op=ALU.add),
                                       rd=[tmb, tt_], wr=[tt_])
                                else:
                                    op(pool, lambda tt_=tt_, tmb=tmb: nc.gpsimd.tensor_tensor(
                                        out=tt_.t[:, 0:tn], in0=tt_.t[:, 0:tn], in1=tmb.t[:, 0:tn], op=ALU.add),
                                       rd=[tmb, tt_], wr=[tt_])
                        op(act, lambda: nc.scalar.activation(out=tgb.t[:, 0:tn], in_=tgb.t[:, 0:tn], func=AF.Silu),
                           rd=[tgb], wr=[tgb])
                        op(dve, lambda j=j: nc.vector.tensor_tensor(out=actb.t[:, j, 0:tn], in0=tub.t[:, 0:tn],
                                                                    in1=tgb.t[:, 0:tn], op=ALU.mult),
                           rd=[tub, tgb], wr=[actb])
                    for oc in range(NKC):
                        pdb = pd[oc % 3]
                        xcb = xc[oc % 2]
                        dma_load(k, sp, xcb, xcb.t[:, 0:tn], xs_a[oc * 128:(oc + 1) * 128, base + s0:base + s0 + tn])

                        def mm_dn(oc=oc, pdb=pdb):
                            ins = None
                            for j in range(NJ):
                                ins = nc.tensor.matmul(pdb.t[:, 0:tn], lhsT=Wd.t[:, j, oc * 128:(oc + 1) * 128],
                                                       rhs=actb.t[:, j, 0:tn], start=(j == 0), stop=(j == NJ - 1))
                            return ins
                        op(pe, mm_dn, rd=[Wd, actb], wr=[pdb])
                        tmb = tm[tmc[0] % 3]
                        tmc[0] += 1
                        op(act, lambda oc=oc, pdb=pdb, tmb=tmb: nc.scalar.activation(
                            out=tmb.t[:, 0:tn], in_=pdb.t[:, 0:tn], func=AF.Copy,
                            scale=mods[l].t[:, 40 + oc, tcol:tcol + 1]), rd=[pdb, mods[l]], wr=[tmb])
                        op(dve, lambda xcb=xcb, tmb=tmb: nc.vector.tensor_tensor(
                            out=xcb.t[:, 0:tn], in0=xcb.t[:, 0:tn], in1=tmb.t[:, 0:tn], op=ALU.add), rd=[tmb, xcb], wr=[xcb])
                        if last:
                            dma_store(k, sp, yTh[oc // 4][(oc % 4) * 128:(oc % 4 + 1) * 128, s0:s0 + tn], xcb, xcb.t[:, 0:tn])
                        else:
                            dma_store(k, sp, xs_b[oc * 128:(oc + 1) * 128, base + s0:base + s0 + tn], xcb, xcb.t[:, 0:tn])
                barrier(k)
            pwu.close()
        barrier(k)
    return nc


def _rope_tables():
    def ang(dim):
        half = dim // 2
        inv = (10000.0 ** (-np.arange(0, half, 2, dtype=np.float32) / half)).astype(np.float32)
        t = np.arange(L)
        row = (t // 64).astype(np.float32)
        col = (t % 64).astype(np.float32)
        return np.concatenate([row[:, None] * inv, col[:, None] * inv], axis=-1).astype(np.float32)

    out = []
    for dim in (64, 32):
        a = ang(dim)
        cos = np.cos(a).astype(np.float32)
        sin = np.sin(a).astype(np.float32)
        d = np.arange(dim)
        c = cos[:, d // 2].T
        s = sin[:, d // 2].T * np.where(d % 2 == 0, -1.0, 1.0)[:, None]
        tab = np.zeros((2, 128, T), np.float32)
        rep = 128 // dim
        tab[0, :, :L] = np.tile(c, (rep, 1))
        tab[1, :, :L] = np.tile(s, (rep, 1))
        tab[0, :, L:] = 1.0
        out.append(tab)
    return out


def _bias_patterns():
    classes = [2, 0, 1, 30, 31]
    valid = np.zeros((25, 128, 128), bool)
    ri = np.zeros((25, 128, 128), np.int64)
    ci = np.zeros((25, 128, 128), np.int64)
    kp = np.arange(128)[:, None]
    qf = np.arange(128)[None, :]
    for cidx, i in enumerate(classes):
        for s in range(5):
            kt = min(max(i - 2, 0), 27) + s
            qr = 2 * i + qf // 64
            qc = qf % 64
            kr = 2 * kt + kp // 64
            kc = kp % 64
            rs_ = np.clip(qr - 4, 0, 56)
            cs_ = np.clip(qc - 8, 0, 48)
            v = (kr >= rs_) & (kr < rs_ + 8) & (kc >= cs_) & (kc < cs_ + 16)
            p = cidx * 5 + s
            valid[p] = v
            ri[p] = np.where(v, kr - qr + 7, 0)
            ci[p] = np.where(v, kc - qc + 15, 0)
    return valid, ri, ci


def _prep_shared(inp):
    f = lambda a: np.ascontiguousarray(np.asarray(a, dtype=np.float32))
    w_in = f(inp["w_in"])
    sw = np.arange(64) ^ 1
    cols = []
    qa = np.arange(0, 512)
    cols += [qa]
    cols += [(qa // 64) * 64 + sw[qa % 64]]
    ka = [512 + 64 * g + np.arange(64) for g in range(2)]
    cols += [np.concatenate([ka[0], ka[0]]), np.concatenate([ka[1], ka[1]])]
    cols += [np.concatenate([ka[0][sw], ka[0][sw]]), np.concatenate([ka[1][sw], ka[1][sw]])]
    cols += [768 + np.arange(256)]
    cols += [1024 + np.arange(256)]
    qc = 1536 + np.arange(256)
    kc = 1792 + np.arange(256)
    cols += [qc, qc ^ 1, kc, kc ^ 1]
    cols += [640 + np.arange(128), 1280 + np.arange(256), 2048 + np.arange(256)]
    cols = np.concatenate(cols)
    assert cols.shape[0] == W1C
    w1 = np.ascontiguousarray(w_in[:, :, cols])

    ptab = np.zeros((DEPTH, 128, NPT), np.float32)
    p = np.arange(128)
    for l in range(DEPTH):
        ptab[l, :, PT_BADA:PT_BADA + 48] = f(inp["b_ada"])[l].reshape(48, 128).T
        ptab[l, :, PT_G1:PT_G1 + 8] = f(inp["g_norm1"])[l].reshape(8, 128).T
        ptab[l, :, PT_G2:PT_G2 + 8] = f(inp["g_norm2"])[l].reshape(8, 128).T
        gqa, gka = f(inp["gq_a"])[l], f(inp["gk_a"])[l]
        gqb, gkb = f(inp["gq_b"])[l], f(inp["gk_b"])[l]
        gqc, gkc = f(inp["gq_c"])[l], f(inp["gk_c"])[l]
        g = np.zeros((128, NSLAB), np.float32)
        for s in range(4):
            g[:, s] = gqa[p % 64]
            g[:, 4 + s] = gqa[(p % 64) ^ 1]
        for s in range(2):
            g[:, 8 + s] = gka[p % 64]
            g[:, 10 + s] = gka[(p % 64) ^ 1]
            g[:, 12 + s] = gqb[p % 64]
            g[:, 14 + s] = gkb[p % 64]
            g[:, 16 + s] = gqc[p % 32]
            g[:, 18 + s] = gqc[(p % 32) ^ 1]
            g[:, 20 + s] = gkc[p % 32]
            g[:, 22 + s] = gkc[(p % 32) ^ 1]
        ptab[l, :, PT_GTAB:PT_GTAB + NSLAB] = g
        cw = f(inp["conv_w"])[l]
        ptab[l, :, PT_CW:PT_CW + 132] = cw.reshape(3, 44, 128).transpose(2, 1, 0).reshape(128, 132)
        ptab[l, :, PT_CB:PT_CB + 44] = f(inp["conv_b"])[l].reshape(44, 128).T
        ptab[l, :, PT_GSUB] = f(inp["g_subln"])[l][p % 64]
        for i, nm in enumerate(("lambda_q1", "lambda_k1", "lambda_q2", "lambda_k2")):
            ptab[l, :, PT_LAM + 32 * i:PT_LAM + 32 * i + 32] = f(inp[nm])[l][None, :]

    ropeA, ropeC = _rope_tables()
    valid, ri, ci = _bias_patterns()
    rpb = f(inp["rpb_b"])
    bias = np.where(valid[None, None], rpb[:, :, ri, ci], np.float32(NEG)).astype(np.float32)
    biasB = np.ascontiguousarray(bias.transpose(0, 1, 3, 2, 4))
    consts = np.zeros((3, 128, 128), np.float32)
    consts[0] = np.eye(128, dtype=np.float32)
    consts[1] = np.kron(np.eye(2, dtype=np.float32), np.ones((64, 64), np.float32))
    consts[2] = np.kron(np.eye(4, dtype=np.float32), np.ones((32, 32), np.float32))
    return dict(consts=consts, w_ada=f(inp["w_ada"]), ptab=ptab, w1=w1, ropeA=ropeA, ropeC=ropeC, biasB=biasB,
                w_out=f(inp["w_out"]), w_up=f(inp["w_up"]), w_down=f(inp["w_down"]))


def _prep_core(inp, b):
    x = np.asarray(inp["x"][b], np.float32)
    ctx = np.asarray(inp["ctx"][b], np.float32)
    xT = np.ascontiguousarray(np.concatenate([x.T, ctx.T], axis=1))
    cv = np.stack([np.asarray(inp["c"][b], np.float32), np.asarray(inp["c_ctx"], np.float32)], axis=-1)
    cvec = np.ascontiguousarray(cv.reshape(8, 128, 2).transpose(1, 0, 2))
    return dict(xT=xT, cvec=cvec)


_NC_CACHE = {}


def kernel(**inputs):
    if "nc" not in _NC_CACHE:
        _NC_CACHE["nc"] = build_nc()
    nc = _NC_CACHE["nc"]
    shared = _prep_shared(inputs)
    in_maps = []
    for b in range(8):
        m = dict(shared)
        m.update(_prep_core(inputs, b))
        in_maps.append(m)
    outs = []
    for g0 in range(0, 8, GROUP):
        res = run_bass_kernel_spmd(nc, in_maps[g0:g0 + GROUP], core_ids=list(range(GROUP)))
        outs.extend(np.ascontiguousarray(np.concatenate([res.results[b]["yT0"], res.results[b]["yT1"]], axis=0).T)
                    for b in range(GROUP))
    return np.stack(outs, axis=0).astype(np.float32)
```
